# Optimizing a Trainium2 kernel written in Bass

```python
import jax, jax.numpy as jnp
from jax import lax
import numpy as np

D_MODEL = 1024
BATCH = 8
SEQ = 4096
DEPTH = 2

N_MIXERS = 2
N_LRU = (DEPTH + 1) // 2
N_FOX = DEPTH // 2
D_RNN = D_MODEL
LRU_BLOCKS = 16
LRU_BLOCK_DIM = D_RNN // LRU_BLOCKS
CONV_WIDTH = 4
LRU_C = 8.0
N_HEADS = 16
HEAD_DIM = D_MODEL // N_HEADS
Q_BLOCK = 128
D_FF = 4 * D_MODEL
EPS = 1e-6
NEG_INF = -1e30

kernel_name = "hawk_fox_interleaved_trunk"


def rms_norm(x, g):
    xf = x.astype(jnp.float32)
    y = xf * lax.rsqrt(jnp.mean(xf * xf, axis=-1, keepdims=True) + EPS)
    return (y * g.astype(jnp.float32)).astype(x.dtype)


def causal_depthwise_conv(x, w, b):
    S = x.shape[1]
    xp = jnp.pad(x, ((0, 0), (CONV_WIDTH - 1, 0), (0, 0)))
    y = b
    for k in range(CONV_WIDTH):
        y = y + xp[:, k:k + S] * w[k]
    return y


def rglru_mixer(h, w_in, conv_w, conv_b, w_r, b_r, w_i, b_i, lam, w_out):
    B, S, _ = h.shape
    u = h @ w_in
    gate_branch, x_branch = jnp.split(u, 2, axis=-1)
    gate_branch = jax.nn.gelu(gate_branch)
    xc = causal_depthwise_conv(x_branch, conv_w, conv_b)
    xb = xc.reshape(B, S, LRU_BLOCKS, LRU_BLOCK_DIM)
    r = jax.nn.sigmoid((jnp.einsum('bsnd,nde->bsne', xb, w_r) + b_r).astype(jnp.float32))
    i = jax.nn.sigmoid((jnp.einsum('bsnd,nde->bsne', xb, w_i) + b_i).astype(jnp.float32))
    r = r.reshape(B, S, D_RNN)
    i = i.reshape(B, S, D_RNN)
    log_a = LRU_C * r * jax.nn.log_sigmoid(lam.astype(jnp.float32))
    a = jnp.exp(log_a)
    mult = jnp.sqrt(-jnp.expm1(2.0 * log_a))
    bt = mult * (i * xc.astype(jnp.float32))

    def combine(left, right):
        a_l, b_l = left
        a_r, b_r_ = right
        return a_l * a_r, a_r * b_l + b_r_

    _, hs = lax.associative_scan(combine, (a, bt), axis=1)
    y = gate_branch * hs.astype(h.dtype)
    return y @ w_out


def fox_mixer(h, w_in, b_f, q_gain, k_gain, w_out):
    B, S, _ = h.shape
    u = h @ w_in
    q, k, v, f_logit = jnp.split(u, [D_MODEL, 2 * D_MODEL, 3 * D_MODEL], axis=-1)

    def heads(t):
        return t.reshape(B, S, N_HEADS, HEAD_DIM).transpose(0, 2, 1, 3)

    q = rms_norm(heads(q), q_gain)
    k = rms_norm(heads(k), k_gain)
    v = heads(v)
    log_f = jax.nn.log_sigmoid((f_logit + b_f).astype(jnp.float32))
    c = jnp.cumsum(log_f, axis=1).transpose(0, 2, 1)
    scale = HEAD_DIM ** -0.5
    nb = S // Q_BLOCK
    q_blocks = q.reshape(B, N_HEADS, nb, Q_BLOCK, HEAD_DIM).transpose(2, 0, 1, 3, 4)
    c_blocks = c.reshape(B, N_HEADS, nb, Q_BLOCK).transpose(2, 0, 1, 3)
    k_pos = jnp.arange(S)

    def attend(args):
        qb, cb, blk = args
        q_pos = blk * Q_BLOCK + jnp.arange(Q_BLOCK)
        s = jnp.einsum('bhqd,bhkd->bhqk', qb, k).astype(jnp.float32) * scale
        s = s + cb[..., :, None] - c[:, :, None, :]
        s = jnp.where(k_pos[None, :] <= q_pos[:, None], s, NEG_INF)
        p = jax.nn.softmax(s, axis=-1)
        return jnp.einsum('bhqk,bhkd->bhqd', p.astype(v.dtype), v)

    o = lax.map(attend, (q_blocks, c_blocks, jnp.arange(nb)))
    o = o.transpose(1, 0, 3, 2, 4).reshape(B, S, D_MODEL)
    return o @ w_out


def sq_relu_mlp(h, w1, w2):
    return jnp.square(jax.nn.relu(h @ w1)) @ w2


def setup_inputs(seed: int = 0) -> dict:
    key = jax.random.key(seed)
    ks = jax.random.split(key, 20)
    f32 = jnp.float32
    nrm = lambda k, shape, fan_in: jax.random.normal(k, shape, f32) * (fan_in ** -0.5)
    a0 = jax.random.uniform(ks[12], (N_LRU, D_RNN), f32, minval=0.9, maxval=0.999)
    return {
        "x": jax.random.normal(ks[0], (BATCH, SEQ, D_MODEL), f32),
        "mix_norm": 1.0 + 0.02 * jax.random.normal(ks[1], (DEPTH, D_MODEL), f32),
        "mlp_norm": 1.0 + 0.02 * jax.random.normal(ks[2], (DEPTH, D_MODEL), f32),
        "mlp_w1": nrm(ks[3], (DEPTH, D_MODEL, D_FF), D_MODEL),
        "mlp_w2": nrm(ks[4], (DEPTH, D_FF, D_MODEL), D_FF),
        "lru_w_in": nrm(ks[5], (N_LRU, D_MODEL, 2 * D_RNN), D_MODEL),
        "lru_conv_w": nrm(ks[6], (N_LRU, CONV_WIDTH, D_RNN), CONV_WIDTH),
        "lru_conv_b": 0.02 * jax.random.normal(ks[7], (N_LRU, D_RNN), f32),
        "lru_w_r": nrm(ks[8], (N_LRU, LRU_BLOCKS, LRU_BLOCK_DIM, LRU_BLOCK_DIM), LRU_BLOCK_DIM),
        "lru_b_r": 0.1 * jax.random.normal(ks[9], (N_LRU, LRU_BLOCKS, LRU_BLOCK_DIM), f32),
        "lru_w_i": nrm(ks[10], (N_LRU, LRU_BLOCKS, LRU_BLOCK_DIM, LRU_BLOCK_DIM), LRU_BLOCK_DIM),
        "lru_b_i": 0.1 * jax.random.normal(ks[11], (N_LRU, LRU_BLOCKS, LRU_BLOCK_DIM), f32),
        "lru_lambda": jnp.log(a0) - jnp.log1p(-a0),
        "lru_w_out": nrm(ks[13], (N_LRU, D_RNN, D_MODEL), D_RNN),
        "fox_w_in": nrm(ks[14], (N_FOX, D_MODEL, 3 * D_MODEL + N_HEADS), D_MODEL),
        "fox_b_f": jax.random.uniform(ks[15], (N_FOX, N_HEADS), f32, minval=1.0, maxval=6.0),
        "fox_q_gain": 1.0 + 0.02 * jax.random.normal(ks[16], (N_FOX, HEAD_DIM), f32),
        "fox_k_gain": 1.0 + 0.02 * jax.random.normal(ks[17], (N_FOX, HEAD_DIM), f32),
        "fox_w_out": nrm(ks[18], (N_FOX, D_MODEL, D_MODEL), D_MODEL),
    }


def reference(x, mix_norm, mlp_norm, mlp_w1, mlp_w2, lru_w_in, lru_conv_w, lru_conv_b,
              lru_w_r, lru_b_r, lru_w_i, lru_b_i, lru_lambda, lru_w_out,
              fox_w_in, fox_b_f, fox_q_gain, fox_k_gain, fox_w_out):
    for layer in range(DEPTH):
        h = rms_norm(x, mix_norm[layer])
        j = layer // N_MIXERS
        if layer % N_MIXERS == 0:
            mixed = rglru_mixer(h, lru_w_in[j], lru_conv_w[j], lru_conv_b[j], lru_w_r[j],
                                lru_b_r[j], lru_w_i[j], lru_b_i[j], lru_lambda[j], lru_w_out[j])
        else:
            mixed = fox_mixer(h, fox_w_in[j], fox_b_f[j], fox_q_gain[j], fox_k_gain[j],
                              fox_w_out[j])
        x = x + mixed
        x = x + sq_relu_mlp(rms_norm(x, mlp_norm[layer]), mlp_w1[layer], mlp_w2[layer])
    return x
```

```python
import contextlib
import numpy as np
import concourse.bass as bass
import concourse.mybir as mybir
from concourse.bass_utils import run_bass_kernel_spmd

F32 = mybir.dt.float32
BF16 = mybir.dt.bfloat16
AF = mybir.ActivationFunctionType
ALU = mybir.AluOpType

S = 4096
D = 1024
DFF = 4096
NH = 16
HD = 64
EPS = 1e-6
NEG = -30000.0


class Prog:
    NDMA_SEMS = 8

    def __init__(self, nc):
        self.nc = nc
        self.names = ("pe", "act", "dve", "pool", "sp")
        self.sems = {e: nc.alloc_semaphore(name="s_" + e) for e in self.names}
        self.dma_sems = {e: [nc.alloc_semaphore(name="d_%s%d" % (e, k)) for k in range(self.NDMA_SEMS)]
                         for e in ("sp", "act", "pool")}
        self.cnt = {e: 0 for e in self.names}
        self.dcnt = {e: 0 for e in self.dma_sems}
        self.seen = {e: {} for e in self.names}
        self._reset()

    def _reset(self):
        self.ops = []
        self.last_w = {}
        self.readers = {}

    def add(self, eng, fn, r=(), w=(), dma=False):
        idx = len(self.ops)
        deps = {}
        for res in r:
            lw = self.last_w.get(res)
            if lw is not None:
                deps[lw] = "raw"
        for res in w:
            lw = self.last_w.get(res)
            if lw is not None and lw not in deps:
                deps[lw] = "waw"
            for rd in self.readers.get(res, {}).values():
                if rd not in deps:
                    deps[rd] = "war"
        for res in w:
            self.last_w[res] = idx
            self.readers[res] = {}
        for res in r:
            d = self.readers.setdefault(res, {})
            d[(eng, idx) if dma else eng] = idx
        deps.pop(idx, None)
        self.ops.append((eng, fn, deps, dma))
        return idx

    def i(self, eng, method, *args, r=(), w=(), **kw):
        return self.add(eng, lambda e: getattr(e, method)(*args, **kw), r=r, w=w)

    def barrier(self):
        last = {}
        dmas = []
        for idx in range(getattr(self, "_bar_from", 0), len(self.ops)):
            eng, fn, deps, dma = self.ops[idx]
            if dma:
                dmas.append(idx)
            elif fn is not None:
                last[eng] = idx
        self._bar_from = len(self.ops)
        for e in self.names:
            deps = {i: "raw" for x, i in last.items() if x != e}
            for i in dmas:
                deps[i] = "raw"
            self.ops.append((e, None, deps, False))
        self.last_w = {}
        self.readers = {}

    def dma(self, eng, out, in_, r=(), w=(), **kw):
        return self.add(eng, lambda e: e.dma_start(out=out, in_=in_, **kw), r=r, w=w, dma=True)

    def emit(self):
        nc = self.nc
        ops = self.ops
        n = len(ops)
        need = [None] * n
        signal = [False] * n
        for i, (eng, fn, deps, dma) in enumerate(ops):
            keep = []
            for d, kind in deps.items():
                deng, _, _, ddma = ops[d]
                if not ddma and deng == eng and not dma and eng == "pe":
                    continue
                keep.append(d)
                signal[d] = True
            need[i] = keep
        token = [None] * n
        plan = {e: [] for e in self.names}
        seen = self.seen

        def wait(lst, eng, tok):
            sem, val = tok
            s = seen[eng]
            if s.get(sem.num, 0) >= val:
                return
            lst.append((sem, val))
            s[sem.num] = val

        for i, (eng, fn, deps, dma) in enumerate(ops):
            waits = []
            best = {}
            for d in need[i]:
                sem, val = token[d]
                if best.get(sem.num, (None, 0))[1] < val:
                    best[sem.num] = (sem, val)
            for k_ in sorted(best):
                wait(waits, eng, best[k_])
            if dma:
                k = self.dcnt[eng]
                self.dcnt[eng] += 1
                sem = self.dma_sems[eng][k % self.NDMA_SEMS]
                val = 16 * (k // self.NDMA_SEMS + 1)
                if val > 16:
                    wait(waits, eng, (sem, val - 16))
                token[i] = (sem, val)
                plan[eng].append((waits, fn, (sem, 16)))
            else:
                inc = None
                if signal[i] and fn is not None:
                    self.cnt[eng] += 1
                    inc = (self.sems[eng], 1)
                    token[i] = (self.sems[eng], self.cnt[eng])
                plan[eng].append((waits, fn, inc))
        waits = []
        for e, lst in self.dma_sems.items():
            k = self.dcnt[e]
            for j in range(max(0, k - self.NDMA_SEMS), k):
                sem = lst[j % self.NDMA_SEMS]
                wait(waits, "sp", (sem, 16 * (j // self.NDMA_SEMS + 1)))
        plan["sp"].append((waits, None, None))

        def run(engobj, items):
            for waits, fn, inc in items:
                for sem, val in waits:
                    engobj.wait_ge(sem, val)
                if fn is not None:
                    ins = fn(engobj)
                    if inc is not None:
                        ins.then_inc(inc[0], inc[1])

        with nc.Block() as block:
            @block.sync
            def _(e):
                run(e, plan["sp"])

            @block.scalar
            def _(e):
                run(e, plan["act"])

            @block.vector
            def _(e):
                run(e, plan["dve"])

            @block.gpsimd
            def _(e):
                run(e, plan["pool"])

            @block.tensor
            def _(e):
                run(e, plan["pe"])
        self._reset()


class Ctx:
    uid = 0

    def __init__(self, nc, P, es):
        self.nc, self.P, self.es = nc, P, es
        self.n = 0
        Ctx.uid += 1
        self.pfx = "c%d_" % Ctx.uid

    def sb(self, shape, dt, name=None):
        self.n += 1
        return self.es.enter_context(self.nc.sbuf_tensor(self.pfx + (name or "t%d" % self.n), list(shape), dt))

    def ps(self, shape, dt=F32, name=None):
        self.n += 1
        return self.es.enter_context(self.nc.psum_tensor(self.pfx + (name or "p%d" % self.n), list(shape), dt))


def res_keys(t0, nt):
    return [("res", t0 // 128 + i) for i in range(nt // 128)]


def rows(ap, t0, nt):
    return ap[t0:t0 + nt, :].rearrange("(s p) d -> p s d", p=128)


def load_col(cx, dst, vec, eng="sp"):
    cx.P.dma(eng, dst, vec.rearrange("(c p) -> p c", p=128), w=[("col", id(dst))],
             allow_slow_non_contiguous=True)


def make_ident(cx, key="ident"):
    P = cx.P
    ident = cx.sb([128, 128], BF16, "ident")
    P.i("pool", "memset", ident[:], 1.0, w=[key])
    P.i("pool", "affine_select", out=ident[:], in_=ident[:], pattern=[[-1, 128]],
                                            compare_op=ALU.is_ge, fill=0.0, base=0,
                                            channel_multiplier=1, r=[key], w=[key])
    P.i("pool", "affine_select", out=ident[:], in_=ident[:], pattern=[[1, 128]],
                                            compare_op=ALU.is_ge, fill=0.0, base=0,
                                            channel_multiplier=-1, r=[key], w=[key])
    return ident


class NormT:
    def __init__(self, cx, NS, gvec, nslots_x=3, nslots_h=2, pst=None, evac="alt"):
        self.cx, self.NS = cx, NS
        self.evac = evac
        P = cx.P
        self.ident = make_ident(cx)
        self.gcol = cx.sb([128, 8], F32, "gcol")
        P.dma("sp", self.gcol[:], gvec.rearrange("(c p) -> p c", p=128), w=["gcol"],
              allow_slow_non_contiguous=True)
        self.mhalf = cx.sb([128, 8], F32, "mhalf")
        P.i("pool", "memset", self.mhalf[:], -0.5, w=["mhalf"])
        self.xt = [cx.sb([128, NS, D], F32, "xt%d" % i) for i in range(nslots_x)]
        self.hb = cx.sb([128, NS, D], BF16, "hb")
        self.hT = [cx.sb([128, 8, NS * 128], BF16, "hT%d" % i) for i in range(nslots_h)]
        self.junk = cx.sb([128, D], BF16, "junk")
        self.ss = [cx.sb([128, NS], F32, "ss%d" % i) for i in range(2)]
        self.rstd = [cx.sb([128, NS], F32, "rstd%d" % i) for i in range(2)]
        self.pst = pst if pst is not None else [cx.ps([128, 1024], BF16, "pst%d" % i) for i in range(2)]
        self.nx, self.nh = nslots_x, nslots_h
        self.ev = 0

    def load(self, b, src):
        T = self.NS * 128
        xs = b % self.nx
        self.cx.P.dma("sp", self.xt[xs][:], rows(src, b * T, T), r=res_keys(b * T, T), w=[("xt", xs)])

    def stats(self, b):
        NS, P = self.NS, self.cx.P
        xs, sl = b % self.nx, b % 2
        xt, ss, rstd, hb = self.xt[xs], self.ss[sl], self.rstd[sl], self.hb
        for s in range(NS):
            P.i("act", "activation", out=hb[:, s, :], in_=xt[:, s, :], func=AF.Square,
                scale=1.0 / 32.0, accum_out=ss[:, s:s + 1], r=[("xt", xs)], w=[("hb", s), ("ss", sl, s)])
        sk = [("ss", sl, s) for s in range(NS)]
        P.i("dve", "tensor_scalar", out=ss[:], in0=ss[:], scalar1=EPS, scalar2=None, op0=ALU.add, r=sk, w=sk)
        P.i("pool", "tensor_tensor", out=rstd[:], in0=ss[:], in1=self.mhalf[:, 0:NS], op=ALU.pow,
            r=sk + ["mhalf"], w=[("rstd", sl)])
        for s in range(NS):
            P.i("dve", "tensor_scalar", out=hb[:, s, :], in0=xt[:, s, :], scalar1=rstd[:, s:s + 1],
                scalar2=None, op0=ALU.mult, r=[("xt", xs), ("rstd", sl)], w=[("hb", s)])
        return xs, b % self.nh

    def tpose(self, b, kc):
        NS, P = self.NS, self.cx.P
        T = NS * 128
        hs = b % self.nh
        hT, hb = self.hT[hs], self.hb
        pst = self.pst[kc % 2]
        pi = self.pst.index(pst)
        for s in range(NS):
            P.i("pe", "transpose", out=pst[:, s * 128:(s + 1) * 128], in_=hb[:, s, kc * 128:(kc + 1) * 128],
                identity=self.ident[:], r=[("hb", s), "ident"], w=[("pst", pi)])
        self.ev += 1
        if self.ev % 2 == 0 and self.evac == "alt":
            P.i("act", "mul", out=hT[:, kc, :], in_=pst[:, 0:T], mul=self.gcol[:, kc:kc + 1],
                r=[("pst", pi), "gcol"], w=[("pst", pi), ("hT", hs, kc)])
        else:
            P.i("dve", "tensor_scalar", out=hT[:, kc, :], in0=pst[:, 0:T], scalar1=self.gcol[:, kc:kc + 1],
                scalar2=None, op0=ALU.mult, r=[("pst", pi), "gcol"], w=[("pst", pi), ("hT", hs, kc)])

    def run(self, b, src, load=True):
        if load:
            self.load(b, src)
        r = self.stats(b)
        for kc in range(8):
            self.tpose(b, kc)
        return r

    def hT_keys(self, hs):
        return [("hT", hs, kc) for kc in range(8)]


def wload(P, dst, srcw, key, cast, maxb=8192):
    if cast:
        P.dma("pool", dst, srcw, w=[key], max_dma_last_dim=maxb)
    else:
        P.dma("sp", dst, srcw, w=[key])


def cast_rows(P, dst, srcw, name, maxb=8192):
    def piece(r0):
        return lambda: P.dma("pool", dst[r0:r0 + 128, :], srcw[r0:r0 + 128, :], w=[(name, r0)],
                             max_dma_last_dim=maxb)
    return [piece(r0) for r0 in range(0, srcw.shape[0], 128)]


class Trickle:
    def __init__(self, pieces, steps):
        self.pieces = list(pieces or [])
        self.per = -(-len(self.pieces) // max(steps, 1)) if self.pieces else 0

    def step(self):
        for _ in range(self.per):
            if self.pieces:
                self.pieces.pop(0)()

    def flush(self):
        while self.pieces:
            self.pieces.pop(0)()


def mlp_phase(nc, P, src, dst, w1, w2, gvec, cast=True, pre=None):
    NS = 2
    T = NS * 128
    NB = S // T
    with contextlib.ExitStack() as es:
        cx = Ctx(nc, P, es)
        w1b = cx.sb([128, 8, DFF], BF16, "w1b")
        w2b = cx.sb([128, 32, D], BF16, "w2b")
        for g in range(8):
            wload(P, w1b[:, :, g * 512:(g + 1) * 512],
                  w1[:, g * 512:(g + 1) * 512].rearrange("(kc p) f -> p kc f", p=128), ("w1b", g), cast)
            wload(P, w2b[:, 4 * g:4 * g + 4, :],
                  w2[g * 512:(g + 1) * 512, :].rearrange("(f p) o -> p f o", p=128), ("w2b", g), cast)
        tk = Trickle(pre, NB - 2)
        nt = NormT(cx, NS, gvec, nslots_x=3, nslots_h=2)
        ps1 = [cx.ps([128, 512], F32, "ps1_%d" % i) for i in range(2)]
        ps2 = [[cx.ps([128, 512], F32, "ps2_%d%d" % (s, o)) for o in range(2)] for s in range(NS)]
        rt = [cx.sb([128, T], BF16, "rt%d" % i) for i in range(2)]
        aT = [cx.sb([128, T], BF16, "aT%d" % i) for i in range(3)]
        w1keys = [("w1b", kc) for kc in range(8)]

        def stage1(fc, hT, hs):
            pi = fc % 2
            for kc in range(8):
                P.i("pe", "matmul", ps1[pi][:, 0:T], lhsT=w1b[:, kc, fc * 128:(fc + 1) * 128],
                                                      rhs=hT[:, kc, :], start=(kc == 0), stop=(kc == 7),
                      r=[("w1b", fc // 4), ("hT", hs, kc)], w=[("ps1", pi)])
            P.i("act", "activation", out=rt[pi][:], in_=ps1[pi][:, 0:T], func=AF.Relu,
                  r=[("ps1", pi)], w=[("ps1", pi), ("rt", pi)])
            ai = fc % 3
            P.i("dve", "tensor_tensor", out=aT[ai][:], in0=rt[pi][:], in1=rt[pi][:], op=ALU.mult,
                  r=[("rt", pi)], w=[("aT", ai)])

        def stage2(fc):
            ai = fc % 3
            for s in range(NS):
                for o in range(2):
                    P.i("pe", "matmul",
                        ps2[s][o][:, :], lhsT=aT[ai][:, s * 128:(s + 1) * 128],
                        rhs=w2b[:, fc, o * 512:(o + 1) * 512], start=(fc == 0), stop=(fc == 31),
                        r=[("aT", ai), ("w2b", fc // 4)], w=[("ps2", s, o)])

        slots = {}
        slots[0] = nt.run(0, src)
        for b in range(NB):
            xs, hs = slots[b]
            hT, xt = nt.hT[hs], nt.xt[xs]
            for fc in range(32):
                stage1(fc, hT, hs)
                if fc >= 1:
                    stage2(fc - 1)
                if fc == 2 and b + 1 < NB:
                    nt.load(b + 1, src)
                if fc == 10 and b + 1 < NB:
                    slots[b + 1] = nt.stats(b + 1)
                if 18 <= fc < 26 and b + 1 < NB:
                    nt.tpose(b + 1, fc - 18)
                if fc == 24 and b >= 1:
                    tk.step()
                    if b == NB - 1:
                        tk.flush()
            stage2(31)
            for s in range(NS):
                for o in range(2):
                    P.i("dve", "tensor_tensor",
                        out=xt[:, s, o * 512:(o + 1) * 512], in0=ps2[s][o][:, :],
                        in1=xt[:, s, o * 512:(o + 1) * 512], op=ALU.add,
                        r=[("ps2", s, o), ("xt", xs)], w=[("ps2", s, o), ("xt", xs)])
            P.dma("sp", rows(dst, b * T, T), xt[:], r=[("xt", xs)], w=res_keys(b * T, T))
        P.barrier()


def lru_phase(nc, P, src, dst, gvec, w_in, conv_w, conv_b, w_r, b_r, w_i, b_i, lam, w_out, pre=None):
    NS = 2
    T = NS * 128
    NB = S // T
    with contextlib.ExitStack() as es:
        cx = Ctx(nc, P, es)
        winb = cx.sb([128, 8, 2 * D], BF16, "winb")
        woutb = cx.sb([128, 8, D], BF16, "woutb")
        wrb = cx.sb([128, 8, 128], BF16, "wrb")
        wib = cx.sb([128, 8, 128], BF16, "wib")
        for c2 in range(4):
            for half in range(2):
                cs = slice(half * D + c2 * 256, half * D + (c2 + 1) * 256)
                P.dma("pool", winb[:, :, cs], w_in[:, cs].rearrange("(kc p) f -> p kc f", p=128),
                      w=[("winb", half, c2)], max_dma_last_dim=8192)
        for g in range(2):
            P.dma("pool", woutb[:, 4 * g:4 * g + 4, :],
                  w_out[g * 512:(g + 1) * 512, :].rearrange("(f p) o -> p f o", p=128), w=[("woutb", g)],
                  max_dma_last_dim=8192)
        for wt, wsrc, key in ((wrb, w_r, "wrb"), (wib, w_i, "wib")):
            P.i("dve", "memset", wt[:], 0.0, w=[key])
            v = wsrc.rearrange("(c two) d e -> two d c e", two=2)
            P.dma("pool", wt[0:64, :, 0:64], v[0], r=[], w=[key])
            P.dma("pool", wt[64:128, :, 64:128], v[1], r=[], w=[key])
        nt = NormT(cx, NS, gvec, nslots_x=3, nslots_h=2, evac="dve")
        nt.load(0, src)
        if NB > 1:
            nt.load(1, src)
        cw = cx.sb([128, 4, 8], F32, "cw")
        for k in range(4):
            P.dma("sp", cw[:, k, :], conv_w[k, :].rearrange("(c p) -> p c", p=128), w=["cw"],
                  allow_slow_non_contiguous=True)
        cb = cx.sb([128, 8], F32, "cb")
        brh = cx.sb([128, 8], F32, "brh")
        bih = cx.sb([128, 8], F32, "bih")
        clh = cx.sb([128, 8], F32, "clh")
        for t, vsrc, key in ((cb, conv_b, "cb"), (brh, b_r, "brh"), (bih, b_i, "bih"), (clh, lam, "clh")):
            P.dma("sp", t[:], vsrc.rearrange("(c p) -> p c", p=128), w=[key], allow_slow_non_contiguous=True)
        P.i("dve", "tensor_scalar", out=brh[:], in0=brh[:], scalar1=0.5, scalar2=None, op0=ALU.mult,
              r=["brh"], w=["brh"])
        P.i("dve", "tensor_scalar", out=bih[:], in0=bih[:], scalar1=0.5, scalar2=None, op0=ALU.mult,
              r=["bih"], w=["bih"])
        P.i("act", "activation", out=clh[:], in_=clh[:], func=AF.Exp, scale=-1.0, r=["clh"], w=["clh"])
        P.i("act", "activation", out=clh[:], in_=clh[:], func=AF.Ln, bias=1.0, r=["clh"], w=["clh"])
        P.i("dve", "tensor_scalar", out=clh[:], in0=clh[:], scalar1=-4.0, scalar2=None, op0=ALU.mult,
              r=["clh"], w=["clh"])
        hstate = cx.sb([128, 8], F32, "hstate")
        tails = cx.sb([128, 8, 4], F32, "tails")
        P.i("dve", "memset", hstate[:], 0.0, w=[("hstate", c) for c in range(8)])
        P.i("dve", "memset", tails[:], 0.0, w=[("tails", c) for c in range(8)])

        psg = [cx.ps([128, 512], F32, "psg%d" % i) for i in range(2)]
        psx = [cx.ps([128, 512], F32, "psx%d" % i) for i in range(2)]
        psr = cx.ps([128, 512], F32, "psr")
        psi = cx.ps([128, 512], F32, "psi")
        pso = psr
        gl = [[cx.sb([128, T], BF16, "gl%d_%d" % (p, c)) for c in range(8)] for p in range(2)]
        xcA = [cx.sb([128, 8, T], F32, "xcA%d" % p) for p in range(2)]
        xc = [[xcA[p][:, c, :] for c in range(8)] for p in range(2)]
        tr = [[cx.sb([128, T], F32, "tr%d_%d" % (p, c)) for c in range(8)] for p in range(2)]
        ti = [[cx.sb([128, T], F32, "ti%d_%d" % (p, c)) for c in range(8)] for p in range(2)]
        aa = [cx.sb([128, T], F32, "aa%d" % c) for c in range(8)]
        xb = [cx.sb([128, T + 4], F32, "xb%d" % i) for i in range(2)]
        xcb = [cx.sb([128, T], BF16, "xcb%d" % i) for i in range(2)]
        yT = [cx.sb([128, 8, T], BF16, "yT%d" % i) for i in range(2)]
        slots = {}

        def XA(b, c):
            p, j = b % 2, c % 2
            xs, hs = slots[b]
            hT = nt.hT[hs]
            for kc in range(8):
                P.i("pe", "matmul", psg[j][:, 0:T], lhsT=winb[:, kc, c * 128:(c + 1) * 128], rhs=hT[:, kc, :],
                    start=(kc == 0), stop=(kc == 7), r=[("winb", 0, c // 2), ("hT", hs, kc)], w=[("psg", j)])
            for kc in range(8):
                P.i("pe", "matmul", psx[j][:, 0:T], lhsT=winb[:, kc, D + c * 128:D + (c + 1) * 128],
                    rhs=hT[:, kc, :], start=(kc == 0), stop=(kc == 7),
                    r=[("winb", 1, c // 2), ("hT", hs, kc)], w=[("psx", j)])
            P.i("act", "activation", out=gl[p][c][:], in_=psg[j][:, 0:T], func=AF.Gelu_apprx_tanh,
                r=[("psg", j)], w=[("psg", j), ("gl", p, c)])
            P.i("act", "copy", out=xb[j][:, 4:T + 4], in_=psx[j][:, 0:T], r=[("psx", j)], w=[("psx", j), ("xbm", j)])
            P.i("pool", "tensor_copy", out=xb[j][:, 0:4], in_=tails[:, c, :], r=[("tails", c)], w=[("xbt", j)])
            P.i("act", "activation", out=xc[p][c][:], in_=psx[j][:, 0:T], func=AF.Identity, scale=cw[:, 3, c:c + 1],
                bias=cb[:, c:c + 1], r=[("psx", j), "cw", "cb"], w=[("psx", j), ("xc", p, c)])
            for k in range(3):
                P.i("dve", "scalar_tensor_tensor", out=xc[p][c][:], in0=xb[j][:, 1 + k:1 + k + T],
                    scalar=cw[:, k, c:c + 1], in1=xc[p][c][:], op0=ALU.mult, op1=ALU.add,
                    r=[("xbm", j), ("xbt", j), "cw", ("xc", p, c)], w=[("xc", p, c)])
            P.i("pool", "tensor_copy", out=tails[:, c, :], in_=xb[j][:, T:T + 4], r=[("xbm", j)], w=[("tails", c)])

        def XB1(b, c):
            p, j = b % 2, c % 2
            P.i("act", "copy", out=xcb[j][:], in_=xc[p][c][:], r=[("xc", p, c)], w=[("xcb", j)])

        def XB2(b, c):
            p, j = b % 2, c % 2
            P.i("pe", "matmul", psr[:, 0:T], lhsT=wrb[:, c, :], rhs=xcb[j][:], start=True, stop=True,
                r=["wrb", ("xcb", j)], w=["psr"])
            P.i("pe", "matmul", psi[:, 0:T], lhsT=wib[:, c, :], rhs=xcb[j][:], start=True, stop=True,
                r=["wib", ("xcb", j)], w=["psi"])
            P.i("act", "activation", out=tr[p][c][:], in_=psr[:, 0:T], func=AF.Tanh, scale=0.5, bias=brh[:, c:c + 1],
                r=["psr", "brh"], w=["psr", ("tr", p, c)])
            P.i("act", "activation", out=ti[p][c][:], in_=psi[:, 0:T], func=AF.Tanh, scale=0.5, bias=bih[:, c:c + 1],
                r=["psi", "bih"], w=["psi", ("ti", p, c)])

        def X_and_Dv(bx, bd):
            if bd is not None and bd >= 1:
                pp = (bd - 1) % 2
                P.i("dve", "tensor_copy", out=hstate[:, :], in_=xcA[pp][:, :, T - 1],
                    r=[("xc", pp, c) for c in range(8)], w=[("hstate", c) for c in range(8)])
            for c in range(10):
                if bd is not None and c < 8:
                    Dv1(bd, c)
                if bx is not None:
                    if c < 8:
                        XA(bx, c)
                    if 1 <= c <= 8:
                        XB1(bx, c - 1)
                    if c >= 2:
                        XB2(bx, c - 2)
                if bd is not None and 1 <= c <= 8:
                    Dv2(bd, c - 1)

        def C(b):
            p = b % 2
            for c in range(8):
                P.i("act", "activation", out=aa[c][:], in_=tr[p][c][:], func=AF.Exp, scale=clh[:, c:c + 1],
                    bias=clh[:, c:c + 1], r=[("tr", p, c), "clh"], w=[("aa", c)])
                P.i("pool", "tensor_tensor", out=tr[p][c][:], in0=aa[c][:], in1=aa[c][:], op=ALU.mult,
                    r=[("aa", c)], w=[("tr", p, c)])
                P.i("dve", "scalar_tensor_tensor", out=ti[p][c][:], in0=ti[p][c][:], scalar=1.0, in1=xc[p][c][:],
                    op0=ALU.add, op1=ALU.mult, r=[("ti", p, c), ("xc", p, c)], w=[("ti", p, c)])

        def Dact(b):
            p = b % 2
            for c in range(8):
                P.i("act", "activation", out=tr[p][c][:], in_=tr[p][c][:], func=AF.Sqrt, scale=-0.25, bias=0.25,
                    r=[("tr", p, c)], w=[("tr", p, c)])

        def Dv1(b, c):
            p = b % 2
            P.i("pool", "tensor_tensor", out=ti[p][c][:], in0=ti[p][c][:], in1=tr[p][c][:], op=ALU.mult,
                r=[("ti", p, c), ("tr", p, c)], w=[("ti", p, c)])
            init, ik = hstate[:, c:c + 1], ("hstate", c)
            P.i("dve", "tensor_tensor_scan", out=xc[p][c][:], data0=aa[c][:], data1=ti[p][c][:],
                initial=init, op0=ALU.mult, op1=ALU.add,
                r=[("aa", c), ("ti", p, c), ik], w=[("xc", p, c)])

        def Dv2(b, c):
            p = b % 2
            P.i("pool", "tensor_tensor", out=yT[p][:, c, :], in0=xc[p][c][:], in1=gl[p][c][:], op=ALU.mult,
                r=[("xc", p, c), ("gl", p, c)], w=[("yT", p, c)])

        def W(b):
            p = b % 2
            xs, hs = slots[b]
            xt = nt.xt[xs]
            for s in range(NS):
                for o in range(2):
                    for c in range(8):
                        P.i("pe", "matmul", pso[:, :], lhsT=yT[p][:, c, s * 128:(s + 1) * 128],
                            rhs=woutb[:, c, o * 512:(o + 1) * 512], start=(c == 0), stop=(c == 7),
                            r=[("yT", p, c), ("woutb", c // 4)], w=["psr"])
                    P.i("dve", "tensor_tensor", out=xt[:, s, o * 512:(o + 1) * 512], in0=pso[:, :],
                        in1=xt[:, s, o * 512:(o + 1) * 512], op=ALU.add,
                        r=["psr", ("xt", xs)], w=["psr", ("xt", xs)])
            P.dma("sp", rows(dst, b * T, T), xt[:], r=[("xt", xs)], w=res_keys(b * T, T))

        tk = Trickle(pre, NB - 2)
        slots[0] = nt.run(0, src, load=False)
        X_and_Dv(0, None)
        if NB > 1:
            slots[1] = nt.run(1, src, load=False)
        for b in range(NB):
            if b + 2 < NB:
                nt.load(b + 2, src)
            if b >= 1:
                tk.step()
                if b == NB - 1:
                    tk.flush()
            C(b)
            Dact(b)
            X_and_Dv(b + 1 if b + 1 < NB else None, b)
            if b + 2 < NB:
                slots[b + 2] = nt.stats(b + 2)
            W(b)
            if b + 2 < NB:
                for kc in range(8):
                    nt.tpose(b + 2, kc)
        P.barrier()


def fox_phase(nc, P, src, dst, gvec, w_in, b_f, q_gain, k_gain, w_out, Qs, Ks, Vs, Rs, cast=True, pre=None):
    NS = 4
    T = NS * 128
    NB = S // T
    NW = 3 * D + NH
    import os
    steps = os.environ.get("FOX_STEPS", "AB")
    if "A" not in steps:
        pass
    else:
      with contextlib.ExitStack() as es:
          cx = Ctx(nc, P, es)
          wfb = cx.sb([128, 8, NW], BF16, "wfb")
          for kc in range(8):
              wload(P, wfb[:, kc, :], w_in[kc * 128:(kc + 1) * 128, :], ("wfb", kc), cast, maxb=4096)
          bd = cx.sb([128, 128], BF16, "bd")
          P.i("dve", "memset", bd[:], 0.0, w=["bd"])
          P.i("dve", "memset", bd[0:64, 0:64], 1.0 / 64.0, w=["bd"])
          P.i("dve", "memset", bd[64:128, 64:128], 1.0 / 64.0, w=["bd"])
          gq = cx.sb([128, 1], F32, "gq")
          gk = cx.sb([128, 1], F32, "gk")
          for t, g, key in ((gq, q_gain, "gq"), (gk, k_gain, "gk")):
              for hh in range(2):
                  P.dma("sp", t[hh * 64:(hh + 1) * 64, :], g.rearrange("(d one) -> d one", one=1), w=[key])
          P.i("dve", "tensor_scalar", out=gq[:], in0=gq[:], scalar1=0.125, scalar2=None, op0=ALU.mult,
                r=["gq"], w=["gq"])
          nbf = cx.sb([NH, 1], F32, "nbf")
          P.dma("sp", nbf[:], b_f.rearrange("(h one) -> h one", one=1), w=["nbf"])
          P.i("dve", "tensor_scalar", out=nbf[:], in0=nbf[:], scalar1=-1.0, scalar2=None, op0=ALU.mult,
                r=["nbf"], w=["nbf"])
          epsb = cx.sb([128, 1], F32, "epsb")
          P.i("pool", "memset", epsb[:], EPS, w=["epsb"])
          ones16 = cx.sb([NH, T], F32, "ones16")
          P.i("pool", "memset", ones16[:], 1.0, w=["ones16"])
          onesb = cx.sb([NH, S], BF16, "onesb")
          P.i("pool", "memset", onesb[:], 1.0, w=["onesb"])
          monesb = cx.sb([NH, S], BF16, "monesb")
          P.i("pool", "memset", monesb[:], -1.0, w=["monesb"])
          for r in range(3):
              P.dma("sp", Qs[:, 67 + r, :], monesb[:], r=["monesb"], w=[("Qaug", 3 + r)])
              P.dma("sp", Ks[:, 64 + r, :], onesb[:], r=["onesb"], w=[("Kaug", r)])
          cstate = cx.sb([NH, 1], F32, "cstate")
          P.i("dve", "memset", cstate[:], 0.0, w=["cstate"])

          nt = NormT(cx, NS, gvec, nslots_x=3, nslots_h=2,
                     pst=[cx.ps([128, 1024], BF16, "pst0")] * 2)
          psq = [cx.ps([128, 512], F32, "psq%d" % i) for i in range(3)]
          psm = [cx.ps([128, 512], F32, "psm%d" % i) for i in range(2)]
          psv = [cx.ps([128, 512], F32, "psv%d" % i) for i in range(2)]
          psf = psv[0]
          sqb = [cx.sb([128, T], BF16, "sqb%d" % i) for i in range(2)]
          mse = [cx.sb([128, T], F32, "mse%d" % i) for i in range(2)]
          rs = [cx.sb([128, T], F32, "rs%d" % i) for i in range(3)]
          qn = [cx.sb([128, T], BF16, "qn%d" % i) for i in range(3)]
          vb = [cx.sb([128, NS, 8, 192], BF16, "vb%d" % i) for i in range(2)]
          for i in range(2):
              P.i("pool", "memset", vb[i][:, :, :, 64:128], 0.0, w=[("vb", i, s_) for s_ in range(NS)])
              P.i("pool", "memset", vb[i][:, :, :, 64:65], 1.0, w=[("vb", i, s_) for s_ in range(NS)])
          e1 = cx.sb([NH, T], F32, "e1")
          cblk = cx.sb([NH, T], F32, "cblk")
          r1 = cx.sb([NH, T], F32, "r1")
          cparts = [[cx.sb([NH, T], BF16, "cp%d_%d" % (i, k)) for k in range(3)] for i in range(2)]

          slots = {}
          slots[0] = nt.run(0, src)
          if NB > 1:
              nt.load(1, src)
          nq = 0
          for b in range(NB):
              xs, hs = slots[b]
              hT = nt.hT[hs]
              cols = slice(b * T, (b + 1) * T)
              if b + 2 < NB:
                  nt.load(b + 2, src)
              def qk_mm(jj):
                  i3 = jj % 3
                  for kc in range(8):
                      P.i("pe", "matmul", psq[i3][:, :], lhsT=wfb[:, kc, jj * 128:(jj + 1) * 128], rhs=hT[:, kc, :],
                          start=(kc == 0), stop=(kc == 7), r=[("wfb", kc), ("hT", hs, kc)], w=[("psq", i3)])
                  P.i("act", "activation", out=sqb[jj % 2][:], in_=psq[i3][:, :], func=AF.Square,
                      r=[("psq", i3)], w=[("psq", i3), ("sqb", jj % 2)])

              def qk_ms(jj):
                  i2 = jj % 2
                  P.i("pe", "matmul", psm[i2][:, :], lhsT=bd[:], rhs=sqb[i2][:], start=True, stop=True,
                      r=["bd", ("sqb", i2)], w=[("psm", i2)])
                  P.i("act", "activation", out=mse[i2][:], in_=psm[i2][:, :], func=AF.Ln, bias=epsb[:, 0:1],
                      r=[("psm", i2), "epsb"], w=[("psm", i2), ("mse", i2)])
                  P.i("act", "activation", out=rs[jj % 3][:], in_=mse[i2][:], func=AF.Exp, scale=-0.5,
                      r=[("mse", i2)], w=[("rs", jj % 3)])

              def qk_out(jj):
                  i3 = jj % 3
                  g = gq if jj < 8 else gk
                  P.i("dve", "scalar_tensor_tensor", out=qn[i3][:], in0=psq[i3][:, :], scalar=g[:, 0:1],
                      in1=rs[i3][:], op0=ALU.mult, op1=ALU.mult,
                      r=[("psq", i3), ("rs", i3), "gq", "gk"], w=[("psq", i3), ("qn", i3)])
                  dstT = Qs if jj < 8 else Ks
                  pr = jj % 8
                  for hh in range(2):
                      P.dma("pool", dstT[2 * pr + hh, 0:64, cols], qn[i3][hh * 64:(hh + 1) * 64, :],
                            r=[("qn", i3)], w=[("QK", jj, b, hh)])

              vs = b % 2

              def v_group(s, o):
                  pv = psv[(s * 2 + o) % 2]
                  pk = ("psv", (s * 2 + o) % 2)
                  for kc in range(8):
                      P.i("pe", "matmul", pv[:, :], lhsT=hT[:, kc, s * 128:(s + 1) * 128],
                          rhs=wfb[:, kc, 2 * D + o * 512:2 * D + (o + 1) * 512],
                          start=(kc == 0), stop=(kc == 7), r=[("wfb", kc), ("hT", hs, kc)], w=[pk])
                  P.i("act", "copy",
                      out=vb[vs][:, s, 4 * o:4 * o + 4, :].rearrange("p a (t e) -> p a t e", e=64)[:, :, 0:3:2, :],
                      in_=pv[:, :].rearrange("p (a t e) -> p a t e", a=4, t=2),
                      r=[pk], w=[pk, ("vb", vs, s)])
                  if o == 1:
                      P.dma("pool", Vs[:, :, b * NS + s, :].rearrange("pr p e -> p pr e"), vb[vs][:, s, :, :],
                            r=[("vb", vs, s)], w=[("V", b, s)])

              for jj in range(18):
                  if jj == 3 and b + 1 < NB:
                      slots[b + 1] = nt.stats(b + 1)
                  if jj < 16:
                      qk_mm(jj)
                  if 8 <= jj < 16 and b + 1 < NB:
                      nt.tpose(b + 1, jj - 8)
                  if jj % 2 == 1 and jj < 16:
                      g = jj // 2
                      v_group(g // 2, g % 2)
                  if 1 <= jj <= 16:
                      qk_ms(jj - 1)
                  if jj >= 2:
                      qk_out(jj - 2)
              for kc in range(8):
                  P.i("pe", "matmul", psf[0:NH, :], lhsT=wfb[:, kc, 3 * D:3 * D + NH],
                                                        rhs=hT[:, kc, :], start=(kc == 0), stop=(kc == 7),
                        r=[("wfb", kc), ("hT", hs, kc)], w=[("psv", 0)])
              P.i("act", "activation", out=e1[:], in_=psf[0:NH, :], func=AF.Exp, scale=-1.0,
                                                  bias=nbf[:, 0:1], r=[("psv", 0), "nbf"], w=[("psv", 0), "e1"])
              P.i("act", "activation", out=e1[:], in_=e1[:], func=AF.Ln, bias=1.0, r=["e1"], w=["e1"])
              P.i("dve", "tensor_tensor_scan", out=cblk[:], data0=ones16[:], data1=e1[:],
                                                          initial=cstate[:, 0:1], op0=ALU.mult, op1=ALU.subtract,
                    r=["ones16", "e1", "cstate"], w=["cblk"])
              P.i("pool", "tensor_copy", out=cstate[:], in_=cblk[:, T - 1:T], r=["cblk"], w=["cstate"])
              ci = b % 2
              cp = cparts[ci]
              P.i("dve", "tensor_copy", out=cp[0][:], in_=cblk[:], r=["cblk"], w=[("cp", ci, 0)])
              P.i("dve", "tensor_tensor", out=r1[:], in0=cblk[:], in1=cp[0][:], op=ALU.subtract,
                    r=["cblk", ("cp", ci, 0)], w=["r1"])
              P.i("dve", "tensor_copy", out=cp[1][:], in_=r1[:], r=["r1"], w=[("cp", ci, 1)])
              P.i("dve", "tensor_tensor", out=r1[:], in0=r1[:], in1=cp[1][:], op=ALU.subtract,
                    r=["r1", ("cp", ci, 1)], w=["r1"])
              P.i("dve", "tensor_copy", out=cp[2][:], in_=r1[:], r=["r1"], w=[("cp", ci, 2)])
              for k in range(3):
                  P.dma("pool", Qs[:, 64 + k, cols], cp[k][:], r=[("cp", ci, k)], w=[("Qaug", k, b)])
                  P.dma("pool", Ks[:, 67 + k, cols], cp[k][:], r=[("cp", ci, k)], w=[("Kaug", 3 + k, b)])
          P.barrier()

    if "B" not in steps:
        return
    with contextlib.ExitStack() as es:
        cx = Ctx(nc, P, es)
        oT = cx.sb([128, 8, S], BF16, "oT")
        wob = cx.sb([128, 8, D], BF16, "wob")
        for g in range(2):
            wload(P, wob[:, 4 * g:4 * g + 4, :],
                  w_out[g * 512:(g + 1) * 512, :].rearrange("(f p) o -> p f o", p=128), ("wob", g), cast)
        maskT = cx.sb([128, 128], F32, "maskT")
        P.i("pool", "memset", maskT[:], 0.0, w=["maskT"])
        P.i("pool", "affine_select", out=maskT[:], in_=maskT[:], pattern=[[1, 128]],
                                                compare_op=ALU.is_ge, fill=NEG, base=0,
                                                channel_multiplier=-1, r=["maskT"], w=["maskT"])
        QT = [cx.sb([70, S], BF16, "QT%d" % i) for i in range(2)]
        KT = [cx.sb([70, S], BF16, "KT%d" % i) for i in range(2)]
        Vp = [cx.sb([128, S // 128, 192], BF16, "Vp%d" % i) for i in range(2)]
        NBUF = 6
        LAG = 4
        pss = [cx.ps([128, 512], F32, "pss%d" % i) for i in range(NBUF)]
        pso = [cx.ps([128, 512], F32, "pso%d" % i) for i in range(2)]
        pT = [cx.sb([128, T], BF16, "pT%d" % i) for i in range(NBUF)]
        rr = [cx.sb([128, T], F32, "rr%d" % i) for i in range(2)]
        rbc = [cx.sb([128, T], F32, "rbc%d" % i) for i in range(2)]

        def head_loads(h):
            pair, odd, hsl, vsl = h // 2, h % 2, h % 2, (h // 2) % 2
            P.dma("sp", QT[hsl][:], Qs[h, :, :], r=[("Qall",)], w=[("QT", hsl)])
            P.dma("sp", KT[hsl][:], Ks[h, :, :], r=[("Kall",)], w=[("KT", hsl)])
            if not odd:
                P.dma("sp", Vp[vsl][:], Vs[pair], r=[("Vall",)], w=[("Vp", vsl)])

        def qk(ti, h, qb, kt):
            hsl = h % 2
            si = ti % NBUF
            jd = kt - 4 * qb
            c0 = max(jd, 0) * 128
            P.i("pe", "matmul", pss[si][:, c0:T], lhsT=KT[hsl][:, kt * 128:(kt + 1) * 128],
                rhs=QT[hsl][:, qb * T + c0:(qb + 1) * T], start=True, stop=True,
                r=[("KT", hsl), ("QT", hsl)], w=[("pss", si)])
            P.i("act", "activation", out=pT[si][:, c0:T], in_=pss[si][:, c0:T], func=AF.Exp,
                r=[("pss", si)], w=[("pss", si), ("pT", si)])
            if jd >= 0:
                P.i("pool", "affine_select", out=pT[si][:, c0:c0 + 128], in_=pT[si][:, c0:c0 + 128],
                    pattern=[[1, 128]], compare_op=ALU.is_ge, fill=0.0, base=0, channel_multiplier=-1,
                    r=[("pT", si)], w=[("pT", si)])
            return si, c0

        def pv(h, qb, kt, gi, si, c0):
            pair, odd, vsl = h // 2, h % 2, (h // 2) % 2
            nkt = 4 * (qb + 1)
            oi = gi % 2
            if odd:
                lhsT = Vp[vsl][:, kt, 64:192]
                out = pso[oi][:, c0:T]
            else:
                lhsT = Vp[vsl][:, kt, 0:128]
                out = pso[oi][:, c0:T]
            P.i("pe", "matmul", out, lhsT=lhsT, rhs=pT[si][:, c0:T], start=(kt == 0), stop=(kt == nkt - 1),
                r=[("pT", si), ("Vp", vsl)], w=[("pso", oi)])
            if kt == nkt - 1:
                ri = oi
                if odd:
                    drow, orows = slice(0, 1), slice(64, 128)
                else:
                    drow, orows = slice(64, 65), slice(0, 64)
                P.i("dve", "reciprocal", out=rr[ri][drow, :], in_=pso[oi][drow, :],
                    r=[("pso", oi)], w=[("pso", oi), ("rr", ri)])
                ridx = h * NB + qb
                P.dma("sp", Rs[ridx:ridx + 1, :], rr[ri][drow, :], r=[("rr", ri)], w=[("Rs", ridx)])
                P.dma("sp", rbc[ri][orows, :], Rs[ridx:ridx + 1, :].partition_broadcast(64),
                      r=[("Rs", ridx)], w=[("rbc", ri)])
                P.i("dve", "tensor_tensor", out=oT[orows, pair, qb * T:(qb + 1) * T], in0=pso[oi][orows, :],
                    in1=rbc[ri][orows, :], op=ALU.mult,
                    r=[("pso", oi), ("rbc", ri)], w=[("pso", oi), ("oT", pair, odd, qb)])

        tk = Trickle(pre, NH - 2)
        head_loads(0)
        pending = []
        ti = 0
        gi = 0
        for h in range(NH):
            if h + 1 < NH:
                head_loads(h + 1)
            if h >= 1:
                tk.step()
                if h == NH - 1:
                    tk.flush()
            for qb in range(NB):
                for kt in range(4 * (qb + 1)):
                    si, c0 = qk(ti, h, qb, kt)
                    pending.append((h, qb, kt, gi, si, c0))
                    ti += 1
                    if len(pending) > LAG:
                        pv(*pending.pop(0))
                gi += 1
        while pending:
            pv(*pending.pop(0))
        psw = [pss[0], pss[1]]
        NS2 = 2
        T2 = NS2 * 128
        xt = [cx.sb([128, NS2, D], F32, "xt%d" % i) for i in range(3)]
        NB2 = S // T2

        def c_load(b):
            P.dma("sp", xt[b % 3][:], rows(src, b * T2, T2), r=res_keys(b * T2, T2), w=[("xt", b % 3)])

        c_load(0)
        for b in range(NB2):
            xs = b % 3
            qb = (b * T2) // T
            if b + 1 < NB2:
                c_load(b + 1)
            for s in range(NS2):
                for o in range(2):
                    wi = (s * 2 + o) % 2
                    for pr in range(8):
                        P.i("pe", "matmul",
                            psw[wi][:, :], lhsT=oT[:, pr, b * T2 + s * 128:b * T2 + (s + 1) * 128],
                            rhs=wob[:, pr, o * 512:(o + 1) * 512], start=(pr == 0), stop=(pr == 7),
                            r=[("oT", pr, 0, qb), ("oT", pr, 1, qb), ("wob", pr // 4)], w=[("pss", wi)])
                    P.i("dve", "tensor_tensor",
                        out=xt[xs][:, s, o * 512:(o + 1) * 512], in0=psw[wi][:, :],
                        in1=xt[xs][:, s, o * 512:(o + 1) * 512], op=ALU.add,
                        r=[("pss", wi), ("xt", xs)], w=[("pss", wi), ("xt", xs)])
            P.dma("sp", rows(dst, b * T2, T2), xt[xs][:], r=[("xt", xs)], w=res_keys(b * T2, T2))
        P.barrier()


PARAM_SHAPES = {
    "mix_norm": [2, D], "mlp_norm": [2, D], "mlp_w1": [2, D, DFF], "mlp_w2": [2, DFF, D],
    "lru_w_in": [1, D, 2 * D], "lru_conv_w": [1, 4, D], "lru_conv_b": [1, D],
    "lru_w_r": [1, 16, 64, 64], "lru_b_r": [1, 16, 64], "lru_w_i": [1, 16, 64, 64], "lru_b_i": [1, 16, 64],
    "lru_lambda": [1, D], "lru_w_out": [1, D, D],
    "fox_w_in": [1, D, 3 * D + NH], "fox_b_f": [1, NH], "fox_q_gain": [1, HD], "fox_k_gain": [1, HD],
    "fox_w_out": [1, D, D],
}


def build(phases=("lru", "mlp0", "fox", "mlp1")):
    nc = bass.Bass("TRN2", target_bir_lowering=False)
    x = nc.dram_tensor("x", [S, D], F32, kind="ExternalInput").ap()
    prm = {k: nc.dram_tensor(k, shp, F32, kind="ExternalInput").ap() for k, shp in PARAM_SHAPES.items()}
    out = nc.dram_tensor("out", [S, D], F32, kind="ExternalOutput").ap()
    Qs = nc.dram_tensor("Qs", [NH, 70, S], BF16, kind="Internal").ap()
    Ks = nc.dram_tensor("Ks", [NH, 70, S], BF16, kind="Internal").ap()
    Vs = nc.dram_tensor("Vs", [8, 128, S // 128, 192], BF16, kind="Internal").ap()
    Rs = nc.dram_tensor("Rs", [NH * (S // 512), 512], F32, kind="Internal").ap()
    w1s = [nc.dram_tensor("w1s%d" % l, [D, DFF], BF16, kind="Internal").ap() for l in range(2)]
    w2s = [nc.dram_tensor("w2s%d" % l, [DFF, D], BF16, kind="Internal").ap() for l in range(2)]
    fwi = nc.dram_tensor("fwi", [D, 3 * D + NH], BF16, kind="Internal").ap()
    fwo = nc.dram_tensor("fwo", [D, D], BF16, kind="Internal").ap()
    P = Prog(nc)

    def caster(ph):
        if ph in ("mlp0", "mlp1"):
            l = int(ph[-1])

            return (cast_rows(P, w1s[l], prm["mlp_w1"][l], "w1s%d" % l)
                    + cast_rows(P, w2s[l], prm["mlp_w2"][l], "w2s%d" % l))
        if ph == "fox":
            return (cast_rows(P, fwi, prm["fox_w_in"][0], "fwi", maxb=4096)
                    + cast_rows(P, fwo, prm["fox_w_out"][0], "fwo"))
        return None

    src = x
    for i, ph in enumerate(phases):
        pre = caster(phases[i + 1]) if i + 1 < len(phases) else None
        cast = (i == 0)
        if ph == "lru":
            lru_phase(nc, P, src, out, prm["mix_norm"][0, :], prm["lru_w_in"][0], prm["lru_conv_w"][0],
                      prm["lru_conv_b"][0, :], prm["lru_w_r"][0], prm["lru_b_r"][0].rearrange("n e -> (n e)"),
                      prm["lru_w_i"][0], prm["lru_b_i"][0].rearrange("n e -> (n e)"),
                      prm["lru_lambda"][0, :], prm["lru_w_out"][0], pre=pre)
        elif ph in ("mlp0", "mlp1"):
            l = int(ph[-1])
            mlp_phase(nc, P, src, out, prm["mlp_w1"][l] if cast else w1s[l], prm["mlp_w2"][l] if cast else w2s[l],
                      prm["mlp_norm"][l, :], cast=cast, pre=pre)
        elif ph == "fox":
            fox_phase(nc, P, src, out, prm["mix_norm"][1, :], prm["fox_w_in"][0] if cast else fwi,
                      prm["fox_b_f"][0, :], prm["fox_q_gain"][0, :], prm["fox_k_gain"][0, :],
                      prm["fox_w_out"][0] if cast else fwo, Qs, Ks, Vs, Rs, cast=cast, pre=pre)
        src = out
    P.emit()
    return nc


def kernel(**inputs):
    n = 8
    nc = build()
    x = np.ascontiguousarray(np.asarray(inputs["x"], dtype=np.float32))
    params = {k: np.ascontiguousarray(np.asarray(inputs[k], dtype=np.float32)) for k in PARAM_SHAPES}
    in_maps = []
    for c in range(n):
        m = {"x": x[c]}
        m.update(params)
        in_maps.append(m)
    res = run_bass_kernel_spmd(nc, in_maps, core_ids=list(range(n)))
    return np.stack([np.asarray(r["out"], dtype=np.float32) for r in res.results], axis=0)
```

```python
import contextlib
import numpy as np
import concourse.bass as bass
import concourse.mybir as mybir
from concourse.bass_utils import run_bass_kernel_spmd

F32 = mybir.dt.float32
BF16 = mybir.dt.bfloat16
AF = mybir.ActivationFunctionType
ALU = mybir.AluOpType

S = 4096
D = 1024
DFF = 4096
NH = 16
HD = 64
EPS = 1e-6
NEG = -30000.0


class Prog:
    NDMA_SEMS = 8

    def __init__(self, nc):
        self.nc = nc
        self.names = ("pe", "act", "dve", "pool", "sp")
        self.sems = {e: nc.alloc_semaphore(name="s_" + e) for e in self.names}
        self.dma_sems = {e: [nc.alloc_semaphore(name="d_%s%d" % (e, k)) for k in range(self.NDMA_SEMS)]
                         for e in ("sp", "act", "pool")}
        self.cnt = {e: 0 for e in self.names}
        self.dcnt = {e: 0 for e in self.dma_sems}
        self.seen = {e: {} for e in self.names}
        self._reset()

    def _reset(self):
        self.ops = []
        self.last_w = {}
        self.readers = {}

    def add(self, eng, fn, r=(), w=(), dma=False):
        idx = len(self.ops)
        deps = {}
        for res in r:
            lw = self.last_w.get(res)
            if lw is not None:
                deps[lw] = "raw"
        for res in w:
            lw = self.last_w.get(res)
            if lw is not None and lw not in deps:
                deps[lw] = "waw"
            for rd in self.readers.get(res, {}).values():
                if rd not in deps:
                    deps[rd] = "war"
        for res in w:
            self.last_w[res] = idx
            self.readers[res] = {}
        for res in r:
            d = self.readers.setdefault(res, {})
            d[(eng, idx) if dma else eng] = idx
        deps.pop(idx, None)
        self.ops.append((eng, fn, deps, dma))
        return idx

    def i(self, eng, method, *args, r=(), w=(), **kw):
        return self.add(eng, lambda e: getattr(e, method)(*args, **kw), r=r, w=w)

    def barrier(self):
        last = {}
        dmas = []
        for idx in range(getattr(self, "_bar_from", 0), len(self.ops)):
            eng, fn, deps, dma = self.ops[idx]
            if dma:
                dmas.append(idx)
            elif fn is not None:
                last[eng] = idx
        self._bar_from = len(self.ops)
        for e in self.names:
            deps = {i: "raw" for x, i in last.items() if x != e}
            for i in dmas:
                deps[i] = "raw"
            self.ops.append((e, None, deps, False))
        self.last_w = {}
        self.readers = {}

    def dma(self, eng, out, in_, r=(), w=(), **kw):
        return self.add(eng, lambda e: e.dma_start(out=out, in_=in_, **kw), r=r, w=w, dma=True)

    def emit(self):
        nc = self.nc
        ops = self.ops
        n = len(ops)
        need = [None] * n
        signal = [False] * n
        for i, (eng, fn, deps, dma) in enumerate(ops):
            keep = []
            for d, kind in deps.items():
                deng, _, _, ddma = ops[d]
                if not ddma and deng == eng and not dma and eng == "pe":
                    continue
                keep.append(d)
                signal[d] = True
            need[i] = keep
        token = [None] * n
        plan = {e: [] for e in self.names}
        seen = self.seen

        def wait(lst, eng, tok):
            sem, val = tok
            s = seen[eng]
            if s.get(sem.num, 0) >= val:
                return
            lst.append((sem, val))
            s[sem.num] = val

        for i, (eng, fn, deps, dma) in enumerate(ops):
            waits = []
            best = {}
            for d in need[i]:
                sem, val = token[d]
                if best.get(sem.num, (None, 0))[1] < val:
                    best[sem.num] = (sem, val)
            for k_ in sorted(best):
                wait(waits, eng, best[k_])
            if dma:
                k = self.dcnt[eng]
                self.dcnt[eng] += 1
                sem = self.dma_sems[eng][k % self.NDMA_SEMS]
                val = 16 * (k // self.NDMA_SEMS + 1)
                if val > 16:
                    wait(waits, eng, (sem, val - 16))
                token[i] = (sem, val)
                plan[eng].append((waits, fn, (sem, 16)))
            else:
                inc = None
                if signal[i] and fn is not None:
                    self.cnt[eng] += 1
                    inc = (self.sems[eng], 1)
                    token[i] = (self.sems[eng], self.cnt[eng])
                plan[eng].append((waits, fn, inc))
        waits = []
        for e, lst in self.dma_sems.items():
            k = self.dcnt[e]
            for j in range(max(0, k - self.NDMA_SEMS), k):
                sem = lst[j % self.NDMA_SEMS]
                wait(waits, "sp", (sem, 16 * (j // self.NDMA_SEMS + 1)))
        plan["sp"].append((waits, None, None))

        def run(engobj, items):
            for waits, fn, inc in items:
                for sem, val in waits:
                    engobj.wait_ge(sem, val)
                if fn is not None:
                    ins = fn(engobj)
                    if inc is not None:
                        ins.then_inc(inc[0], inc[1])

        with nc.Block() as block:
            @block.sync
            def _(e):
                run(e, plan["sp"])

            @block.scalar
            def _(e):
                run(e, plan["act"])

            @block.vector
            def _(e):
                run(e, plan["dve"])

            @block.gpsimd
            def _(e):
                run(e, plan["pool"])

            @block.tensor
            def _(e):
                run(e, plan["pe"])
        self._reset()


class Ctx:
    uid = 0

    def __init__(self, nc, P, es):
        self.nc, self.P, self.es = nc, P, es
        self.n = 0
        Ctx.uid += 1
        self.pfx = "c%d_" % Ctx.uid

    def sb(self, shape, dt, name=None):
        self.n += 1
        return self.es.enter_context(self.nc.sbuf_tensor(self.pfx + (name or "t%d" % self.n), list(shape), dt))

    def ps(self, shape, dt=F32, name=None):
        self.n += 1
        return self.es.enter_context(self.nc.psum_tensor(self.pfx + (name or "p%d" % self.n), list(shape), dt))


def res_keys(t0, nt):
    return [("res", t0 // 128 + i) for i in range(nt // 128)]


def rows(ap, t0, nt):
    return ap[t0:t0 + nt, :].rearrange("(s p) d -> p s d", p=128)


def load_col(cx, dst, vec, eng="sp"):
    cx.P.dma(eng, dst, vec.rearrange("(c p) -> p c", p=128), w=[("col", id(dst))],
             allow_slow_non_contiguous=True)


def make_ident(cx, key="ident"):
    P = cx.P
    ident = cx.sb([128, 128], BF16, "ident")
    P.i("pool", "memset", ident[:], 1.0, w=[key])
    P.i("pool", "affine_select", out=ident[:], in_=ident[:], pattern=[[-1, 128]],
                                            compare_op=ALU.is_ge, fill=0.0, base=0,
                                            channel_multiplier=1, r=[key], w=[key])
    P.i("pool", "affine_select", out=ident[:], in_=ident[:], pattern=[[1, 128]],
                                            compare_op=ALU.is_ge, fill=0.0, base=0,
                                            channel_multiplier=-1, r=[key], w=[key])
    return ident


class NormT:
    def __init__(self, cx, NS, gvec, nslots_x=3, nslots_h=2, pst=None, evac="alt"):
        self.cx, self.NS = cx, NS
        self.evac = evac
        P = cx.P
        self.ident = make_ident(cx)
        self.gcol = cx.sb([128, 8], F32, "gcol")
        P.dma("sp", self.gcol[:], gvec.rearrange("(c p) -> p c", p=128), w=["gcol"],
              allow_slow_non_contiguous=True)
        self.mhalf = cx.sb([128, 8], F32, "mhalf")
        P.i("pool", "memset", self.mhalf[:], -0.5, w=["mhalf"])
        self.xt = [cx.sb([128, NS, D], F32, "xt%d" % i) for i in range(nslots_x)]
        self.hb = cx.sb([128, NS, D], BF16, "hb")
        self.hT = [cx.sb([128, 8, NS * 128], BF16, "hT%d" % i) for i in range(nslots_h)]
        self.junk = cx.sb([128, D], BF16, "junk")
        self.ss = [cx.sb([128, NS], F32, "ss%d" % i) for i in range(2)]
        self.rstd = [cx.sb([128, NS], F32, "rstd%d" % i) for i in range(2)]
        self.pst = pst if pst is not None else [cx.ps([128, 1024], BF16, "pst%d" % i) for i in range(2)]
        self.nx, self.nh = nslots_x, nslots_h
        self.ev = 0

    def load(self, b, src):
        T = self.NS * 128
        xs = b % self.nx
        self.cx.P.dma("sp", self.xt[xs][:], rows(src, b * T, T), r=res_keys(b * T, T), w=[("xt", xs)])

    def stats(self, b):
        NS, P = self.NS, self.cx.P
        xs, sl = b % self.nx, b % 2
        xt, ss, rstd, hb = self.xt[xs], self.ss[sl], self.rstd[sl], self.hb
        for s in range(NS):
            P.i("act", "activation", out=hb[:, s, :], in_=xt[:, s, :], func=AF.Square,
                scale=1.0 / 32.0, accum_out=ss[:, s:s + 1], r=[("xt", xs)], w=[("hb", s), ("ss", sl, s)])
        sk = [("ss", sl, s) for s in range(NS)]
        P.i("dve", "tensor_scalar", out=ss[:], in0=ss[:], scalar1=EPS, scalar2=None, op0=ALU.add, r=sk, w=sk)
        P.i("pool", "tensor_tensor", out=rstd[:], in0=ss[:], in1=self.mhalf[:, 0:NS], op=ALU.pow,
            r=sk + ["mhalf"], w=[("rstd", sl)])
        for s in range(NS):
            P.i("dve", "tensor_scalar", out=hb[:, s, :], in0=xt[:, s, :], scalar1=rstd[:, s:s + 1],
                scalar2=None, op0=ALU.mult, r=[("xt", xs), ("rstd", sl)], w=[("hb", s)])
        return xs, b % self.nh

    def tpose(self, b, kc):
        NS, P = self.NS, self.cx.P
        T = NS * 128
        hs = b % self.nh
        hT, hb = self.hT[hs], self.hb
        pst = self.pst[kc % 2]
        pi = self.pst.index(pst)
        for s in range(NS):
            P.i("pe", "transpose", out=pst[:, s * 128:(s + 1) * 128], in_=hb[:, s, kc * 128:(kc + 1) * 128],
                identity=self.ident[:], r=[("hb", s), "ident"], w=[("pst", pi)])
        self.ev += 1
        if self.ev % 2 == 0 and self.evac == "alt":
            P.i("act", "mul", out=hT[:, kc, :], in_=pst[:, 0:T], mul=self.gcol[:, kc:kc + 1],
                r=[("pst", pi), "gcol"], w=[("pst", pi), ("hT", hs, kc)])
        else:
            P.i("dve", "tensor_scalar", out=hT[:, kc, :], in0=pst[:, 0:T], scalar1=self.gcol[:, kc:kc + 1],
                scalar2=None, op0=ALU.mult, r=[("pst", pi), "gcol"], w=[("pst", pi), ("hT", hs, kc)])

    def run(self, b, src, load=True):
        if load:
            self.load(b, src)
        r = self.stats(b)
        for kc in range(8):
            self.tpose(b, kc)
        return r

    def hT_keys(self, hs):
        return [("hT", hs, kc) for kc in range(8)]


def wload(P, dst, srcw, key, cast, maxb=8192):
    if cast:
        P.dma("pool", dst, srcw, w=[key], max_dma_last_dim=maxb)
    else:
        P.dma("sp", dst, srcw, w=[key])


def cast_rows(P, dst, srcw, name, maxb=8192):
    def piece(r0):
        return lambda: P.dma("pool", dst[r0:r0 + 128, :], srcw[r0:r0 + 128, :], w=[(name, r0)],
                             max_dma_last_dim=maxb)
    return [piece(r0) for r0 in range(0, srcw.shape[0], 128)]


class Trickle:
    def __init__(self, pieces, steps):
        self.pieces = list(pieces or [])
        self.per = -(-len(self.pieces) // max(steps, 1)) if self.pieces else 0

    def step(self):
        for _ in range(self.per):
            if self.pieces:
                self.pieces.pop(0)()

    def flush(self):
        while self.pieces:
            self.pieces.pop(0)()


def mlp_phase(nc, P, src, dst, w1, w2, gvec, cast=True, pre=None):
    NS = 2
    T = NS * 128
    NB = S // T
    with contextlib.ExitStack() as es:
        cx = Ctx(nc, P, es)
        w1b = cx.sb([128, 8, DFF], BF16, "w1b")
        w2b = cx.sb([128, 32, D], BF16, "w2b")
        for g in range(8):
            wload(P, w1b[:, :, g * 512:(g + 1) * 512],
                  w1[:, g * 512:(g + 1) * 512].rearrange("(kc p) f -> p kc f", p=128), ("w1b", g), cast)
            wload(P, w2b[:, 4 * g:4 * g + 4, :],
                  w2[g * 512:(g + 1) * 512, :].rearrange("(f p) o -> p f o", p=128), ("w2b", g), cast)
        tk = Trickle(pre, NB - 2)
        nt = NormT(cx, NS, gvec, nslots_x=3, nslots_h=2)
        ps1 = [cx.ps([128, 512], F32, "ps1_%d" % i) for i in range(2)]
        ps2 = [[cx.ps([128, 512], F32, "ps2_%d%d" % (s, o)) for o in range(2)] for s in range(NS)]
        rt = [cx.sb([128, T], BF16, "rt%d" % i) for i in range(2)]
        aT = [cx.sb([128, T], BF16, "aT%d" % i) for i in range(3)]
        w1keys = [("w1b", kc) for kc in range(8)]

        def stage1(fc, hT, hs):
            pi = fc % 2
            for kc in range(8):
                P.i("pe", "matmul", ps1[pi][:, 0:T], lhsT=w1b[:, kc, fc * 128:(fc + 1) * 128],
                                                      rhs=hT[:, kc, :], start=(kc == 0), stop=(kc == 7),
                      r=[("w1b", fc // 4), ("hT", hs, kc)], w=[("ps1", pi)])
            P.i("act", "activation", out=rt[pi][:], in_=ps1[pi][:, 0:T], func=AF.Relu,
                  r=[("ps1", pi)], w=[("ps1", pi), ("rt", pi)])
            ai = fc % 3
            P.i("dve", "tensor_tensor", out=aT[ai][:], in0=rt[pi][:], in1=rt[pi][:], op=ALU.mult,
                  r=[("rt", pi)], w=[("aT", ai)])

        def stage2(fc):
            ai = fc % 3
            for s in range(NS):
                for o in range(2):
                    P.i("pe", "matmul",
                        ps2[s][o][:, :], lhsT=aT[ai][:, s * 128:(s + 1) * 128],
                        rhs=w2b[:, fc, o * 512:(o + 1) * 512], start=(fc == 0), stop=(fc == 31),
                        r=[("aT", ai), ("w2b", fc // 4)], w=[("ps2", s, o)])

        slots = {}
        slots[0] = nt.run(0, src)
        for b in range(NB):
            xs, hs = slots[b]
            hT, xt = nt.hT[hs], nt.xt[xs]
            for fc in range(32):
                stage1(fc, hT, hs)
                if fc >= 1:
                    stage2(fc - 1)
                if fc == 2 and b + 1 < NB:
                    nt.load(b + 1, src)
                if fc == 10 and b + 1 < NB:
                    slots[b + 1] = nt.stats(b + 1)
                if 18 <= fc < 26 and b + 1 < NB:
                    nt.tpose(b + 1, fc - 18)
                if fc == 24 and b >= 1:
                    tk.step()
                    if b == NB - 1:
                        tk.flush()
            stage2(31)
            for s in range(NS):
                for o in range(2):
                    P.i("dve", "tensor_tensor",
                        out=xt[:, s, o * 512:(o + 1) * 512], in0=ps2[s][o][:, :],
                        in1=xt[:, s, o * 512:(o + 1) * 512], op=ALU.add,
                        r=[("ps2", s, o), ("xt", xs)], w=[("ps2", s, o), ("xt", xs)])
            P.dma("sp", rows(dst, b * T, T), xt[:], r=[("xt", xs)], w=res_keys(b * T, T))
        P.barrier()


def lru_phase(nc, P, src, dst, gvec, w_in, conv_w, conv_b, w_r, b_r, w_i, b_i, lam, w_out, pre=None):
    NS = 2
    T = NS * 128
    NB = S // T
    with contextlib.ExitStack() as es:
        cx = Ctx(nc, P, es)
        winb = cx.sb([128, 8, 2 * D], BF16, "winb")
        woutb = cx.sb([128, 8, D], BF16, "woutb")
        wrb = cx.sb([128, 8, 128], BF16, "wrb")
        wib = cx.sb([128, 8, 128], BF16, "wib")
        for c2 in range(4):
            for half in range(2):
                cs = slice(half * D + c2 * 256, half * D + (c2 + 1) * 256)
                P.dma("pool", winb[:, :, cs], w_in[:, cs].rearrange("(kc p) f -> p kc f", p=128),
                      w=[("winb", half, c2)], max_dma_last_dim=8192)
        for g in range(2):
            P.dma("pool", woutb[:, 4 * g:4 * g + 4, :],
                  w_out[g * 512:(g + 1) * 512, :].rearrange("(f p) o -> p f o", p=128), w=[("woutb", g)],
                  max_dma_last_dim=8192)
        for wt, wsrc, key in ((wrb, w_r, "wrb"), (wib, w_i, "wib")):
            P.i("dve", "memset", wt[:], 0.0, w=[key])
            v = wsrc.rearrange("(c two) d e -> two d c e", two=2)
            P.dma("pool", wt[0:64, :, 0:64], v[0], r=[], w=[key])
            P.dma("pool", wt[64:128, :, 64:128], v[1], r=[], w=[key])
        nt = NormT(cx, NS, gvec, nslots_x=3, nslots_h=2, evac="dve")
        nt.load(0, src)
        if NB > 1:
            nt.load(1, src)
        cw = cx.sb([128, 4, 8], F32, "cw")
        for k in range(4):
            P.dma("sp", cw[:, k, :], conv_w[k, :].rearrange("(c p) -> p c", p=128), w=["cw"],
                  allow_slow_non_contiguous=True)
        cb = cx.sb([128, 8], F32, "cb")
        brh = cx.sb([128, 8], F32, "brh")
        bih = cx.sb([128, 8], F32, "bih")
        clh = cx.sb([128, 8], F32, "clh")
        for t, vsrc, key in ((cb, conv_b, "cb"), (brh, b_r, "brh"), (bih, b_i, "bih"), (clh, lam, "clh")):
            P.dma("sp", t[:], vsrc.rearrange("(c p) -> p c", p=128), w=[key], allow_slow_non_contiguous=True)
        P.i("dve", "tensor_scalar", out=brh[:], in0=brh[:], scalar1=0.5, scalar2=None, op0=ALU.mult,
              r=["brh"], w=["brh"])
        P.i("dve", "tensor_scalar", out=bih[:], in0=bih[:], scalar1=0.5, scalar2=None, op0=ALU.mult,
              r=["bih"], w=["bih"])
        P.i("act", "activation", out=clh[:], in_=clh[:], func=AF.Exp, scale=-1.0, r=["clh"], w=["clh"])
        P.i("act", "activation", out=clh[:], in_=clh[:], func=AF.Ln, bias=1.0, r=["clh"], w=["clh"])
        P.i("dve", "tensor_scalar", out=clh[:], in0=clh[:], scalar1=-4.0, scalar2=None, op0=ALU.mult,
              r=["clh"], w=["clh"])
        hstate = cx.sb([128, 8], F32, "hstate")
        tails = cx.sb([128, 8, 4], F32, "tails")
        P.i("dve", "memset", hstate[:], 0.0, w=[("hstate", c) for c in range(8)])
        P.i("dve", "memset", tails[:], 0.0, w=[("tails", c) for c in range(8)])

        psg = [cx.ps([128, 512], F32, "psg%d" % i) for i in range(2)]
        psx = [cx.ps([128, 512], F32, "psx%d" % i) for i in range(2)]
        psr = cx.ps([128, 512], F32, "psr")
        psi = cx.ps([128, 512], F32, "psi")
        pso = psr
        gl = [[cx.sb([128, T], BF16, "gl%d_%d" % (p, c)) for c in range(8)] for p in range(2)]
        xcA = [cx.sb([128, 8, T], F32, "xcA%d" % p) for p in range(2)]
        xc = [[xcA[p][:, c, :] for c in range(8)] for p in range(2)]
        tr = [[cx.sb([128, T], F32, "tr%d_%d" % (p, c)) for c in range(8)] for p in range(2)]
        ti = [[cx.sb([128, T], F32, "ti%d_%d" % (p, c)) for c in range(8)] for p in range(2)]
        aa = [cx.sb([128, T], F32, "aa%d" % c) for c in range(8)]
        xb = [cx.sb([128, T + 4], F32, "xb%d" % i) for i in range(2)]
        xcb = [cx.sb([128, T], BF16, "xcb%d" % i) for i in range(2)]
        yT = [cx.sb([128, 8, T], BF16, "yT%d" % i) for i in range(2)]
        slots = {}

        def XA(b, c):
            p, j = b % 2, c % 2
            xs, hs = slots[b]
            hT = nt.hT[hs]
            for kc in range(8):
                P.i("pe", "matmul", psg[j][:, 0:T], lhsT=winb[:, kc, c * 128:(c + 1) * 128], rhs=hT[:, kc, :],
                    start=(kc == 0), stop=(kc == 7), r=[("winb", 0, c // 2), ("hT", hs, kc)], w=[("psg", j)])
            for kc in range(8):
                P.i("pe", "matmul", psx[j][:, 0:T], lhsT=winb[:, kc, D + c * 128:D + (c + 1) * 128],
                    rhs=hT[:, kc, :], start=(kc == 0), stop=(kc == 7),
                    r=[("winb", 1, c // 2), ("hT", hs, kc)], w=[("psx", j)])
            P.i("act", "activation", out=gl[p][c][:], in_=psg[j][:, 0:T], func=AF.Gelu_apprx_tanh,
                r=[("psg", j)], w=[("psg", j), ("gl", p, c)])
            P.i("act", "copy", out=xb[j][:, 4:T + 4], in_=psx[j][:, 0:T], r=[("psx", j)], w=[("psx", j), ("xbm", j)])
            P.i("pool", "tensor_copy", out=xb[j][:, 0:4], in_=tails[:, c, :], r=[("tails", c)], w=[("xbt", j)])
            P.i("act", "activation", out=xc[p][c][:], in_=psx[j][:, 0:T], func=AF.Identity, scale=cw[:, 3, c:c + 1],
                bias=cb[:, c:c + 1], r=[("psx", j), "cw", "cb"], w=[("psx", j), ("xc", p, c)])
            for k in range(3):
                P.i("dve", "scalar_tensor_tensor", out=xc[p][c][:], in0=xb[j][:, 1 + k:1 + k + T],
                    scalar=cw[:, k, c:c + 1], in1=xc[p][c][:], op0=ALU.mult, op1=ALU.add,
                    r=[("xbm", j), ("xbt", j), "cw", ("xc", p, c)], w=[("xc", p, c)])
            P.i("pool", "tensor_copy", out=tails[:, c, :], in_=xb[j][:, T:T + 4], r=[("xbm", j)], w=[("tails", c)])

        def XB1(b, c):
            p, j = b % 2, c % 2
            P.i("act", "copy", out=xcb[j][:], in_=xc[p][c][:], r=[("xc", p, c)], w=[("xcb", j)])

        def XB2(b, c):
            p, j = b % 2, c % 2
            P.i("pe", "matmul", psr[:, 0:T], lhsT=wrb[:, c, :], rhs=xcb[j][:], start=True, stop=True,
                r=["wrb", ("xcb", j)], w=["psr"])
            P.i("pe", "matmul", psi[:, 0:T], lhsT=wib[:, c, :], rhs=xcb[j][:], start=True, stop=True,
                r=["wib", ("xcb", j)], w=["psi"])
            P.i("act", "activation", out=tr[p][c][:], in_=psr[:, 0:T], func=AF.Tanh, scale=0.5, bias=brh[:, c:c + 1],
                r=["psr", "brh"], w=["psr", ("tr", p, c)])
            P.i("act", "activation", out=ti[p][c][:], in_=psi[:, 0:T], func=AF.Tanh, scale=0.5, bias=bih[:, c:c + 1],
                r=["psi", "bih"], w=["psi", ("ti", p, c)])

        def X_and_Dv(bx, bd):
            if bd is not None and bd >= 1:
                pp = (bd - 1) % 2
                P.i("dve", "tensor_copy", out=hstate[:, :], in_=xcA[pp][:, :, T - 1],
                    r=[("xc", pp, c) for c in range(8)], w=[("hstate", c) for c in range(8)])
            for c in range(10):
                if bd is not None and c < 8:
                    Dv1(bd, c)
                if bx is not None:
                    if c < 8:
                        XA(bx, c)
                    if 1 <= c <= 8:
                        XB1(bx, c - 1)
                    if c >= 2:
                        XB2(bx, c - 2)
                if bd is not None and 1 <= c <= 8:
                    Dv2(bd, c - 1)

        def C(b):
            p = b % 2
            for c in range(8):
                P.i("act", "activation", out=aa[c][:], in_=tr[p][c][:], func=AF.Exp, scale=clh[:, c:c + 1],
                    bias=clh[:, c:c + 1], r=[("tr", p, c), "clh"], w=[("aa", c)])
                P.i("pool", "tensor_scalar", out=tr[p][c][:], in0=aa[c][:], scalar1=1.0, scalar2=0.0,
                    op0=ALU.min, op1=ALU.max, r=[("aa", c)], w=[("tr", p, c)])
                P.i("pool", "tensor_tensor", out=tr[p][c][:], in0=tr[p][c][:], in1=tr[p][c][:], op=ALU.mult,
                    r=[("tr", p, c)], w=[("tr", p, c)])
                P.i("dve", "scalar_tensor_tensor", out=ti[p][c][:], in0=ti[p][c][:], scalar=1.0, in1=xc[p][c][:],
                    op0=ALU.add, op1=ALU.mult, r=[("ti", p, c), ("xc", p, c)], w=[("ti", p, c)])

        def Dact(b):
            p = b % 2
            for c in range(8):
                P.i("act", "activation", out=tr[p][c][:], in_=tr[p][c][:], func=AF.Sqrt, scale=-0.25, bias=0.25,
                    r=[("tr", p, c)], w=[("tr", p, c)])

        def Dv1(b, c):
            p = b % 2
            P.i("pool", "tensor_tensor", out=ti[p][c][:], in0=ti[p][c][:], in1=tr[p][c][:], op=ALU.mult,
                r=[("ti", p, c), ("tr", p, c)], w=[("ti", p, c)])
            init, ik = hstate[:, c:c + 1], ("hstate", c)
            P.i("dve", "tensor_tensor_scan", out=xc[p][c][:], data0=aa[c][:], data1=ti[p][c][:],
                initial=init, op0=ALU.mult, op1=ALU.add,
                r=[("aa", c), ("ti", p, c), ik], w=[("xc", p, c)])

        def Dv2(b, c):
            p = b % 2
            P.i("pool", "tensor_tensor", out=yT[p][:, c, :], in0=xc[p][c][:], in1=gl[p][c][:], op=ALU.mult,
                r=[("xc", p, c), ("gl", p, c)], w=[("yT", p, c)])

        def W(b):
            p = b % 2
            xs, hs = slots[b]
            xt = nt.xt[xs]
            for s in range(NS):
                for o in range(2):
                    for c in range(8):
                        P.i("pe", "matmul", pso[:, :], lhsT=yT[p][:, c, s * 128:(s + 1) * 128],
                            rhs=woutb[:, c, o * 512:(o + 1) * 512], start=(c == 0), stop=(c == 7),
                            r=[("yT", p, c), ("woutb", c // 4)], w=["psr"])
                    P.i("dve", "tensor_tensor", out=xt[:, s, o * 512:(o + 1) * 512], in0=pso[:, :],
                        in1=xt[:, s, o * 512:(o + 1) * 512], op=ALU.add,
                        r=["psr", ("xt", xs)], w=["psr", ("xt", xs)])
            P.dma("sp", rows(dst, b * T, T), xt[:], r=[("xt", xs)], w=res_keys(b * T, T))

        tk = Trickle(pre, NB - 2)
        slots[0] = nt.run(0, src, load=False)
        X_and_Dv(0, None)
        if NB > 1:
            slots[1] = nt.run(1, src, load=False)
        for b in range(NB):
            if b + 2 < NB:
                nt.load(b + 2, src)
            if b >= 1:
                tk.step()
                if b == NB - 1:
                    tk.flush()
            C(b)
            Dact(b)
            X_and_Dv(b + 1 if b + 1 < NB else None, b)
            if b + 2 < NB:
                slots[b + 2] = nt.stats(b + 2)
            W(b)
            if b + 2 < NB:
                for kc in range(8):
                    nt.tpose(b + 2, kc)
        P.barrier()


def fox_phase(nc, P, src, dst, gvec, w_in, b_f, q_gain, k_gain, w_out, Qs, Ks, Vs, Rs, cast=True, pre=None):
    NS = 4
    T = NS * 128
    NB = S // T
    NW = 3 * D + NH
    if True:
      with contextlib.ExitStack() as es:
          cx = Ctx(nc, P, es)
          wfb = cx.sb([128, 8, NW], BF16, "wfb")
          for kc in range(8):
              wload(P, wfb[:, kc, :], w_in[kc * 128:(kc + 1) * 128, :], ("wfb", kc), cast, maxb=4096)
          bd = cx.sb([128, 128], BF16, "bd")
          P.i("dve", "memset", bd[:], 0.0, w=["bd"])
          P.i("dve", "memset", bd[0:64, 0:64], 1.0 / 64.0, w=["bd"])
          P.i("dve", "memset", bd[64:128, 64:128], 1.0 / 64.0, w=["bd"])
          gq = cx.sb([128, 1], F32, "gq")
          gk = cx.sb([128, 1], F32, "gk")
          for t, g, key in ((gq, q_gain, "gq"), (gk, k_gain, "gk")):
              for hh in range(2):
                  P.dma("sp", t[hh * 64:(hh + 1) * 64, :], g.rearrange("(d one) -> d one", one=1), w=[key])
          P.i("dve", "tensor_scalar", out=gq[:], in0=gq[:], scalar1=0.125, scalar2=None, op0=ALU.mult,
                r=["gq"], w=["gq"])
          nbf = cx.sb([NH, 1], F32, "nbf")
          P.dma("sp", nbf[:], b_f.rearrange("(h one) -> h one", one=1), w=["nbf"])
          P.i("dve", "tensor_scalar", out=nbf[:], in0=nbf[:], scalar1=-1.0, scalar2=None, op0=ALU.mult,
                r=["nbf"], w=["nbf"])
          epsb = cx.sb([128, 1], F32, "epsb")
          P.i("pool", "memset", epsb[:], EPS, w=["epsb"])
          ones16 = cx.sb([NH, T], F32, "ones16")
          P.i("pool", "memset", ones16[:], 1.0, w=["ones16"])
          onesb = cx.sb([NH, S], BF16, "onesb")
          P.i("pool", "memset", onesb[:], 1.0, w=["onesb"])
          monesb = cx.sb([NH, S], BF16, "monesb")
          P.i("pool", "memset", monesb[:], -1.0, w=["monesb"])
          for r in range(3):
              P.dma("sp", Qs[:, 67 + r, :], monesb[:], r=["monesb"], w=[("Qaug", 3 + r)])
              P.dma("sp", Ks[:, 64 + r, :], onesb[:], r=["onesb"], w=[("Kaug", r)])
          cstate = cx.sb([NH, 1], F32, "cstate")
          P.i("dve", "memset", cstate[:], 0.0, w=["cstate"])

          nt = NormT(cx, NS, gvec, nslots_x=3, nslots_h=2,
                     pst=[cx.ps([128, 1024], BF16, "pst0")] * 2)
          psq = [cx.ps([128, 512], F32, "psq%d" % i) for i in range(3)]
          psm = [cx.ps([128, 512], F32, "psm%d" % i) for i in range(2)]
          psv = [cx.ps([128, 512], F32, "psv%d" % i) for i in range(2)]
          psf = psv[0]
          sqb = [cx.sb([128, T], BF16, "sqb%d" % i) for i in range(2)]
          mse = [cx.sb([128, T], F32, "mse%d" % i) for i in range(2)]
          rs = [cx.sb([128, T], F32, "rs%d" % i) for i in range(3)]
          qn = [cx.sb([128, T], BF16, "qn%d" % i) for i in range(3)]
          vb = [cx.sb([128, NS, 8, 192], BF16, "vb%d" % i) for i in range(2)]
          for i in range(2):
              P.i("pool", "memset", vb[i][:, :, :, 64:128], 0.0, w=[("vb", i, s_) for s_ in range(NS)])
              P.i("pool", "memset", vb[i][:, :, :, 64:65], 1.0, w=[("vb", i, s_) for s_ in range(NS)])
          e1 = cx.sb([NH, T], F32, "e1")
          cblk = cx.sb([NH, T], F32, "cblk")
          r1 = cx.sb([NH, T], F32, "r1")
          cparts = [[cx.sb([NH, T], BF16, "cp%d_%d" % (i, k)) for k in range(3)] for i in range(2)]

          slots = {}
          slots[0] = nt.run(0, src)
          if NB > 1:
              nt.load(1, src)
          nq = 0
          for b in range(NB):
              xs, hs = slots[b]
              hT = nt.hT[hs]
              cols = slice(b * T, (b + 1) * T)
              if b + 2 < NB:
                  nt.load(b + 2, src)
              def qk_mm(jj):
                  i3 = jj % 3
                  for kc in range(8):
                      P.i("pe", "matmul", psq[i3][:, :], lhsT=wfb[:, kc, jj * 128:(jj + 1) * 128], rhs=hT[:, kc, :],
                          start=(kc == 0), stop=(kc == 7), r=[("wfb", kc), ("hT", hs, kc)], w=[("psq", i3)])
                  P.i("act", "activation", out=sqb[jj % 2][:], in_=psq[i3][:, :], func=AF.Square,
                      r=[("psq", i3)], w=[("psq", i3), ("sqb", jj % 2)])

              def qk_ms(jj):
                  i2 = jj % 2
                  P.i("pe", "matmul", psm[i2][:, :], lhsT=bd[:], rhs=sqb[i2][:], start=True, stop=True,
                      r=["bd", ("sqb", i2)], w=[("psm", i2)])
                  P.i("act", "activation", out=mse[i2][:], in_=psm[i2][:, :], func=AF.Ln, bias=epsb[:, 0:1],
                      r=[("psm", i2), "epsb"], w=[("psm", i2), ("mse", i2)])
                  P.i("act", "activation", out=rs[jj % 3][:], in_=mse[i2][:], func=AF.Exp, scale=-0.5,
                      r=[("mse", i2)], w=[("rs", jj % 3)])

              def qk_out(jj):
                  i3 = jj % 3
                  g = gq if jj < 8 else gk
                  P.i("dve", "scalar_tensor_tensor", out=qn[i3][:], in0=psq[i3][:, :], scalar=g[:, 0:1],
                      in1=rs[i3][:], op0=ALU.mult, op1=ALU.mult,
                      r=[("psq", i3), ("rs", i3), "gq", "gk"], w=[("psq", i3), ("qn", i3)])
                  dstT = Qs if jj < 8 else Ks
                  pr = jj % 8
                  for hh in range(2):
                      P.dma("pool", dstT[2 * pr + hh, 0:64, cols], qn[i3][hh * 64:(hh + 1) * 64, :],
                            r=[("qn", i3)], w=[("QK", jj, b, hh)])

              vs = b % 2

              def v_group(s, o):
                  pv = psv[(s * 2 + o) % 2]
                  pk = ("psv", (s * 2 + o) % 2)
                  for kc in range(8):
                      P.i("pe", "matmul", pv[:, :], lhsT=hT[:, kc, s * 128:(s + 1) * 128],
                          rhs=wfb[:, kc, 2 * D + o * 512:2 * D + (o + 1) * 512],
                          start=(kc == 0), stop=(kc == 7), r=[("wfb", kc), ("hT", hs, kc)], w=[pk])
                  P.i("act", "copy",
                      out=vb[vs][:, s, 4 * o:4 * o + 4, :].rearrange("p a (t e) -> p a t e", e=64)[:, :, 0:3:2, :],
                      in_=pv[:, :].rearrange("p (a t e) -> p a t e", a=4, t=2),
                      r=[pk], w=[pk, ("vb", vs, s)])
                  if o == 1:
                      P.dma("pool", Vs[:, :, b * NS + s, :].rearrange("pr p e -> p pr e"), vb[vs][:, s, :, :],
                            r=[("vb", vs, s)], w=[("V", b, s)])

              for jj in range(18):
                  if jj == 3 and b + 1 < NB:
                      slots[b + 1] = nt.stats(b + 1)
                  if jj < 16:
                      qk_mm(jj)
                  if 8 <= jj < 16 and b + 1 < NB:
                      nt.tpose(b + 1, jj - 8)
                  if jj % 2 == 1 and jj < 16:
                      g = jj // 2
                      v_group(g // 2, g % 2)
                  if 1 <= jj <= 16:
                      qk_ms(jj - 1)
                  if jj >= 2:
                      qk_out(jj - 2)
              for kc in range(8):
                  P.i("pe", "matmul", psf[0:NH, :], lhsT=wfb[:, kc, 3 * D:3 * D + NH],
                                                        rhs=hT[:, kc, :], start=(kc == 0), stop=(kc == 7),
                        r=[("wfb", kc), ("hT", hs, kc)], w=[("psv", 0)])
              P.i("act", "activation", out=e1[:], in_=psf[0:NH, :], func=AF.Exp, scale=-1.0,
                                                  bias=nbf[:, 0:1], r=[("psv", 0), "nbf"], w=[("psv", 0), "e1"])
              P.i("act", "activation", out=e1[:], in_=e1[:], func=AF.Ln, bias=1.0, r=["e1"], w=["e1"])
              P.i("dve", "tensor_tensor_scan", out=cblk[:], data0=ones16[:], data1=e1[:],
                                                          initial=cstate[:, 0:1], op0=ALU.mult, op1=ALU.subtract,
                    r=["ones16", "e1", "cstate"], w=["cblk"])
              P.i("pool", "tensor_copy", out=cstate[:], in_=cblk[:, T - 1:T], r=["cblk"], w=["cstate"])
              ci = b % 2
              cp = cparts[ci]
              P.i("dve", "tensor_copy", out=cp[0][:], in_=cblk[:], r=["cblk"], w=[("cp", ci, 0)])
              P.i("dve", "tensor_tensor", out=r1[:], in0=cblk[:], in1=cp[0][:], op=ALU.subtract,
                    r=["cblk", ("cp", ci, 0)], w=["r1"])
              P.i("dve", "tensor_copy", out=cp[1][:], in_=r1[:], r=["r1"], w=[("cp", ci, 1)])
              P.i("dve", "tensor_tensor", out=r1[:], in0=r1[:], in1=cp[1][:], op=ALU.subtract,
                    r=["r1", ("cp", ci, 1)], w=["r1"])
              P.i("dve", "tensor_copy", out=cp[2][:], in_=r1[:], r=["r1"], w=[("cp", ci, 2)])
              for k in range(3):
                  P.dma("pool", Qs[:, 64 + k, cols], cp[k][:], r=[("cp", ci, k)], w=[("Qaug", k, b)])
                  P.dma("pool", Ks[:, 67 + k, cols], cp[k][:], r=[("cp", ci, k)], w=[("Kaug", 3 + k, b)])
          P.barrier()

    with contextlib.ExitStack() as es:
        cx = Ctx(nc, P, es)
        oT = cx.sb([128, 8, S], BF16, "oT")
        wob = cx.sb([128, 8, D], BF16, "wob")
        for g in range(2):
            wload(P, wob[:, 4 * g:4 * g + 4, :],
                  w_out[g * 512:(g + 1) * 512, :].rearrange("(f p) o -> p f o", p=128), ("wob", g), cast)
        maskT = cx.sb([128, 128], F32, "maskT")
        P.i("pool", "memset", maskT[:], 0.0, w=["maskT"])
        P.i("pool", "affine_select", out=maskT[:], in_=maskT[:], pattern=[[1, 128]],
                                                compare_op=ALU.is_ge, fill=NEG, base=0,
                                                channel_multiplier=-1, r=["maskT"], w=["maskT"])
        QT = [cx.sb([70, S], BF16, "QT%d" % i) for i in range(2)]
        KT = [cx.sb([70, S], BF16, "KT%d" % i) for i in range(2)]
        Vp = [cx.sb([128, S // 128, 192], BF16, "Vp%d" % i) for i in range(2)]
        NBUF = 6
        LAG = 4
        pss = [cx.ps([128, 512], F32, "pss%d" % i) for i in range(NBUF)]
        pso = [cx.ps([128, 512], F32, "pso%d" % i) for i in range(2)]
        pT = [cx.sb([128, T], BF16, "pT%d" % i) for i in range(NBUF)]
        rr = [cx.sb([128, T], F32, "rr%d" % i) for i in range(2)]
        rbc = [cx.sb([128, T], F32, "rbc%d" % i) for i in range(2)]

        def head_loads(h):
            pair, odd, hsl, vsl = h // 2, h % 2, h % 2, (h // 2) % 2
            P.dma("sp", QT[hsl][:], Qs[h, :, :], r=[("Qall",)], w=[("QT", hsl)])
            P.dma("sp", KT[hsl][:], Ks[h, :, :], r=[("Kall",)], w=[("KT", hsl)])
            if not odd:
                P.dma("sp", Vp[vsl][:], Vs[pair], r=[("Vall",)], w=[("Vp", vsl)])

        def qk(ti, h, qb, kt):
            hsl = h % 2
            si = ti % NBUF
            jd = kt - 4 * qb
            c0 = max(jd, 0) * 128
            P.i("pe", "matmul", pss[si][:, c0:T], lhsT=KT[hsl][:, kt * 128:(kt + 1) * 128],
                rhs=QT[hsl][:, qb * T + c0:(qb + 1) * T], start=True, stop=True,
                r=[("KT", hsl), ("QT", hsl)], w=[("pss", si)])
            P.i("act", "activation", out=pT[si][:, c0:T], in_=pss[si][:, c0:T], func=AF.Exp,
                r=[("pss", si)], w=[("pss", si), ("pT", si)])
            if jd >= 0:
                P.i("pool", "affine_select", out=pT[si][:, c0:c0 + 128], in_=pT[si][:, c0:c0 + 128],
                    pattern=[[1, 128]], compare_op=ALU.is_ge, fill=0.0, base=0, channel_multiplier=-1,
                    r=[("pT", si)], w=[("pT", si)])
            return si, c0

        def pv(h, qb, kt, gi, si, c0):
            pair, odd, vsl = h // 2, h % 2, (h // 2) % 2
            nkt = 4 * (qb + 1)
            oi = gi % 2
            if odd:
                lhsT = Vp[vsl][:, kt, 64:192]
                out = pso[oi][:, c0:T]
            else:
                lhsT = Vp[vsl][:, kt, 0:128]
                out = pso[oi][:, c0:T]
            P.i("pe", "matmul", out, lhsT=lhsT, rhs=pT[si][:, c0:T], start=(kt == 0), stop=(kt == nkt - 1),
                r=[("pT", si), ("Vp", vsl)], w=[("pso", oi)])
            if kt == nkt - 1:
                ri = oi
                if odd:
                    drow, orows = slice(0, 1), slice(64, 128)
                else:
                    drow, orows = slice(64, 65), slice(0, 64)
                P.i("dve", "reciprocal", out=rr[ri][drow, :], in_=pso[oi][drow, :],
                    r=[("pso", oi)], w=[("pso", oi), ("rr", ri)])
                ridx = h * NB + qb
                P.dma("sp", Rs[ridx:ridx + 1, :], rr[ri][drow, :], r=[("rr", ri)], w=[("Rs", ridx)])
                P.dma("sp", rbc[ri][orows, :], Rs[ridx:ridx + 1, :].partition_broadcast(64),
                      r=[("Rs", ridx)], w=[("rbc", ri)])
                P.i("dve", "tensor_tensor", out=oT[orows, pair, qb * T:(qb + 1) * T], in0=pso[oi][orows, :],
                    in1=rbc[ri][orows, :], op=ALU.mult,
                    r=[("pso", oi), ("rbc", ri)], w=[("pso", oi), ("oT", pair, odd, qb)])

        tk = Trickle(pre, NH - 2)
        head_loads(0)
        pending = []
        ti = 0
        gi = 0
        for h in range(NH):
            if h + 1 < NH:
                head_loads(h + 1)
            if h >= 1:
                tk.step()
                if h == NH - 1:
                    tk.flush()
            for qb in range(NB):
                for kt in range(4 * (qb + 1)):
                    si, c0 = qk(ti, h, qb, kt)
                    pending.append((h, qb, kt, gi, si, c0))
                    ti += 1
                    if len(pending) > LAG:
                        pv(*pending.pop(0))
                gi += 1
        while pending:
            pv(*pending.pop(0))
        psw = [pss[0], pss[1]]
        NS2 = 2
        T2 = NS2 * 128
        xt = [cx.sb([128, NS2, D], F32, "xt%d" % i) for i in range(3)]
        NB2 = S // T2

        def c_load(b):
            P.dma("sp", xt[b % 3][:], rows(src, b * T2, T2), r=res_keys(b * T2, T2), w=[("xt", b % 3)])

        c_load(0)
        for b in range(NB2):
            xs = b % 3
            qb = (b * T2) // T
            if b + 1 < NB2:
                c_load(b + 1)
            for s in range(NS2):
                for o in range(2):
                    wi = (s * 2 + o) % 2
                    for pr in range(8):
                        P.i("pe", "matmul",
                            psw[wi][:, :], lhsT=oT[:, pr, b * T2 + s * 128:b * T2 + (s + 1) * 128],
                            rhs=wob[:, pr, o * 512:(o + 1) * 512], start=(pr == 0), stop=(pr == 7),
                            r=[("oT", pr, 0, qb), ("oT", pr, 1, qb), ("wob", pr // 4)], w=[("pss", wi)])
                    P.i("dve", "tensor_tensor",
                        out=xt[xs][:, s, o * 512:(o + 1) * 512], in0=psw[wi][:, :],
                        in1=xt[xs][:, s, o * 512:(o + 1) * 512], op=ALU.add,
                        r=[("pss", wi), ("xt", xs)], w=[("pss", wi), ("xt", xs)])
            P.dma("sp", rows(dst, b * T2, T2), xt[xs][:], r=[("xt", xs)], w=res_keys(b * T2, T2))
        P.barrier()


PARAM_SHAPES = {
    "mix_norm": [2, D], "mlp_norm": [2, D], "mlp_w1": [2, D, DFF], "mlp_w2": [2, DFF, D],
    "lru_w_in": [1, D, 2 * D], "lru_conv_w": [1, 4, D], "lru_conv_b": [1, D],
    "lru_w_r": [1, 16, 64, 64], "lru_b_r": [1, 16, 64], "lru_w_i": [1, 16, 64, 64], "lru_b_i": [1, 16, 64],
    "lru_lambda": [1, D], "lru_w_out": [1, D, D],
    "fox_w_in": [1, D, 3 * D + NH], "fox_b_f": [1, NH], "fox_q_gain": [1, HD], "fox_k_gain": [1, HD],
    "fox_w_out": [1, D, D],
}


def build(phases=("lru", "mlp0", "fox", "mlp1")):
    nc = bass.Bass("TRN2", target_bir_lowering=False)
    x = nc.dram_tensor("x", [S, D], F32, kind="ExternalInput").ap()
    prm = {k: nc.dram_tensor(k, shp, F32, kind="ExternalInput").ap() for k, shp in PARAM_SHAPES.items()}
    out = nc.dram_tensor("out", [S, D], F32, kind="ExternalOutput").ap()
    Qs = nc.dram_tensor("Qs", [NH, 70, S], BF16, kind="Internal").ap()
    Ks = nc.dram_tensor("Ks", [NH, 70, S], BF16, kind="Internal").ap()
    Vs = nc.dram_tensor("Vs", [8, 128, S // 128, 192], BF16, kind="Internal").ap()
    Rs = nc.dram_tensor("Rs", [NH * (S // 512), 512], F32, kind="Internal").ap()
    w1s = [nc.dram_tensor("w1s%d" % l, [D, DFF], BF16, kind="Internal").ap() for l in range(2)]
    w2s = [nc.dram_tensor("w2s%d" % l, [DFF, D], BF16, kind="Internal").ap() for l in range(2)]
    fwi = nc.dram_tensor("fwi", [D, 3 * D + NH], BF16, kind="Internal").ap()
    fwo = nc.dram_tensor("fwo", [D, D], BF16, kind="Internal").ap()
    P = Prog(nc)

    def caster(ph):
        if ph in ("mlp0", "mlp1"):
            l = int(ph[-1])

            return (cast_rows(P, w1s[l], prm["mlp_w1"][l], "w1s%d" % l)
                    + cast_rows(P, w2s[l], prm["mlp_w2"][l], "w2s%d" % l))
        if ph == "fox":
            return (cast_rows(P, fwi, prm["fox_w_in"][0], "fwi", maxb=4096)
                    + cast_rows(P, fwo, prm["fox_w_out"][0], "fwo"))
        return None

    src = x
    for i, ph in enumerate(phases):
        pre = caster(phases[i + 1]) if i + 1 < len(phases) else None
        cast = (i == 0)
        if ph == "lru":
            lru_phase(nc, P, src, out, prm["mix_norm"][0, :], prm["lru_w_in"][0], prm["lru_conv_w"][0],
                      prm["lru_conv_b"][0, :], prm["lru_w_r"][0], prm["lru_b_r"][0].rearrange("n e -> (n e)"),
                      prm["lru_w_i"][0], prm["lru_b_i"][0].rearrange("n e -> (n e)"),
                      prm["lru_lambda"][0, :], prm["lru_w_out"][0], pre=pre)
        elif ph in ("mlp0", "mlp1"):
            l = int(ph[-1])
            mlp_phase(nc, P, src, out, prm["mlp_w1"][l] if cast else w1s[l], prm["mlp_w2"][l] if cast else w2s[l],
                      prm["mlp_norm"][l, :], cast=cast, pre=pre)
        elif ph == "fox":
            fox_phase(nc, P, src, out, prm["mix_norm"][1, :], prm["fox_w_in"][0] if cast else fwi,
                      prm["fox_b_f"][0, :], prm["fox_q_gain"][0, :], prm["fox_k_gain"][0, :],
                      prm["fox_w_out"][0] if cast else fwo, Qs, Ks, Vs, Rs, cast=cast, pre=pre)
        src = out
    P.emit()
    return nc


def kernel(**inputs):
    n = 8
    nc = build()
    x = np.ascontiguousarray(np.asarray(inputs["x"], dtype=np.float32))
    params = {k: np.ascontiguousarray(np.asarray(inputs[k], dtype=np.float32)) for k in PARAM_SHAPES}
    in_maps = []
    for c in range(n):
        m = {"x": x[c]}
        m.update(params)
        in_maps.append(m)
    res = run_bass_kernel_spmd(nc, in_maps, core_ids=list(range(n)))
    return np.stack([np.asarray(r["out"], dtype=np.float32) for r in res.results], axis=0)
```

```python
import contextlib
import numpy as np
import concourse.bass as bass
import concourse.mybir as mybir
from concourse.bass_utils import run_bass_kernel_spmd

F32 = mybir.dt.float32
BF16 = mybir.dt.bfloat16
AF = mybir.ActivationFunctionType
ALU = mybir.AluOpType

S = 4096
D = 1024
DFF = 4096
NH = 16
HD = 64
EPS = 1e-6
NEG = -30000.0


class Prog:
    NDMA_SEMS = 8

    def __init__(self, nc):
        self.nc = nc
        self.names = ("pe", "act", "dve", "pool", "sp")
        self.sems = {e: nc.alloc_semaphore(name="s_" + e) for e in self.names}
        self.dma_sems = {e: [nc.alloc_semaphore(name="d_%s%d" % (e, k)) for k in range(self.NDMA_SEMS)]
                         for e in ("sp", "act", "pool")}
        self.cnt = {e: 0 for e in self.names}
        self.dcnt = {e: 0 for e in self.dma_sems}
        self.seen = {e: {} for e in self.names}
        self._reset()

    def _reset(self):
        self.ops = []
        self.last_w = {}
        self.readers = {}

    def add(self, eng, fn, r=(), w=(), dma=False):
        idx = len(self.ops)
        deps = {}
        for res in r:
            lw = self.last_w.get(res)
            if lw is not None:
                deps[lw] = "raw"
        for res in w:
            lw = self.last_w.get(res)
            if lw is not None and lw not in deps:
                deps[lw] = "waw"
            for rd in self.readers.get(res, {}).values():
                if rd not in deps:
                    deps[rd] = "war"
        for res in w:
            self.last_w[res] = idx
            self.readers[res] = {}
        for res in r:
            d = self.readers.setdefault(res, {})
            d[(eng, idx) if dma else eng] = idx
        deps.pop(idx, None)
        self.ops.append((eng, fn, deps, dma))
        return idx

    def i(self, eng, method, *args, r=(), w=(), **kw):
        return self.add(eng, lambda e: getattr(e, method)(*args, **kw), r=r, w=w)

    def barrier(self):
        last = {}
        dmas = []
        for idx in range(getattr(self, "_bar_from", 0), len(self.ops)):
            eng, fn, deps, dma = self.ops[idx]
            if dma:
                dmas.append(idx)
            elif fn is not None:
                last[eng] = idx
        self._bar_from = len(self.ops)
        for e in self.names:
            deps = {i: "raw" for x, i in last.items() if x != e}
            for i in dmas:
                deps[i] = "raw"
            self.ops.append((e, None, deps, False))
        self.last_w = {}
        self.readers = {}

    def dma(self, eng, out, in_, r=(), w=(), **kw):
        return self.add(eng, lambda e: e.dma_start(out=out, in_=in_, **kw), r=r, w=w, dma=True)

    def emit(self):
        nc = self.nc
        ops = self.ops
        n = len(ops)
        need = [None] * n
        signal = [False] * n
        for i, (eng, fn, deps, dma) in enumerate(ops):
            keep = []
            for d, kind in deps.items():
                deng, _, _, ddma = ops[d]
                if not ddma and deng == eng and not dma and eng == "pe":
                    continue
                keep.append(d)
                signal[d] = True
            need[i] = keep
        token = [None] * n
        plan = {e: [] for e in self.names}
        seen = self.seen

        def wait(lst, eng, tok):
            sem, val = tok
            s = seen[eng]
            if s.get(sem.num, 0) >= val:
                return
            lst.append((sem, val))
            s[sem.num] = val

        for i, (eng, fn, deps, dma) in enumerate(ops):
            waits = []
            best = {}
            for d in need[i]:
                sem, val = token[d]
                if best.get(sem.num, (None, 0))[1] < val:
                    best[sem.num] = (sem, val)
            for k_ in sorted(best):
                wait(waits, eng, best[k_])
            if dma:
                k = self.dcnt[eng]
                self.dcnt[eng] += 1
                sem = self.dma_sems[eng][k % self.NDMA_SEMS]
                val = 16 * (k // self.NDMA_SEMS + 1)
                if val > 16:
                    wait(waits, eng, (sem, val - 16))
                token[i] = (sem, val)
                plan[eng].append((waits, fn, (sem, 16)))
            else:
                inc = None
                if signal[i] and fn is not None:
                    self.cnt[eng] += 1
                    inc = (self.sems[eng], 1)
                    token[i] = (self.sems[eng], self.cnt[eng])
                plan[eng].append((waits, fn, inc))
        waits = []
        for e, lst in self.dma_sems.items():
            k = self.dcnt[e]
            for j in range(max(0, k - self.NDMA_SEMS), k):
                sem = lst[j % self.NDMA_SEMS]
                wait(waits, "sp", (sem, 16 * (j // self.NDMA_SEMS + 1)))
        plan["sp"].append((waits, None, None))

        def run(engobj, items):
            for waits, fn, inc in items:
                for sem, val in waits:
                    engobj.wait_ge(sem, val)
                if fn is not None:
                    ins = fn(engobj)
                    if inc is not None:
                        ins.then_inc(inc[0], inc[1])

        with nc.Block() as block:
            @block.sync
            def _(e):
                run(e, plan["sp"])

            @block.scalar
            def _(e):
                run(e, plan["act"])

            @block.vector
            def _(e):
                run(e, plan["dve"])

            @block.gpsimd
            def _(e):
                run(e, plan["pool"])

            @block.tensor
            def _(e):
                run(e, plan["pe"])
        self._reset()


class Ctx:
    uid = 0

    def __init__(self, nc, P, es):
        self.nc, self.P, self.es = nc, P, es
        self.n = 0
        Ctx.uid += 1
        self.pfx = "c%d_" % Ctx.uid

    def sb(self, shape, dt, name=None):
        self.n += 1
        return self.es.enter_context(self.nc.sbuf_tensor(self.pfx + (name or "t%d" % self.n), list(shape), dt))

    def ps(self, shape, dt=F32, name=None):
        self.n += 1
        return self.es.enter_context(self.nc.psum_tensor(self.pfx + (name or "p%d" % self.n), list(shape), dt))


def res_keys(t0, nt):
    return [("res", t0 // 128 + i) for i in range(nt // 128)]


def rows(ap, t0, nt):
    return ap[t0:t0 + nt, :].rearrange("(s p) d -> p s d", p=128)


def load_col(cx, dst, vec, eng="sp"):
    cx.P.dma(eng, dst, vec.rearrange("(c p) -> p c", p=128), w=[("col", id(dst))],
             allow_slow_non_contiguous=True)


def make_ident(cx, key="ident"):
    P = cx.P
    ident = cx.sb([128, 128], BF16, "ident")
    P.i("pool", "memset", ident[:], 1.0, w=[key])
    P.i("pool", "affine_select", out=ident[:], in_=ident[:], pattern=[[-1, 128]],
                                            compare_op=ALU.is_ge, fill=0.0, base=0,
                                            channel_multiplier=1, r=[key], w=[key])
    P.i("pool", "affine_select", out=ident[:], in_=ident[:], pattern=[[1, 128]],
                                            compare_op=ALU.is_ge, fill=0.0, base=0,
                                            channel_multiplier=-1, r=[key], w=[key])
    return ident


class NormT:
    def __init__(self, cx, NS, gvec, nslots_x=3, nslots_h=2, pst=None, evac="alt"):
        self.cx, self.NS = cx, NS
        self.evac = evac
        P = cx.P
        self.ident = make_ident(cx)
        self.gcol = cx.sb([128, 8], F32, "gcol")
        P.dma("sp", self.gcol[:], gvec.rearrange("(c p) -> p c", p=128), w=["gcol"],
              allow_slow_non_contiguous=True)
        self.mhalf = cx.sb([128, 8], F32, "mhalf")
        P.i("pool", "memset", self.mhalf[:], -0.5, w=["mhalf"])
        self.xt = [cx.sb([128, NS, D], F32, "xt%d" % i) for i in range(nslots_x)]
        self.hb = cx.sb([128, NS, D], BF16, "hb")
        self.hT = [cx.sb([128, 8, NS * 128], BF16, "hT%d" % i) for i in range(nslots_h)]
        self.junk = cx.sb([128, D], BF16, "junk")
        self.ss = [cx.sb([128, NS], F32, "ss%d" % i) for i in range(2)]
        self.rstd = [cx.sb([128, NS], F32, "rstd%d" % i) for i in range(2)]
        self.pst = pst if pst is not None else [cx.ps([128, 1024], BF16, "pst%d" % i) for i in range(2)]
        self.nx, self.nh = nslots_x, nslots_h
        self.ev = 0

    def load(self, b, src):
        T = self.NS * 128
        xs = b % self.nx
        self.cx.P.dma("sp", self.xt[xs][:], rows(src, b * T, T), r=res_keys(b * T, T), w=[("xt", xs)])

    def stats(self, b):
        NS, P = self.NS, self.cx.P
        xs, sl = b % self.nx, b % 2
        xt, ss, rstd, hb = self.xt[xs], self.ss[sl], self.rstd[sl], self.hb
        for s in range(NS):
            P.i("act", "activation", out=hb[:, s, :], in_=xt[:, s, :], func=AF.Square,
                scale=1.0 / 32.0, accum_out=ss[:, s:s + 1], r=[("xt", xs)], w=[("hb", s), ("ss", sl, s)])
        sk = [("ss", sl, s) for s in range(NS)]
        P.i("dve", "tensor_scalar", out=ss[:], in0=ss[:], scalar1=EPS, scalar2=None, op0=ALU.add, r=sk, w=sk)
        P.i("pool", "tensor_tensor", out=rstd[:], in0=ss[:], in1=self.mhalf[:, 0:NS], op=ALU.pow,
            r=sk + ["mhalf"], w=[("rstd", sl)])
        for s in range(NS):
            P.i("dve", "tensor_scalar", out=hb[:, s, :], in0=xt[:, s, :], scalar1=rstd[:, s:s + 1],
                scalar2=None, op0=ALU.mult, r=[("xt", xs), ("rstd", sl)], w=[("hb", s)])
        return xs, b % self.nh

    def tpose(self, b, kc):
        NS, P = self.NS, self.cx.P
        T = NS * 128
        hs = b % self.nh
        hT, hb = self.hT[hs], self.hb
        pst = self.pst[kc % 2]
        pi = self.pst.index(pst)
        for s in range(NS):
            P.i("pe", "transpose", out=pst[:, s * 128:(s + 1) * 128], in_=hb[:, s, kc * 128:(kc + 1) * 128],
                identity=self.ident[:], r=[("hb", s), "ident"], w=[("pst", pi)])
        self.ev += 1
        if self.ev % 2 == 0 and self.evac == "alt":
            P.i("act", "mul", out=hT[:, kc, :], in_=pst[:, 0:T], mul=self.gcol[:, kc:kc + 1],
                r=[("pst", pi), "gcol"], w=[("pst", pi), ("hT", hs, kc)])
        else:
            P.i("dve", "tensor_scalar", out=hT[:, kc, :], in0=pst[:, 0:T], scalar1=self.gcol[:, kc:kc + 1],
                scalar2=None, op0=ALU.mult, r=[("pst", pi), "gcol"], w=[("pst", pi), ("hT", hs, kc)])

    def run(self, b, src, load=True):
        if load:
            self.load(b, src)
        r = self.stats(b)
        for kc in range(8):
            self.tpose(b, kc)
        return r

    def hT_keys(self, hs):
        return [("hT", hs, kc) for kc in range(8)]


def wload(P, dst, srcw, key, cast, maxb=8192):
    if cast:
        P.dma("pool", dst, srcw, w=[key], max_dma_last_dim=maxb)
    else:
        P.dma("sp", dst, srcw, w=[key])


def cast_rows(P, dst, srcw, name, maxb=8192):
    def piece(r0):
        return lambda: P.dma("pool", dst[r0:r0 + 128, :], srcw[r0:r0 + 128, :], w=[(name, r0)],
                             max_dma_last_dim=maxb)
    return [piece(r0) for r0 in range(0, srcw.shape[0], 128)]


class Trickle:
    def __init__(self, pieces, steps):
        self.pieces = list(pieces or [])
        self.per = -(-len(self.pieces) // max(steps, 1)) if self.pieces else 0

    def step(self):
        for _ in range(self.per):
            if self.pieces:
                self.pieces.pop(0)()

    def flush(self):
        while self.pieces:
            self.pieces.pop(0)()


def mlp_phase(nc, P, src, dst, w1, w2, gvec, cast=True, pre=None):
    NS = 2
    T = NS * 128
    NB = S // T
    with contextlib.ExitStack() as es:
        cx = Ctx(nc, P, es)
        w1b = cx.sb([128, 8, 8, 512], BF16, "w1b")
        w2b = cx.sb([128, 32, D], BF16, "w2b")
        for g in range(8):
            if cast:
                wload(P, w1b[:, g, :, :], w1[:, g * 512:(g + 1) * 512].rearrange("(kc p) f -> p kc f", p=128),
                      ("w1b", g), True)
                wload(P, w2b[:, 4 * g:4 * g + 4, :],
                      w2[g * 512:(g + 1) * 512, :].rearrange("(f p) o -> p f o", p=128), ("w2b", g), True)
            else:
                wload(P, w1b[:, g, :, :], w1[g], ("w1b", g), False)
                wload(P, w2b[:, 4 * g:4 * g + 4, :], w2[g], ("w2b", g), False)
        tk = Trickle(pre, NB - 2)
        nt = NormT(cx, NS, gvec, nslots_x=3, nslots_h=2)
        ps1 = [cx.ps([128, 512], F32, "ps1_%d" % i) for i in range(2)]
        ps2 = [[cx.ps([128, 512], F32, "ps2_%d%d" % (s, o)) for o in range(2)] for s in range(NS)]
        rt = [cx.sb([128, T], BF16, "rt%d" % i) for i in range(2)]
        aT = [cx.sb([128, T], BF16, "aT%d" % i) for i in range(3)]
        w1keys = [("w1b", kc) for kc in range(8)]

        def stage1(fc, hT, hs):
            pi = fc % 2
            for kc in range(8):
                P.i("pe", "matmul", ps1[pi][:, 0:T], lhsT=w1b[:, fc // 4, kc, (fc % 4) * 128:(fc % 4 + 1) * 128],
                                                      rhs=hT[:, kc, :], start=(kc == 0), stop=(kc == 7),
                      r=[("w1b", fc // 4), ("hT", hs, kc)], w=[("ps1", pi)])
            P.i("act", "activation", out=rt[pi][:], in_=ps1[pi][:, 0:T], func=AF.Relu,
                  r=[("ps1", pi)], w=[("ps1", pi), ("rt", pi)])
            ai = fc % 3
            P.i("dve", "tensor_tensor", out=aT[ai][:], in0=rt[pi][:], in1=rt[pi][:], op=ALU.mult,
                  r=[("rt", pi)], w=[("aT", ai)])

        def stage2(fc):
            ai = fc % 3
            for s in range(NS):
                for o in range(2):
                    P.i("pe", "matmul",
                        ps2[s][o][:, :], lhsT=aT[ai][:, s * 128:(s + 1) * 128],
                        rhs=w2b[:, fc, o * 512:(o + 1) * 512], start=(fc == 0), stop=(fc == 31),
                        r=[("aT", ai), ("w2b", fc // 4)], w=[("ps2", s, o)])

        slots = {}
        slots[0] = nt.run(0, src)
        for b in range(NB):
            xs, hs = slots[b]
            hT, xt = nt.hT[hs], nt.xt[xs]
            for fc in range(32):
                stage1(fc, hT, hs)
                if fc >= 1:
                    stage2(fc - 1)
                if fc == 2 and b + 1 < NB:
                    nt.load(b + 1, src)
                if fc == 10 and b + 1 < NB:
                    slots[b + 1] = nt.stats(b + 1)
                if 18 <= fc < 26 and b + 1 < NB:
                    nt.tpose(b + 1, fc - 18)
                if fc == 24 and b >= 1:
                    tk.step()
                    if b == NB - 1:
                        tk.flush()
            stage2(31)
            for s in range(NS):
                for o in range(2):
                    P.i("dve", "tensor_tensor",
                        out=xt[:, s, o * 512:(o + 1) * 512], in0=ps2[s][o][:, :],
                        in1=xt[:, s, o * 512:(o + 1) * 512], op=ALU.add,
                        r=[("ps2", s, o), ("xt", xs)], w=[("ps2", s, o), ("xt", xs)])
            P.dma("sp", rows(dst, b * T, T), xt[:], r=[("xt", xs)], w=res_keys(b * T, T))
        P.barrier()


def lru_phase(nc, P, src, dst, gvec, w_in, conv_w, conv_b, w_r, b_r, w_i, b_i, lam, w_out, pre=None):
    NS = 2
    T = NS * 128
    NB = S // T
    with contextlib.ExitStack() as es:
        cx = Ctx(nc, P, es)
        winb = cx.sb([128, 8, 2 * D], BF16, "winb")
        woutb = cx.sb([128, 8, D], BF16, "woutb")
        wrb = cx.sb([128, 8, 128], BF16, "wrb")
        wib = cx.sb([128, 8, 128], BF16, "wib")
        for c2 in range(4):
            for half in range(2):
                cs = slice(half * D + c2 * 256, half * D + (c2 + 1) * 256)
                P.dma("pool", winb[:, :, cs], w_in[:, cs].rearrange("(kc p) f -> p kc f", p=128),
                      w=[("winb", half, c2)], max_dma_last_dim=8192)
        for g in range(2):
            P.dma("pool", woutb[:, 4 * g:4 * g + 4, :],
                  w_out[g * 512:(g + 1) * 512, :].rearrange("(f p) o -> p f o", p=128), w=[("woutb", g)],
                  max_dma_last_dim=8192)
        for wt, wsrc, key in ((wrb, w_r, "wrb"), (wib, w_i, "wib")):
            P.i("dve", "memset", wt[:], 0.0, w=[key])
            v = wsrc.rearrange("(c two) d e -> two d c e", two=2)
            P.dma("pool", wt[0:64, :, 0:64], v[0], r=[], w=[key])
            P.dma("pool", wt[64:128, :, 64:128], v[1], r=[], w=[key])
        nt = NormT(cx, NS, gvec, nslots_x=3, nslots_h=2, evac="dve")
        nt.load(0, src)
        if NB > 1:
            nt.load(1, src)
        cw = cx.sb([128, 4, 8], F32, "cw")
        for k in range(4):
            P.dma("sp", cw[:, k, :], conv_w[k, :].rearrange("(c p) -> p c", p=128), w=["cw"],
                  allow_slow_non_contiguous=True)
        cb = cx.sb([128, 8], F32, "cb")
        brh = cx.sb([128, 8], F32, "brh")
        bih = cx.sb([128, 8], F32, "bih")
        clh = cx.sb([128, 8], F32, "clh")
        for t, vsrc, key in ((cb, conv_b, "cb"), (brh, b_r, "brh"), (bih, b_i, "bih"), (clh, lam, "clh")):
            P.dma("sp", t[:], vsrc.rearrange("(c p) -> p c", p=128), w=[key], allow_slow_non_contiguous=True)
        P.i("dve", "tensor_scalar", out=brh[:], in0=brh[:], scalar1=0.5, scalar2=None, op0=ALU.mult,
              r=["brh"], w=["brh"])
        P.i("dve", "tensor_scalar", out=bih[:], in0=bih[:], scalar1=0.5, scalar2=None, op0=ALU.mult,
              r=["bih"], w=["bih"])
        P.i("act", "activation", out=clh[:], in_=clh[:], func=AF.Exp, scale=-1.0, r=["clh"], w=["clh"])
        P.i("act", "activation", out=clh[:], in_=clh[:], func=AF.Ln, bias=1.0, r=["clh"], w=["clh"])
        P.i("dve", "tensor_scalar", out=clh[:], in0=clh[:], scalar1=-4.0, scalar2=None, op0=ALU.mult,
              r=["clh"], w=["clh"])
        hstate = cx.sb([128, 8], F32, "hstate")
        tails = cx.sb([128, 8, 4], F32, "tails")
        P.i("dve", "memset", hstate[:], 0.0, w=[("hstate", c) for c in range(8)])
        P.i("dve", "memset", tails[:], 0.0, w=[("tails", c) for c in range(8)])

        psg = [cx.ps([128, 512], F32, "psg%d" % i) for i in range(2)]
        psx = [cx.ps([128, 512], F32, "psx%d" % i) for i in range(2)]
        psr = cx.ps([128, 512], F32, "psr")
        psi = cx.ps([128, 512], F32, "psi")
        pso = psr
        gl = [[cx.sb([128, T], BF16, "gl%d_%d" % (p, c)) for c in range(8)] for p in range(2)]
        xcA = [cx.sb([128, 8, T], F32, "xcA%d" % p) for p in range(2)]
        xc = [[xcA[p][:, c, :] for c in range(8)] for p in range(2)]
        tr = [[cx.sb([128, T], F32, "tr%d_%d" % (p, c)) for c in range(8)] for p in range(2)]
        ti = [[cx.sb([128, T], F32, "ti%d_%d" % (p, c)) for c in range(8)] for p in range(2)]
        aa = [cx.sb([128, T], F32, "aa%d" % c) for c in range(8)]
        xb = [cx.sb([128, T + 4], F32, "xb%d" % i) for i in range(2)]
        xcb = [cx.sb([128, T], BF16, "xcb%d" % i) for i in range(2)]
        yT = [cx.sb([128, 8, T], BF16, "yT%d" % i) for i in range(2)]
        slots = {}

        def XA(b, c):
            p, j = b % 2, c % 2
            xs, hs = slots[b]
            hT = nt.hT[hs]
            for kc in range(8):
                P.i("pe", "matmul", psg[j][:, 0:T], lhsT=winb[:, kc, c * 128:(c + 1) * 128], rhs=hT[:, kc, :],
                    start=(kc == 0), stop=(kc == 7), r=[("winb", 0, c // 2), ("hT", hs, kc)], w=[("psg", j)])
            for kc in range(8):
                P.i("pe", "matmul", psx[j][:, 0:T], lhsT=winb[:, kc, D + c * 128:D + (c + 1) * 128],
                    rhs=hT[:, kc, :], start=(kc == 0), stop=(kc == 7),
                    r=[("winb", 1, c // 2), ("hT", hs, kc)], w=[("psx", j)])
            P.i("act", "activation", out=gl[p][c][:], in_=psg[j][:, 0:T], func=AF.Gelu_apprx_tanh,
                r=[("psg", j)], w=[("psg", j), ("gl", p, c)])
            P.i("act", "copy", out=xb[j][:, 4:T + 4], in_=psx[j][:, 0:T], r=[("psx", j)], w=[("psx", j), ("xbm", j)])
            P.i("pool", "tensor_copy", out=xb[j][:, 0:4], in_=tails[:, c, :], r=[("tails", c)], w=[("xbt", j)])
            P.i("act", "activation", out=xc[p][c][:], in_=psx[j][:, 0:T], func=AF.Identity, scale=cw[:, 3, c:c + 1],
                bias=cb[:, c:c + 1], r=[("psx", j), "cw", "cb"], w=[("psx", j), ("xc", p, c)])
            for k in range(3):
                P.i("dve", "scalar_tensor_tensor", out=xc[p][c][:], in0=xb[j][:, 1 + k:1 + k + T],
                    scalar=cw[:, k, c:c + 1], in1=xc[p][c][:], op0=ALU.mult, op1=ALU.add,
                    r=[("xbm", j), ("xbt", j), "cw", ("xc", p, c)], w=[("xc", p, c)])
            P.i("pool", "tensor_copy", out=tails[:, c, :], in_=xb[j][:, T:T + 4], r=[("xbm", j)], w=[("tails", c)])

        def XB1(b, c):
            p, j = b % 2, c % 2
            P.i("act", "copy", out=xcb[j][:], in_=xc[p][c][:], r=[("xc", p, c)], w=[("xcb", j)])

        def XB2(b, c):
            p, j = b % 2, c % 2
            P.i("pe", "matmul", psr[:, 0:T], lhsT=wrb[:, c, :], rhs=xcb[j][:], start=True, stop=True,
                r=["wrb", ("xcb", j)], w=["psr"])
            P.i("pe", "matmul", psi[:, 0:T], lhsT=wib[:, c, :], rhs=xcb[j][:], start=True, stop=True,
                r=["wib", ("xcb", j)], w=["psi"])
            P.i("act", "activation", out=tr[p][c][:], in_=psr[:, 0:T], func=AF.Tanh, scale=0.5, bias=brh[:, c:c + 1],
                r=["psr", "brh"], w=["psr", ("tr", p, c)])
            P.i("act", "activation", out=ti[p][c][:], in_=psi[:, 0:T], func=AF.Tanh, scale=0.5, bias=bih[:, c:c + 1],
                r=["psi", "bih"], w=["psi", ("ti", p, c)])

        def X_and_Dv(bx, bd):
            if bd is not None and bd >= 1:
                pp = (bd - 1) % 2
                P.i("dve", "tensor_copy", out=hstate[:, :], in_=xcA[pp][:, :, T - 1],
                    r=[("xc", pp, c) for c in range(8)], w=[("hstate", c) for c in range(8)])
            for c in range(10):
                if bd is not None and c < 8:
                    Dv1(bd, c)
                if bx is not None:
                    if c < 8:
                        XA(bx, c)
                    if 1 <= c <= 8:
                        XB1(bx, c - 1)
                    if c >= 2:
                        XB2(bx, c - 2)
                if bd is not None and 1 <= c <= 8:
                    Dv2(bd, c - 1)

        def C(b):
            p = b % 2
            for c in range(8):
                P.i("act", "activation", out=aa[c][:], in_=tr[p][c][:], func=AF.Exp, scale=clh[:, c:c + 1],
                    bias=clh[:, c:c + 1], r=[("tr", p, c), "clh"], w=[("aa", c)])
                P.i("pool", "tensor_scalar", out=tr[p][c][:], in0=aa[c][:], scalar1=1.0, scalar2=0.0,
                    op0=ALU.min, op1=ALU.max, r=[("aa", c)], w=[("tr", p, c)])
                P.i("pool", "tensor_tensor", out=tr[p][c][:], in0=tr[p][c][:], in1=tr[p][c][:], op=ALU.mult,
                    r=[("tr", p, c)], w=[("tr", p, c)])
                P.i("dve", "scalar_tensor_tensor", out=ti[p][c][:], in0=ti[p][c][:], scalar=1.0, in1=xc[p][c][:],
                    op0=ALU.add, op1=ALU.mult, r=[("ti", p, c), ("xc", p, c)], w=[("ti", p, c)])

        def Dact(b):
            p = b % 2
            for c in range(8):
                P.i("act", "activation", out=tr[p][c][:], in_=tr[p][c][:], func=AF.Sqrt, scale=-0.25, bias=0.25,
                    r=[("tr", p, c)], w=[("tr", p, c)])

        def Dv1(b, c):
            p = b % 2
            P.i("pool", "tensor_tensor", out=ti[p][c][:], in0=ti[p][c][:], in1=tr[p][c][:], op=ALU.mult,
                r=[("ti", p, c), ("tr", p, c)], w=[("ti", p, c)])
            init, ik = hstate[:, c:c + 1], ("hstate", c)
            P.i("dve", "tensor_tensor_scan", out=xc[p][c][:], data0=aa[c][:], data1=ti[p][c][:],
                initial=init, op0=ALU.mult, op1=ALU.add,
                r=[("aa", c), ("ti", p, c), ik], w=[("xc", p, c)])

        def Dv2(b, c):
            p = b % 2
            P.i("pool", "tensor_tensor", out=yT[p][:, c, :], in0=xc[p][c][:], in1=gl[p][c][:], op=ALU.mult,
                r=[("xc", p, c), ("gl", p, c)], w=[("yT", p, c)])

        def W(b):
            p = b % 2
            xs, hs = slots[b]
            xt = nt.xt[xs]
            for s in range(NS):
                for o in range(2):
                    for c in range(8):
                        P.i("pe", "matmul", pso[:, :], lhsT=yT[p][:, c, s * 128:(s + 1) * 128],
                            rhs=woutb[:, c, o * 512:(o + 1) * 512], start=(c == 0), stop=(c == 7),
                            r=[("yT", p, c), ("woutb", c // 4)], w=["psr"])
                    P.i("dve", "tensor_tensor", out=xt[:, s, o * 512:(o + 1) * 512], in0=pso[:, :],
                        in1=xt[:, s, o * 512:(o + 1) * 512], op=ALU.add,
                        r=["psr", ("xt", xs)], w=["psr", ("xt", xs)])
            P.dma("sp", rows(dst, b * T, T), xt[:], r=[("xt", xs)], w=res_keys(b * T, T))

        tk = Trickle(pre, NB - 2)
        slots[0] = nt.run(0, src, load=False)
        X_and_Dv(0, None)
        if NB > 1:
            slots[1] = nt.run(1, src, load=False)
        for b in range(NB):
            if b + 2 < NB:
                nt.load(b + 2, src)
            if b >= 1:
                tk.step()
                if b == NB - 1:
                    tk.flush()
            C(b)
            Dact(b)
            X_and_Dv(b + 1 if b + 1 < NB else None, b)
            if b + 2 < NB:
                slots[b + 2] = nt.stats(b + 2)
            W(b)
            if b + 2 < NB:
                for kc in range(8):
                    nt.tpose(b + 2, kc)
        P.barrier()


def fox_phase(nc, P, src, dst, gvec, w_in, b_f, q_gain, k_gain, w_out, Qs, Ks, Vs, Rs, cast=True, pre=None):
    NS = 4
    T = NS * 128
    NB = S // T
    NW = 3 * D + NH
    if True:
      with contextlib.ExitStack() as es:
          cx = Ctx(nc, P, es)
          wfb = cx.sb([128, 8, NW], BF16, "wfb")
          for kc in range(8):
              wload(P, wfb[:, kc, :], w_in[kc * 128:(kc + 1) * 128, :], ("wfb", kc), cast, maxb=4096)
          bd = cx.sb([128, 128], BF16, "bd")
          P.i("dve", "memset", bd[:], 0.0, w=["bd"])
          P.i("dve", "memset", bd[0:64, 0:64], 1.0 / 64.0, w=["bd"])
          P.i("dve", "memset", bd[64:128, 64:128], 1.0 / 64.0, w=["bd"])
          gq = cx.sb([128, 1], F32, "gq")
          gk = cx.sb([128, 1], F32, "gk")
          for t, g, key in ((gq, q_gain, "gq"), (gk, k_gain, "gk")):
              for hh in range(2):
                  P.dma("sp", t[hh * 64:(hh + 1) * 64, :], g.rearrange("(d one) -> d one", one=1), w=[key])
          P.i("dve", "tensor_scalar", out=gq[:], in0=gq[:], scalar1=0.125, scalar2=None, op0=ALU.mult,
                r=["gq"], w=["gq"])
          nbf = cx.sb([NH, 1], F32, "nbf")
          P.dma("sp", nbf[:], b_f.rearrange("(h one) -> h one", one=1), w=["nbf"])
          P.i("dve", "tensor_scalar", out=nbf[:], in0=nbf[:], scalar1=-1.0, scalar2=None, op0=ALU.mult,
                r=["nbf"], w=["nbf"])
          epsb = cx.sb([128, 1], F32, "epsb")
          P.i("pool", "memset", epsb[:], EPS, w=["epsb"])
          ones16 = cx.sb([NH, T], F32, "ones16")
          P.i("pool", "memset", ones16[:], 1.0, w=["ones16"])
          onesb = cx.sb([NH, S], BF16, "onesb")
          P.i("pool", "memset", onesb[:], 1.0, w=["onesb"])
          monesb = cx.sb([NH, S], BF16, "monesb")
          P.i("pool", "memset", monesb[:], -1.0, w=["monesb"])
          for r in range(3):
              P.dma("sp", Qs[:, 67 + r, :], monesb[:], r=["monesb"], w=[("Qaug", 3 + r)])
              P.dma("sp", Ks[:, 64 + r, :], onesb[:], r=["onesb"], w=[("Kaug", r)])
          cstate = cx.sb([NH, 1], F32, "cstate")
          P.i("dve", "memset", cstate[:], 0.0, w=["cstate"])

          nt = NormT(cx, NS, gvec, nslots_x=3, nslots_h=2,
                     pst=[cx.ps([128, 1024], BF16, "pst0")] * 2)
          psq = [cx.ps([128, 512], F32, "psq%d" % i) for i in range(3)]
          psm = [cx.ps([128, 512], F32, "psm%d" % i) for i in range(2)]
          psv = [cx.ps([128, 512], F32, "psv%d" % i) for i in range(2)]
          psf = psv[0]
          sqb = [cx.sb([128, T], BF16, "sqb%d" % i) for i in range(2)]
          mse = [cx.sb([128, T], F32, "mse%d" % i) for i in range(2)]
          rs = [cx.sb([128, T], F32, "rs%d" % i) for i in range(3)]
          qn = [cx.sb([128, T], BF16, "qn%d" % i) for i in range(3)]
          vb = [cx.sb([128, NS, 8, 192], BF16, "vb%d" % i) for i in range(2)]
          for i in range(2):
              P.i("pool", "memset", vb[i][:, :, :, 64:128], 0.0, w=[("vb", i, s_) for s_ in range(NS)])
              P.i("pool", "memset", vb[i][:, :, :, 64:65], 1.0, w=[("vb", i, s_) for s_ in range(NS)])
          e1 = cx.sb([NH, T], F32, "e1")
          cblk = cx.sb([NH, T], F32, "cblk")
          r1 = cx.sb([NH, T], F32, "r1")
          cparts = [[cx.sb([NH, T], BF16, "cp%d_%d" % (i, k)) for k in range(3)] for i in range(2)]

          slots = {}
          slots[0] = nt.run(0, src)
          if NB > 1:
              nt.load(1, src)
          nq = 0
          for b in range(NB):
              xs, hs = slots[b]
              hT = nt.hT[hs]
              cols = slice(b * T, (b + 1) * T)
              if b + 2 < NB:
                  nt.load(b + 2, src)
              def qk_mm(jj):
                  i3 = jj % 3
                  for kc in range(8):
                      P.i("pe", "matmul", psq[i3][:, :], lhsT=wfb[:, kc, jj * 128:(jj + 1) * 128], rhs=hT[:, kc, :],
                          start=(kc == 0), stop=(kc == 7), r=[("wfb", kc), ("hT", hs, kc)], w=[("psq", i3)])
                  P.i("act", "activation", out=sqb[jj % 2][:], in_=psq[i3][:, :], func=AF.Square,
                      r=[("psq", i3)], w=[("psq", i3), ("sqb", jj % 2)])

              def qk_ms(jj):
                  i2 = jj % 2
                  P.i("pe", "matmul", psm[i2][:, :], lhsT=bd[:], rhs=sqb[i2][:], start=True, stop=True,
                      r=["bd", ("sqb", i2)], w=[("psm", i2)])
                  P.i("act", "activation", out=mse[i2][:], in_=psm[i2][:, :], func=AF.Ln, bias=epsb[:, 0:1],
                      r=[("psm", i2), "epsb"], w=[("psm", i2), ("mse", i2)])
                  P.i("act", "activation", out=rs[jj % 3][:], in_=mse[i2][:], func=AF.Exp, scale=-0.5,
                      r=[("mse", i2)], w=[("rs", jj % 3)])

              def qk_out(jj):
                  i3 = jj % 3
                  g = gq if jj < 8 else gk
                  P.i("dve", "scalar_tensor_tensor", out=qn[i3][:], in0=psq[i3][:, :], scalar=g[:, 0:1],
                      in1=rs[i3][:], op0=ALU.mult, op1=ALU.mult,
                      r=[("psq", i3), ("rs", i3), "gq", "gk"], w=[("psq", i3), ("qn", i3)])
                  dstT = Qs if jj < 8 else Ks
                  pr = jj % 8
                  for hh in range(2):
                      P.dma("pool", dstT[2 * pr + hh, 0:64, cols], qn[i3][hh * 64:(hh + 1) * 64, :],
                            r=[("qn", i3)], w=[("QK", jj, b, hh)])

              vs = b % 2

              def v_group(s, o):
                  pv = psv[(s * 2 + o) % 2]
                  pk = ("psv", (s * 2 + o) % 2)
                  for kc in range(8):
                      P.i("pe", "matmul", pv[:, :], lhsT=hT[:, kc, s * 128:(s + 1) * 128],
                          rhs=wfb[:, kc, 2 * D + o * 512:2 * D + (o + 1) * 512],
                          start=(kc == 0), stop=(kc == 7), r=[("wfb", kc), ("hT", hs, kc)], w=[pk])
                  P.i("act", "copy",
                      out=vb[vs][:, s, 4 * o:4 * o + 4, :].rearrange("p a (t e) -> p a t e", e=64)[:, :, 0:3:2, :],
                      in_=pv[:, :].rearrange("p (a t e) -> p a t e", a=4, t=2),
                      r=[pk], w=[pk, ("vb", vs, s)])
                  if o == 1:
                      P.dma("pool", Vs[:, :, b * NS + s, :].rearrange("pr p e -> p pr e"), vb[vs][:, s, :, :],
                            r=[("vb", vs, s)], w=[("V", b, s)])

              for jj in range(18):
                  if jj == 3 and b + 1 < NB:
                      slots[b + 1] = nt.stats(b + 1)
                  if jj < 16:
                      qk_mm(jj)
                  if 8 <= jj < 16 and b + 1 < NB:
                      nt.tpose(b + 1, jj - 8)
                  if jj % 2 == 1 and jj < 16:
                      g = jj // 2
                      v_group(g // 2, g % 2)
                  if 1 <= jj <= 16:
                      qk_ms(jj - 1)
                  if jj >= 2:
                      qk_out(jj - 2)
              for kc in range(8):
                  P.i("pe", "matmul", psf[0:NH, :], lhsT=wfb[:, kc, 3 * D:3 * D + NH],
                                                        rhs=hT[:, kc, :], start=(kc == 0), stop=(kc == 7),
                        r=[("wfb", kc), ("hT", hs, kc)], w=[("psv", 0)])
              P.i("act", "activation", out=e1[:], in_=psf[0:NH, :], func=AF.Exp, scale=-1.0,
                                                  bias=nbf[:, 0:1], r=[("psv", 0), "nbf"], w=[("psv", 0), "e1"])
              P.i("act", "activation", out=e1[:], in_=e1[:], func=AF.Ln, bias=1.0, r=["e1"], w=["e1"])
              P.i("dve", "tensor_tensor_scan", out=cblk[:], data0=ones16[:], data1=e1[:],
                                                          initial=cstate[:, 0:1], op0=ALU.mult, op1=ALU.subtract,
                    r=["ones16", "e1", "cstate"], w=["cblk"])
              P.i("pool", "tensor_copy", out=cstate[:], in_=cblk[:, T - 1:T], r=["cblk"], w=["cstate"])
              ci = b % 2
              cp = cparts[ci]
              P.i("dve", "tensor_copy", out=cp[0][:], in_=cblk[:], r=["cblk"], w=[("cp", ci, 0)])
              P.i("dve", "tensor_tensor", out=r1[:], in0=cblk[:], in1=cp[0][:], op=ALU.subtract,
                    r=["cblk", ("cp", ci, 0)], w=["r1"])
              P.i("dve", "tensor_copy", out=cp[1][:], in_=r1[:], r=["r1"], w=[("cp", ci, 1)])
              P.i("dve", "tensor_tensor", out=r1[:], in0=r1[:], in1=cp[1][:], op=ALU.subtract,
                    r=["r1", ("cp", ci, 1)], w=["r1"])
              P.i("dve", "tensor_copy", out=cp[2][:], in_=r1[:], r=["r1"], w=[("cp", ci, 2)])
              for k in range(3):
                  P.dma("pool", Qs[:, 64 + k, cols], cp[k][:], r=[("cp", ci, k)], w=[("Qaug", k, b)])
                  P.dma("pool", Ks[:, 67 + k, cols], cp[k][:], r=[("cp", ci, k)], w=[("Kaug", 3 + k, b)])
          P.barrier()

    with contextlib.ExitStack() as es:
        cx = Ctx(nc, P, es)
        oT = cx.sb([128, 8, S], BF16, "oT")
        wob = cx.sb([128, 8, D], BF16, "wob")
        for g in range(2):
            wload(P, wob[:, 4 * g:4 * g + 4, :],
                  w_out[g * 512:(g + 1) * 512, :].rearrange("(f p) o -> p f o", p=128), ("wob", g), cast)
        maskT = cx.sb([128, 128], F32, "maskT")
        P.i("pool", "memset", maskT[:], 0.0, w=["maskT"])
        P.i("pool", "affine_select", out=maskT[:], in_=maskT[:], pattern=[[1, 128]],
                                                compare_op=ALU.is_ge, fill=NEG, base=0,
                                                channel_multiplier=-1, r=["maskT"], w=["maskT"])
        QT = [cx.sb([70, S], BF16, "QT%d" % i) for i in range(2)]
        KT = [cx.sb([70, S], BF16, "KT%d" % i) for i in range(2)]
        Vp = [cx.sb([128, S // 128, 192], BF16, "Vp%d" % i) for i in range(2)]
        NBUF = 6
        LAG = 4
        pss = [cx.ps([128, 512], F32, "pss%d" % i) for i in range(NBUF)]
        pso = [cx.ps([128, 512], F32, "pso%d" % i) for i in range(2)]
        pT = [cx.sb([128, T], BF16, "pT%d" % i) for i in range(NBUF)]
        rr = [cx.sb([128, T], F32, "rr%d" % i) for i in range(2)]
        rbc = [cx.sb([128, T], F32, "rbc%d" % i) for i in range(2)]

        def head_loads(h):
            pair, odd, hsl, vsl = h // 2, h % 2, h % 2, (h // 2) % 2
            P.dma("sp", QT[hsl][:], Qs[h, :, :], r=[("Qall",)], w=[("QT", hsl)])
            P.dma("sp", KT[hsl][:], Ks[h, :, :], r=[("Kall",)], w=[("KT", hsl)])
            if not odd:
                P.dma("sp", Vp[vsl][:], Vs[pair], r=[("Vall",)], w=[("Vp", vsl)])

        def qk(ti, h, qb, kt):
            hsl = h % 2
            si = ti % NBUF
            jd = kt - 4 * qb
            c0 = max(jd, 0) * 128
            P.i("pe", "matmul", pss[si][:, c0:T], lhsT=KT[hsl][:, kt * 128:(kt + 1) * 128],
                rhs=QT[hsl][:, qb * T + c0:(qb + 1) * T], start=True, stop=True,
                r=[("KT", hsl), ("QT", hsl)], w=[("pss", si)])
            P.i("act", "activation", out=pT[si][:, c0:T], in_=pss[si][:, c0:T], func=AF.Exp,
                r=[("pss", si)], w=[("pss", si), ("pT", si)])
            if jd >= 0:
                P.i("pool", "affine_select", out=pT[si][:, c0:c0 + 128], in_=pT[si][:, c0:c0 + 128],
                    pattern=[[1, 128]], compare_op=ALU.is_ge, fill=0.0, base=0, channel_multiplier=-1,
                    r=[("pT", si)], w=[("pT", si)])
            return si, c0

        def pv(h, qb, kt, gi, si, c0):
            pair, odd, vsl = h // 2, h % 2, (h // 2) % 2
            nkt = 4 * (qb + 1)
            oi = gi % 2
            if odd:
                lhsT = Vp[vsl][:, kt, 64:192]
                out = pso[oi][:, c0:T]
            else:
                lhsT = Vp[vsl][:, kt, 0:128]
                out = pso[oi][:, c0:T]
            P.i("pe", "matmul", out, lhsT=lhsT, rhs=pT[si][:, c0:T], start=(kt == 0), stop=(kt == nkt - 1),
                r=[("pT", si), ("Vp", vsl)], w=[("pso", oi)])
            if kt == nkt - 1:
                ri = oi
                if odd:
                    drow, orows = slice(0, 1), slice(64, 128)
                else:
                    drow, orows = slice(64, 65), slice(0, 64)
                P.i("dve", "reciprocal", out=rr[ri][drow, :], in_=pso[oi][drow, :],
                    r=[("pso", oi)], w=[("pso", oi), ("rr", ri)])
                ridx = h * NB + qb
                P.dma("sp", Rs[ridx:ridx + 1, :], rr[ri][drow, :], r=[("rr", ri)], w=[("Rs", ridx)])
                P.dma("sp", rbc[ri][orows, :], Rs[ridx:ridx + 1, :].partition_broadcast(64),
                      r=[("Rs", ridx)], w=[("rbc", ri)])
                P.i("dve", "tensor_tensor", out=oT[orows, pair, qb * T:(qb + 1) * T], in0=pso[oi][orows, :],
                    in1=rbc[ri][orows, :], op=ALU.mult,
                    r=[("pso", oi), ("rbc", ri)], w=[("pso", oi), ("oT", pair, odd, qb)])

        tk = Trickle(pre, NH - 2)
        head_loads(0)
        pending = []
        ti = 0
        gi = 0
        for h in range(NH):
            if h + 1 < NH:
                head_loads(h + 1)
            if h >= 1:
                tk.step()
                if h == NH - 1:
                    tk.flush()
            for qb in range(NB):
                for kt in range(4 * (qb + 1)):
                    si, c0 = qk(ti, h, qb, kt)
                    pending.append((h, qb, kt, gi, si, c0))
                    ti += 1
                    if len(pending) > LAG:
                        pv(*pending.pop(0))
                gi += 1
        while pending:
            pv(*pending.pop(0))
        psw = [pss[0], pss[1]]
        NS2 = 2
        T2 = NS2 * 128
        xt = [cx.sb([128, NS2, D], F32, "xt%d" % i) for i in range(3)]
        NB2 = S // T2

        def c_load(b):
            P.dma("sp", xt[b % 3][:], rows(src, b * T2, T2), r=res_keys(b * T2, T2), w=[("xt", b % 3)])

        c_load(0)
        for b in range(NB2):
            xs = b % 3
            qb = (b * T2) // T
            if b + 1 < NB2:
                c_load(b + 1)
            for s in range(NS2):
                for o in range(2):
                    wi = (s * 2 + o) % 2
                    for pr in range(8):
                        P.i("pe", "matmul",
                            psw[wi][:, :], lhsT=oT[:, pr, b * T2 + s * 128:b * T2 + (s + 1) * 128],
                            rhs=wob[:, pr, o * 512:(o + 1) * 512], start=(pr == 0), stop=(pr == 7),
                            r=[("oT", pr, 0, qb), ("oT", pr, 1, qb), ("wob", pr // 4)], w=[("pss", wi)])
                    P.i("dve", "tensor_tensor",
                        out=xt[xs][:, s, o * 512:(o + 1) * 512], in0=psw[wi][:, :],
                        in1=xt[xs][:, s, o * 512:(o + 1) * 512], op=ALU.add,
                        r=[("pss", wi), ("xt", xs)], w=[("pss", wi), ("xt", xs)])
            P.dma("sp", rows(dst, b * T2, T2), xt[xs][:], r=[("xt", xs)], w=res_keys(b * T2, T2))
        P.barrier()


PARAM_SHAPES = {
    "mix_norm": [2, D], "mlp_norm": [2, D], "mlp_w1": [2, D, DFF], "mlp_w2": [2, DFF, D],
    "lru_w_in": [1, D, 2 * D], "lru_conv_w": [1, 4, D], "lru_conv_b": [1, D],
    "lru_w_r": [1, 16, 64, 64], "lru_b_r": [1, 16, 64], "lru_w_i": [1, 16, 64, 64], "lru_b_i": [1, 16, 64],
    "lru_lambda": [1, D], "lru_w_out": [1, D, D],
    "fox_w_in": [1, D, 3 * D + NH], "fox_b_f": [1, NH], "fox_q_gain": [1, HD], "fox_k_gain": [1, HD],
    "fox_w_out": [1, D, D],
}


def build(phases=("lru", "mlp0", "fox", "mlp1")):
    nc = bass.Bass("TRN2", target_bir_lowering=False)
    x = nc.dram_tensor("x", [S, D], F32, kind="ExternalInput").ap()
    prm = {k: nc.dram_tensor(k, shp, F32, kind="ExternalInput").ap() for k, shp in PARAM_SHAPES.items()}
    out = nc.dram_tensor("out", [S, D], F32, kind="ExternalOutput").ap()
    Qs = nc.dram_tensor("Qs", [NH, 70, S], BF16, kind="Internal").ap()
    Ks = nc.dram_tensor("Ks", [NH, 70, S], BF16, kind="Internal").ap()
    Vs = nc.dram_tensor("Vs", [8, 128, S // 128, 192], BF16, kind="Internal").ap()
    Rs = nc.dram_tensor("Rs", [NH * (S // 512), 512], F32, kind="Internal").ap()
    w1s = [nc.dram_tensor("w1s%d" % l, [8, 128, 8, 512], BF16, kind="Internal").ap() for l in range(2)]
    w2s = [nc.dram_tensor("w2s%d" % l, [8, 128, 4, D], BF16, kind="Internal").ap() for l in range(2)]
    fwi = nc.dram_tensor("fwi", [D, 3 * D + NH], BF16, kind="Internal").ap()
    fwo = nc.dram_tensor("fwo", [D, D], BF16, kind="Internal").ap()
    P = Prog(nc)

    def caster(ph):
        if ph in ("mlp0", "mlp1"):
            l = int(ph[-1])

            w1f, w2f = prm["mlp_w1"][l], prm["mlp_w2"][l]

            def p1(kc):
                return lambda: P.dma("pool", w1s[l][:, :, kc, :],
                                     w1f[kc * 128:(kc + 1) * 128, :].rearrange("p (g f) -> g p f", g=8),
                                     w=[("w1s", l, kc)], max_dma_last_dim=8192)

            def p2(fc):
                return lambda: P.dma("pool", w2s[l][fc // 4, :, fc % 4, :], w2f[fc * 128:(fc + 1) * 128, :],
                                     w=[("w2s", l, fc)], max_dma_last_dim=8192)
            pieces = []
            for i in range(8):
                pieces.append(p1(i))
                pieces += [p2(4 * i + j) for j in range(4)]
            return pieces
        if ph == "fox":
            return (cast_rows(P, fwi, prm["fox_w_in"][0], "fwi", maxb=4096)
                    + cast_rows(P, fwo, prm["fox_w_out"][0], "fwo"))
        return None

    src = x
    for i, ph in enumerate(phases):
        pre = caster(phases[i + 1]) if i + 1 < len(phases) else None
        cast = (i == 0)
        if ph == "lru":
            lru_phase(nc, P, src, out, prm["mix_norm"][0, :], prm["lru_w_in"][0], prm["lru_conv_w"][0],
                      prm["lru_conv_b"][0, :], prm["lru_w_r"][0], prm["lru_b_r"][0].rearrange("n e -> (n e)"),
                      prm["lru_w_i"][0], prm["lru_b_i"][0].rearrange("n e -> (n e)"),
                      prm["lru_lambda"][0, :], prm["lru_w_out"][0], pre=pre)
        elif ph in ("mlp0", "mlp1"):
            l = int(ph[-1])
            mlp_phase(nc, P, src, out, prm["mlp_w1"][l] if cast else w1s[l], prm["mlp_w2"][l] if cast else w2s[l],
                      prm["mlp_norm"][l, :], cast=cast, pre=pre)
        elif ph == "fox":
            fox_phase(nc, P, src, out, prm["mix_norm"][1, :], prm["fox_w_in"][0] if cast else fwi,
                      prm["fox_b_f"][0, :], prm["fox_q_gain"][0, :], prm["fox_k_gain"][0, :],
                      prm["fox_w_out"][0] if cast else fwo, Qs, Ks, Vs, Rs, cast=cast, pre=pre)
        src = out
    P.emit()
    return nc


def kernel(**inputs):
    n = 8
    nc = build()
    x = np.ascontiguousarray(np.asarray(inputs["x"], dtype=np.float32))
    params = {k: np.ascontiguousarray(np.asarray(inputs[k], dtype=np.float32)) for k in PARAM_SHAPES}
    in_maps = []
    for c in range(n):
        m = {"x": x[c]}
        m.update(params)
        in_maps.append(m)
    res = run_bass_kernel_spmd(nc, in_maps, core_ids=list(range(n)))
    return np.stack([np.asarray(r["out"], dtype=np.float32) for r in res.results], axis=0)
```

```python
import contextlib
import numpy as np
import concourse.bass as bass
import concourse.mybir as mybir
from concourse.bass_utils import run_bass_kernel_spmd

F32 = mybir.dt.float32
BF16 = mybir.dt.bfloat16
AF = mybir.ActivationFunctionType
ALU = mybir.AluOpType

S = 4096
D = 1024
DFF = 4096
NH = 16
HD = 64
EPS = 1e-6
NEG = -30000.0


class Prog:
    NDMA_SEMS = 8

    def __init__(self, nc):
        self.nc = nc
        self.names = ("pe", "act", "dve", "pool", "sp")
        self.sems = {e: nc.alloc_semaphore(name="s_" + e) for e in self.names}
        self.dma_sems = {e: [nc.alloc_semaphore(name="d_%s%d" % (e, k)) for k in range(self.NDMA_SEMS)]
                         for e in ("sp", "act", "pool")}
        self.cnt = {e: 0 for e in self.names}
        self.dcnt = {e: 0 for e in self.dma_sems}
        self.seen = {e: {} for e in self.names}
        self._reset()

    def _reset(self):
        self.ops = []
        self.last_w = {}
        self.readers = {}

    def add(self, eng, fn, r=(), w=(), dma=False):
        idx = len(self.ops)
        deps = {}
        for res in r:
            lw = self.last_w.get(res)
            if lw is not None:
                deps[lw] = "raw"
        for res in w:
            lw = self.last_w.get(res)
            if lw is not None and lw not in deps:
                deps[lw] = "waw"
            for rd in self.readers.get(res, {}).values():
                if rd not in deps:
                    deps[rd] = "war"
        for res in w:
            self.last_w[res] = idx
            self.readers[res] = {}
        for res in r:
            d = self.readers.setdefault(res, {})
            d[(eng, idx) if dma else eng] = idx
        deps.pop(idx, None)
        self.ops.append((eng, fn, deps, dma))
        return idx

    def i(self, eng, method, *args, r=(), w=(), **kw):
        return self.add(eng, lambda e: getattr(e, method)(*args, **kw), r=r, w=w)

    def barrier(self):
        last = {}
        dmas = []
        for idx in range(getattr(self, "_bar_from", 0), len(self.ops)):
            eng, fn, deps, dma = self.ops[idx]
            if dma:
                dmas.append(idx)
            elif fn is not None:
                last[eng] = idx
        self._bar_from = len(self.ops)
        for e in self.names:
            deps = {i: "raw" for x, i in last.items() if x != e}
            for i in dmas:
                deps[i] = "raw"
            self.ops.append((e, None, deps, False))
        self.last_w = {}
        self.readers = {}

    def dma(self, eng, out, in_, r=(), w=(), **kw):
        return self.add(eng, lambda e: e.dma_start(out=out, in_=in_, **kw), r=r, w=w, dma=True)

    def emit(self):
        nc = self.nc
        ops = self.ops
        n = len(ops)
        need = [None] * n
        signal = [False] * n
        for i, (eng, fn, deps, dma) in enumerate(ops):
            keep = []
            for d, kind in deps.items():
                deng, _, _, ddma = ops[d]
                if not ddma and deng == eng and not dma and eng == "pe":
                    continue
                keep.append(d)
                signal[d] = True
            need[i] = keep
        token = [None] * n
        plan = {e: [] for e in self.names}
        seen = self.seen

        def wait(lst, eng, tok):
            sem, val = tok
            s = seen[eng]
            if s.get(sem.num, 0) >= val:
                return
            lst.append((sem, val))
            s[sem.num] = val

        for i, (eng, fn, deps, dma) in enumerate(ops):
            waits = []
            best = {}
            for d in need[i]:
                sem, val = token[d]
                if best.get(sem.num, (None, 0))[1] < val:
                    best[sem.num] = (sem, val)
            for k_ in sorted(best):
                wait(waits, eng, best[k_])
            if dma:
                k = self.dcnt[eng]
                self.dcnt[eng] += 1
                sem = self.dma_sems[eng][k % self.NDMA_SEMS]
                val = 16 * (k // self.NDMA_SEMS + 1)
                if val > 16:
                    wait(waits, eng, (sem, val - 16))
                token[i] = (sem, val)
                plan[eng].append((waits, fn, (sem, 16)))
            else:
                inc = None
                if signal[i] and fn is not None:
                    self.cnt[eng] += 1
                    inc = (self.sems[eng], 1)
                    token[i] = (self.sems[eng], self.cnt[eng])
                plan[eng].append((waits, fn, inc))
        waits = []
        for e, lst in self.dma_sems.items():
            k = self.dcnt[e]
            for j in range(max(0, k - self.NDMA_SEMS), k):
                sem = lst[j % self.NDMA_SEMS]
                wait(waits, "sp", (sem, 16 * (j // self.NDMA_SEMS + 1)))
        plan["sp"].append((waits, None, None))

        def run(engobj, items):
            for waits, fn, inc in items:
                for sem, val in waits:
                    engobj.wait_ge(sem, val)
                if fn is not None:
                    ins = fn(engobj)
                    if inc is not None:
                        ins.then_inc(inc[0], inc[1])

        with nc.Block() as block:
            @block.sync
            def _(e):
                run(e, plan["sp"])

            @block.scalar
            def _(e):
                run(e, plan["act"])

            @block.vector
            def _(e):
                run(e, plan["dve"])

            @block.gpsimd
            def _(e):
                run(e, plan["pool"])

            @block.tensor
            def _(e):
                run(e, plan["pe"])
        self._reset()


class Ctx:
    uid = 0

    def __init__(self, nc, P, es):
        self.nc, self.P, self.es = nc, P, es
        self.n = 0
        Ctx.uid += 1
        self.pfx = "c%d_" % Ctx.uid

    def sb(self, shape, dt, name=None):
        self.n += 1
        return self.es.enter_context(self.nc.sbuf_tensor(self.pfx + (name or "t%d" % self.n), list(shape), dt))

    def ps(self, shape, dt=F32, name=None):
        self.n += 1
        return self.es.enter_context(self.nc.psum_tensor(self.pfx + (name or "p%d" % self.n), list(shape), dt))


def res_keys(t0, nt):
    return [("res", t0 // 128 + i) for i in range(nt // 128)]


def rows(ap, t0, nt):
    return ap[t0:t0 + nt, :].rearrange("(s p) d -> p s d", p=128)


def load_col(cx, dst, vec, eng="sp"):
    cx.P.dma(eng, dst, vec.rearrange("(c p) -> p c", p=128), w=[("col", id(dst))],
             allow_slow_non_contiguous=True)


def make_ident(cx, key="ident"):
    P = cx.P
    ident = cx.sb([128, 128], BF16, "ident")
    P.i("pool", "memset", ident[:], 1.0, w=[key])
    P.i("pool", "affine_select", out=ident[:], in_=ident[:], pattern=[[-1, 128]],
                                            compare_op=ALU.is_ge, fill=0.0, base=0,
                                            channel_multiplier=1, r=[key], w=[key])
    P.i("pool", "affine_select", out=ident[:], in_=ident[:], pattern=[[1, 128]],
                                            compare_op=ALU.is_ge, fill=0.0, base=0,
                                            channel_multiplier=-1, r=[key], w=[key])
    return ident


class NormT:
    def __init__(self, cx, NS, gvec, nslots_x=3, nslots_h=2, pst=None, evac="alt"):
        self.cx, self.NS = cx, NS
        self.evac = evac
        P = cx.P
        self.ident = make_ident(cx)
        self.gcol = cx.sb([128, 8], F32, "gcol")
        P.dma("sp", self.gcol[:], gvec.rearrange("(c p) -> p c", p=128), w=["gcol"],
              allow_slow_non_contiguous=True)
        self.mhalf = cx.sb([128, 8], F32, "mhalf")
        P.i("pool", "memset", self.mhalf[:], -0.5, w=["mhalf"])
        self.xt = [cx.sb([128, NS, D], F32, "xt%d" % i) for i in range(nslots_x)]
        self.hb = cx.sb([128, NS, D], BF16, "hb")
        self.hT = [cx.sb([128, 8, NS * 128], BF16, "hT%d" % i) for i in range(nslots_h)]
        self.junk = cx.sb([128, D], BF16, "junk")
        self.ss = [cx.sb([128, NS], F32, "ss%d" % i) for i in range(2)]
        self.rstd = [cx.sb([128, NS], F32, "rstd%d" % i) for i in range(2)]
        self.pst = pst if pst is not None else [cx.ps([128, 1024], BF16, "pst%d" % i) for i in range(2)]
        self.nx, self.nh = nslots_x, nslots_h
        self.ev = 0

    def load(self, b, src):
        T = self.NS * 128
        xs = b % self.nx
        self.cx.P.dma("sp", self.xt[xs][:], rows(src, b * T, T), r=res_keys(b * T, T), w=[("xt", xs)])

    def stats(self, b):
        NS, P = self.NS, self.cx.P
        xs, sl = b % self.nx, b % 2
        xt, ss, rstd, hb = self.xt[xs], self.ss[sl], self.rstd[sl], self.hb
        for s in range(NS):
            P.i("act", "activation", out=hb[:, s, :], in_=xt[:, s, :], func=AF.Square,
                scale=1.0 / 32.0, accum_out=ss[:, s:s + 1], r=[("xt", xs)], w=[("hb", s), ("ss", sl, s)])
        sk = [("ss", sl, s) for s in range(NS)]
        P.i("dve", "tensor_scalar", out=ss[:], in0=ss[:], scalar1=EPS, scalar2=None, op0=ALU.add, r=sk, w=sk)
        P.i("pool", "tensor_tensor", out=rstd[:], in0=ss[:], in1=self.mhalf[:, 0:NS], op=ALU.pow,
            r=sk + ["mhalf"], w=[("rstd", sl)])
        for s in range(NS):
            P.i("dve", "tensor_scalar", out=hb[:, s, :], in0=xt[:, s, :], scalar1=rstd[:, s:s + 1],
                scalar2=None, op0=ALU.mult, r=[("xt", xs), ("rstd", sl)], w=[("hb", s)])
        return xs, b % self.nh

    def tpose(self, b, kc):
        NS, P = self.NS, self.cx.P
        T = NS * 128
        hs = b % self.nh
        hT, hb = self.hT[hs], self.hb
        pst = self.pst[kc % 2]
        pi = self.pst.index(pst)
        for s in range(NS):
            P.i("pe", "transpose", out=pst[:, s * 128:(s + 1) * 128], in_=hb[:, s, kc * 128:(kc + 1) * 128],
                identity=self.ident[:], r=[("hb", s), "ident"], w=[("pst", pi)])
        self.ev += 1
        if self.ev % 2 == 0 and self.evac == "alt":
            P.i("act", "mul", out=hT[:, kc, :], in_=pst[:, 0:T], mul=self.gcol[:, kc:kc + 1],
                r=[("pst", pi), "gcol"], w=[("pst", pi), ("hT", hs, kc)])
        else:
            P.i("dve", "tensor_scalar", out=hT[:, kc, :], in0=pst[:, 0:T], scalar1=self.gcol[:, kc:kc + 1],
                scalar2=None, op0=ALU.mult, r=[("pst", pi), "gcol"], w=[("pst", pi), ("hT", hs, kc)])

    def run(self, b, src, load=True):
        if load:
            self.load(b, src)
        r = self.stats(b)
        for kc in range(8):
            self.tpose(b, kc)
        return r

    def hT_keys(self, hs):
        return [("hT", hs, kc) for kc in range(8)]


def wload(P, dst, srcw, key, cast, maxb=8192):
    if cast:
        P.dma("pool", dst, srcw, w=[key], max_dma_last_dim=maxb)
    else:
        P.dma("sp", dst, srcw, w=[key])


def cast_rows(P, dst, srcw, name, maxb=8192):
    def piece(r0):
        return lambda: P.dma("pool", dst[r0:r0 + 128, :], srcw[r0:r0 + 128, :], w=[(name, r0)],
                             max_dma_last_dim=maxb)
    return [piece(r0) for r0 in range(0, srcw.shape[0], 128)]


class Trickle:
    def __init__(self, pieces, steps):
        self.pieces = list(pieces or [])
        self.per = -(-len(self.pieces) // max(steps, 1)) if self.pieces else 0

    def step(self):
        for _ in range(self.per):
            if self.pieces:
                self.pieces.pop(0)()

    def flush(self):
        while self.pieces:
            self.pieces.pop(0)()


def mlp_phase(nc, P, src, dst, w1, w2, gvec, cast=True, pre=None):
    NS = 2
    T = NS * 128
    NB = S // T
    with contextlib.ExitStack() as es:
        cx = Ctx(nc, P, es)
        w1b = cx.sb([128, 8, 8, 512], BF16, "w1b")
        w2b = cx.sb([128, 32, D], BF16, "w2b")
        nt = NormT(cx, NS, gvec, nslots_x=3, nslots_h=2)
        nt.load(0, src)
        for g in range(8):
            if cast:
                wload(P, w1b[:, g, :, :], w1[:, g * 512:(g + 1) * 512].rearrange("(kc p) f -> p kc f", p=128),
                      ("w1b", g), True)
                wload(P, w2b[:, 4 * g:4 * g + 4, :],
                      w2[g * 512:(g + 1) * 512, :].rearrange("(f p) o -> p f o", p=128), ("w2b", g), True)
            else:
                wload(P, w1b[:, g, :, :], w1[g], ("w1b", g), False)
                wload(P, w2b[:, 4 * g:4 * g + 4, :], w2[g], ("w2b", g), False)
        tk = Trickle(pre, NB - 2)
        ps1 = [cx.ps([128, 512], F32, "ps1_%d" % i) for i in range(2)]
        ps2 = [[cx.ps([128, 512], F32, "ps2_%d%d" % (s, o)) for o in range(2)] for s in range(NS)]
        rt = [cx.sb([128, T], BF16, "rt%d" % i) for i in range(2)]
        aT = [cx.sb([128, T], BF16, "aT%d" % i) for i in range(3)]
        w1keys = [("w1b", kc) for kc in range(8)]

        def stage1(fc, hT, hs):
            pi = fc % 2
            for kc in range(8):
                P.i("pe", "matmul", ps1[pi][:, 0:T], lhsT=w1b[:, fc // 4, kc, (fc % 4) * 128:(fc % 4 + 1) * 128],
                                                      rhs=hT[:, kc, :], start=(kc == 0), stop=(kc == 7),
                      r=[("w1b", fc // 4), ("hT", hs, kc)], w=[("ps1", pi)])
            P.i("act", "activation", out=rt[pi][:], in_=ps1[pi][:, 0:T], func=AF.Relu,
                  r=[("ps1", pi)], w=[("ps1", pi), ("rt", pi)])
            ai = fc % 3
            P.i("dve", "tensor_tensor", out=aT[ai][:], in0=rt[pi][:], in1=rt[pi][:], op=ALU.mult,
                  r=[("rt", pi)], w=[("aT", ai)])

        def stage2(fc):
            ai = fc % 3
            for s in range(NS):
                for o in range(2):
                    P.i("pe", "matmul",
                        ps2[s][o][:, :], lhsT=aT[ai][:, s * 128:(s + 1) * 128],
                        rhs=w2b[:, fc, o * 512:(o + 1) * 512], start=(fc == 0), stop=(fc == 31),
                        r=[("aT", ai), ("w2b", fc // 4)], w=[("ps2", s, o)])

        slots = {}
        slots[0] = nt.run(0, src, load=False)
        for b in range(NB):
            xs, hs = slots[b]
            hT, xt = nt.hT[hs], nt.xt[xs]
            for fc in range(32):
                stage1(fc, hT, hs)
                if fc >= 1:
                    stage2(fc - 1)
                if fc == 2 and b + 1 < NB:
                    nt.load(b + 1, src)
                if fc == 10 and b + 1 < NB:
                    slots[b + 1] = nt.stats(b + 1)
                if 18 <= fc < 26 and b + 1 < NB:
                    nt.tpose(b + 1, fc - 18)
                if fc == 24 and b >= 1:
                    tk.step()
                    if b == NB - 1:
                        tk.flush()
            stage2(31)
            for s in range(NS):
                for o in range(2):
                    P.i("dve", "tensor_tensor",
                        out=xt[:, s, o * 512:(o + 1) * 512], in0=ps2[s][o][:, :],
                        in1=xt[:, s, o * 512:(o + 1) * 512], op=ALU.add,
                        r=[("ps2", s, o), ("xt", xs)], w=[("ps2", s, o), ("xt", xs)])
            P.dma("sp", rows(dst, b * T, T), xt[:], r=[("xt", xs)], w=res_keys(b * T, T))
        P.barrier()


def lru_phase(nc, P, src, dst, gvec, w_in, conv_w, conv_b, w_r, b_r, w_i, b_i, lam, w_out, pre=None):
    NS = 2
    T = NS * 128
    NB = S // T
    with contextlib.ExitStack() as es:
        cx = Ctx(nc, P, es)
        winb = cx.sb([128, 8, 2 * D], BF16, "winb")
        woutb = cx.sb([128, 8, D], BF16, "woutb")
        wrb = cx.sb([128, 8, 128], BF16, "wrb")
        wib = cx.sb([128, 8, 128], BF16, "wib")
        for c2 in range(4):
            for half in range(2):
                cs = slice(half * D + c2 * 256, half * D + (c2 + 1) * 256)
                P.dma("pool", winb[:, :, cs], w_in[:, cs].rearrange("(kc p) f -> p kc f", p=128),
                      w=[("winb", half, c2)], max_dma_last_dim=8192)
        for g in range(2):
            P.dma("pool", woutb[:, 4 * g:4 * g + 4, :],
                  w_out[g * 512:(g + 1) * 512, :].rearrange("(f p) o -> p f o", p=128), w=[("woutb", g)],
                  max_dma_last_dim=8192)
        for wt, wsrc, key in ((wrb, w_r, "wrb"), (wib, w_i, "wib")):
            P.i("dve", "memset", wt[:], 0.0, w=[key])
            v = wsrc.rearrange("(c two) d e -> two d c e", two=2)
            P.dma("pool", wt[0:64, :, 0:64], v[0], r=[], w=[key])
            P.dma("pool", wt[64:128, :, 64:128], v[1], r=[], w=[key])
        nt = NormT(cx, NS, gvec, nslots_x=3, nslots_h=2, evac="dve")
        nt.load(0, src)
        if NB > 1:
            nt.load(1, src)
        cw = cx.sb([128, 4, 8], F32, "cw")
        for k in range(4):
            P.dma("sp", cw[:, k, :], conv_w[k, :].rearrange("(c p) -> p c", p=128), w=["cw"],
                  allow_slow_non_contiguous=True)
        cb = cx.sb([128, 8], F32, "cb")
        brh = cx.sb([128, 8], F32, "brh")
        bih = cx.sb([128, 8], F32, "bih")
        clh = cx.sb([128, 8], F32, "clh")
        for t, vsrc, key in ((cb, conv_b, "cb"), (brh, b_r, "brh"), (bih, b_i, "bih"), (clh, lam, "clh")):
            P.dma("sp", t[:], vsrc.rearrange("(c p) -> p c", p=128), w=[key], allow_slow_non_contiguous=True)
        P.i("dve", "tensor_scalar", out=brh[:], in0=brh[:], scalar1=0.5, scalar2=None, op0=ALU.mult,
              r=["brh"], w=["brh"])
        P.i("dve", "tensor_scalar", out=bih[:], in0=bih[:], scalar1=0.5, scalar2=None, op0=ALU.mult,
              r=["bih"], w=["bih"])
        P.i("act", "activation", out=clh[:], in_=clh[:], func=AF.Exp, scale=-1.0, r=["clh"], w=["clh"])
        P.i("act", "activation", out=clh[:], in_=clh[:], func=AF.Ln, bias=1.0, r=["clh"], w=["clh"])
        P.i("dve", "tensor_scalar", out=clh[:], in0=clh[:], scalar1=-4.0, scalar2=None, op0=ALU.mult,
              r=["clh"], w=["clh"])
        hstate = cx.sb([128, 8], F32, "hstate")
        tails = cx.sb([128, 8, 4], F32, "tails")
        P.i("dve", "memset", hstate[:], 0.0, w=[("hstate", c) for c in range(8)])
        P.i("dve", "memset", tails[:], 0.0, w=[("tails", c) for c in range(8)])

        psg = [cx.ps([128, 512], F32, "psg%d" % i) for i in range(2)]
        psx = [cx.ps([128, 512], F32, "psx%d" % i) for i in range(2)]
        psr = cx.ps([128, 512], F32, "psr")
        psi = cx.ps([128, 512], F32, "psi")
        pso = psr
        gl = [[cx.sb([128, T], BF16, "gl%d_%d" % (p, c)) for c in range(8)] for p in range(2)]
        xcA = [cx.sb([128, 8, T], F32, "xcA%d" % p) for p in range(2)]
        xc = [[xcA[p][:, c, :] for c in range(8)] for p in range(2)]
        tr = [[cx.sb([128, T], F32, "tr%d_%d" % (p, c)) for c in range(8)] for p in range(2)]
        ti = [[cx.sb([128, T], F32, "ti%d_%d" % (p, c)) for c in range(8)] for p in range(2)]
        aa = [cx.sb([128, T], F32, "aa%d" % c) for c in range(8)]
        xb = [cx.sb([128, T + 4], F32, "xb%d" % i) for i in range(2)]
        xcb = [cx.sb([128, T], BF16, "xcb%d" % i) for i in range(2)]
        yT = [cx.sb([128, 8, T], BF16, "yT%d" % i) for i in range(2)]
        slots = {}

        def XA(b, c):
            p, j = b % 2, c % 2
            xs, hs = slots[b]
            hT = nt.hT[hs]
            for kc in range(8):
                P.i("pe", "matmul", psg[j][:, 0:T], lhsT=winb[:, kc, c * 128:(c + 1) * 128], rhs=hT[:, kc, :],
                    start=(kc == 0), stop=(kc == 7), r=[("winb", 0, c // 2), ("hT", hs, kc)], w=[("psg", j)])
            for kc in range(8):
                P.i("pe", "matmul", psx[j][:, 0:T], lhsT=winb[:, kc, D + c * 128:D + (c + 1) * 128],
                    rhs=hT[:, kc, :], start=(kc == 0), stop=(kc == 7),
                    r=[("winb", 1, c // 2), ("hT", hs, kc)], w=[("psx", j)])
            P.i("act", "activation", out=gl[p][c][:], in_=psg[j][:, 0:T], func=AF.Gelu_apprx_tanh,
                r=[("psg", j)], w=[("psg", j), ("gl", p, c)])
            P.i("act", "copy", out=xb[j][:, 4:T + 4], in_=psx[j][:, 0:T], r=[("psx", j)], w=[("psx", j), ("xbm", j)])
            P.i("pool", "tensor_copy", out=xb[j][:, 0:4], in_=tails[:, c, :], r=[("tails", c)], w=[("xbt", j)])
            P.i("act", "activation", out=xc[p][c][:], in_=psx[j][:, 0:T], func=AF.Identity, scale=cw[:, 3, c:c + 1],
                bias=cb[:, c:c + 1], r=[("psx", j), "cw", "cb"], w=[("psx", j), ("xc", p, c)])
            for k in range(3):
                P.i("dve", "scalar_tensor_tensor", out=xc[p][c][:], in0=xb[j][:, 1 + k:1 + k + T],
                    scalar=cw[:, k, c:c + 1], in1=xc[p][c][:], op0=ALU.mult, op1=ALU.add,
                    r=[("xbm", j), ("xbt", j), "cw", ("xc", p, c)], w=[("xc", p, c)])
            P.i("pool", "tensor_copy", out=tails[:, c, :], in_=xb[j][:, T:T + 4], r=[("xbm", j)], w=[("tails", c)])

        def XB1(b, c):
            p, j = b % 2, c % 2
            P.i("act", "copy", out=xcb[j][:], in_=xc[p][c][:], r=[("xc", p, c)], w=[("xcb", j)])

        def XB2(b, c):
            p, j = b % 2, c % 2
            P.i("pe", "matmul", psr[:, 0:T], lhsT=wrb[:, c, :], rhs=xcb[j][:], start=True, stop=True,
                r=["wrb", ("xcb", j)], w=["psr"])
            P.i("pe", "matmul", psi[:, 0:T], lhsT=wib[:, c, :], rhs=xcb[j][:], start=True, stop=True,
                r=["wib", ("xcb", j)], w=["psi"])
            P.i("act", "activation", out=tr[p][c][:], in_=psr[:, 0:T], func=AF.Tanh, scale=0.5, bias=brh[:, c:c + 1],
                r=["psr", "brh"], w=["psr", ("tr", p, c)])
            P.i("act", "activation", out=ti[p][c][:], in_=psi[:, 0:T], func=AF.Tanh, scale=0.5, bias=bih[:, c:c + 1],
                r=["psi", "bih"], w=["psi", ("ti", p, c)])

        def X_and_Dv(bx, bd):
            if bd is not None and bd >= 1:
                pp = (bd - 1) % 2
                P.i("dve", "tensor_copy", out=hstate[:, :], in_=xcA[pp][:, :, T - 1],
                    r=[("xc", pp, c) for c in range(8)], w=[("hstate", c) for c in range(8)])
            for c in range(10):
                if bd is not None and c < 8:
                    Dv1(bd, c)
                if bx is not None:
                    if c < 8:
                        XA(bx, c)
                    if 1 <= c <= 8:
                        XB1(bx, c - 1)
                    if c >= 2:
                        XB2(bx, c - 2)
                if bd is not None and 1 <= c <= 8:
                    Dv2(bd, c - 1)

        def C(b):
            p = b % 2
            for c in range(8):
                P.i("act", "activation", out=aa[c][:], in_=tr[p][c][:], func=AF.Exp, scale=clh[:, c:c + 1],
                    bias=clh[:, c:c + 1], r=[("tr", p, c), "clh"], w=[("aa", c)])
                P.i("pool", "tensor_scalar", out=tr[p][c][:], in0=aa[c][:], scalar1=1.0, scalar2=0.0,
                    op0=ALU.min, op1=ALU.max, r=[("aa", c)], w=[("tr", p, c)])
                P.i("pool", "tensor_tensor", out=tr[p][c][:], in0=tr[p][c][:], in1=tr[p][c][:], op=ALU.mult,
                    r=[("tr", p, c)], w=[("tr", p, c)])
                P.i("dve", "scalar_tensor_tensor", out=ti[p][c][:], in0=ti[p][c][:], scalar=1.0, in1=xc[p][c][:],
                    op0=ALU.add, op1=ALU.mult, r=[("ti", p, c), ("xc", p, c)], w=[("ti", p, c)])

        def Dact(b):
            p = b % 2
            for c in range(8):
                P.i("act", "activation", out=tr[p][c][:], in_=tr[p][c][:], func=AF.Sqrt, scale=-0.25, bias=0.25,
                    r=[("tr", p, c)], w=[("tr", p, c)])

        def Dv1(b, c):
            p = b % 2
            P.i("pool", "tensor_tensor", out=ti[p][c][:], in0=ti[p][c][:], in1=tr[p][c][:], op=ALU.mult,
                r=[("ti", p, c), ("tr", p, c)], w=[("ti", p, c)])
            init, ik = hstate[:, c:c + 1], ("hstate", c)
            P.i("dve", "tensor_tensor_scan", out=xc[p][c][:], data0=aa[c][:], data1=ti[p][c][:],
                initial=init, op0=ALU.mult, op1=ALU.add,
                r=[("aa", c), ("ti", p, c), ik], w=[("xc", p, c)])

        def Dv2(b, c):
            p = b % 2
            P.i("pool", "tensor_tensor", out=yT[p][:, c, :], in0=xc[p][c][:], in1=gl[p][c][:], op=ALU.mult,
                r=[("xc", p, c), ("gl", p, c)], w=[("yT", p, c)])

        def W(b):
            p = b % 2
            xs, hs = slots[b]
            xt = nt.xt[xs]
            for s in range(NS):
                for o in range(2):
                    for c in range(8):
                        P.i("pe", "matmul", pso[:, :], lhsT=yT[p][:, c, s * 128:(s + 1) * 128],
                            rhs=woutb[:, c, o * 512:(o + 1) * 512], start=(c == 0), stop=(c == 7),
                            r=[("yT", p, c), ("woutb", c // 4)], w=["psr"])
                    P.i("dve", "tensor_tensor", out=xt[:, s, o * 512:(o + 1) * 512], in0=pso[:, :],
                        in1=xt[:, s, o * 512:(o + 1) * 512], op=ALU.add,
                        r=["psr", ("xt", xs)], w=["psr", ("xt", xs)])
            P.dma("sp", rows(dst, b * T, T), xt[:], r=[("xt", xs)], w=res_keys(b * T, T))

        tk = Trickle(pre, NB - 2)
        slots[0] = nt.run(0, src, load=False)
        X_and_Dv(0, None)
        if NB > 1:
            slots[1] = nt.run(1, src, load=False)
        for b in range(NB):
            if b + 2 < NB:
                nt.load(b + 2, src)
            if b >= 1:
                tk.step()
                if b == NB - 1:
                    tk.flush()
            C(b)
            Dact(b)
            X_and_Dv(b + 1 if b + 1 < NB else None, b)
            if b + 2 < NB:
                slots[b + 2] = nt.stats(b + 2)
            W(b)
            if b + 2 < NB:
                for kc in range(8):
                    nt.tpose(b + 2, kc)
        P.barrier()


def fox_phase(nc, P, src, dst, gvec, w_in, b_f, q_gain, k_gain, w_out, Qs, Ks, Vs, Rs, cast=True, pre=None):
    NS = 4
    T = NS * 128
    NB = S // T
    NW = 3 * D + NH
    if True:
      with contextlib.ExitStack() as es:
          cx = Ctx(nc, P, es)
          wfb = cx.sb([128, 8, NW], BF16, "wfb")
          nt = NormT(cx, NS, gvec, nslots_x=3, nslots_h=2,
                     pst=[cx.ps([128, 1024], BF16, "pst0")] * 2)
          nt.load(0, src)
          for kc in range(8):
              wload(P, wfb[:, kc, :], w_in[kc * 128:(kc + 1) * 128, :], ("wfb", kc), cast, maxb=4096)
          bd = cx.sb([128, 128], BF16, "bd")
          P.i("dve", "memset", bd[:], 0.0, w=["bd"])
          P.i("dve", "memset", bd[0:64, 0:64], 1.0 / 64.0, w=["bd"])
          P.i("dve", "memset", bd[64:128, 64:128], 1.0 / 64.0, w=["bd"])
          gq = cx.sb([128, 1], F32, "gq")
          gk = cx.sb([128, 1], F32, "gk")
          for t, g, key in ((gq, q_gain, "gq"), (gk, k_gain, "gk")):
              for hh in range(2):
                  P.dma("sp", t[hh * 64:(hh + 1) * 64, :], g.rearrange("(d one) -> d one", one=1), w=[key])
          P.i("dve", "tensor_scalar", out=gq[:], in0=gq[:], scalar1=0.125, scalar2=None, op0=ALU.mult,
                r=["gq"], w=["gq"])
          nbf = cx.sb([NH, 1], F32, "nbf")
          P.dma("sp", nbf[:], b_f.rearrange("(h one) -> h one", one=1), w=["nbf"])
          P.i("dve", "tensor_scalar", out=nbf[:], in0=nbf[:], scalar1=-1.0, scalar2=None, op0=ALU.mult,
                r=["nbf"], w=["nbf"])
          epsb = cx.sb([128, 1], F32, "epsb")
          P.i("pool", "memset", epsb[:], EPS, w=["epsb"])
          ones16 = cx.sb([NH, T], F32, "ones16")
          P.i("pool", "memset", ones16[:], 1.0, w=["ones16"])
          onesb = cx.sb([NH, S], BF16, "onesb")
          P.i("pool", "memset", onesb[:], 1.0, w=["onesb"])
          monesb = cx.sb([NH, S], BF16, "monesb")
          P.i("pool", "memset", monesb[:], -1.0, w=["monesb"])
          for r in range(3):
              P.dma("sp", Qs[:, 67 + r, :], monesb[:], r=["monesb"], w=[("Qaug", 3 + r)])
              P.dma("sp", Ks[:, 64 + r, :], onesb[:], r=["onesb"], w=[("Kaug", r)])
          cstate = cx.sb([NH, 1], F32, "cstate")
          P.i("dve", "memset", cstate[:], 0.0, w=["cstate"])

          psq = [cx.ps([128, 512], F32, "psq%d" % i) for i in range(3)]
          psm = [cx.ps([128, 512], F32, "psm%d" % i) for i in range(2)]
          psv = [cx.ps([128, 512], F32, "psv%d" % i) for i in range(2)]
          psf = psv[0]
          sqb = [cx.sb([128, T], BF16, "sqb%d" % i) for i in range(2)]
          mse = [cx.sb([128, T], F32, "mse%d" % i) for i in range(2)]
          rs = [cx.sb([128, T], F32, "rs%d" % i) for i in range(3)]
          qn = [cx.sb([128, T], BF16, "qn%d" % i) for i in range(3)]
          vb = [cx.sb([128, NS, 8, 192], BF16, "vb%d" % i) for i in range(2)]
          for i in range(2):
              P.i("pool", "memset", vb[i][:, :, :, 64:128], 0.0, w=[("vb", i, s_) for s_ in range(NS)])
              P.i("pool", "memset", vb[i][:, :, :, 64:65], 1.0, w=[("vb", i, s_) for s_ in range(NS)])
          e1 = cx.sb([NH, T], F32, "e1")
          cblk = cx.sb([NH, T], F32, "cblk")
          r1 = cx.sb([NH, T], F32, "r1")
          cparts = [[cx.sb([NH, T], BF16, "cp%d_%d" % (i, k)) for k in range(3)] for i in range(2)]

          slots = {}
          slots[0] = nt.run(0, src, load=False)
          if NB > 1:
              nt.load(1, src)
          nq = 0
          for b in range(NB):
              xs, hs = slots[b]
              hT = nt.hT[hs]
              cols = slice(b * T, (b + 1) * T)
              if b + 2 < NB:
                  nt.load(b + 2, src)
              def qk_mm(jj):
                  i3 = jj % 3
                  for kc in range(8):
                      P.i("pe", "matmul", psq[i3][:, :], lhsT=wfb[:, kc, jj * 128:(jj + 1) * 128], rhs=hT[:, kc, :],
                          start=(kc == 0), stop=(kc == 7), r=[("wfb", kc), ("hT", hs, kc)], w=[("psq", i3)])
                  P.i("act", "activation", out=sqb[jj % 2][:], in_=psq[i3][:, :], func=AF.Square,
                      r=[("psq", i3)], w=[("psq", i3), ("sqb", jj % 2)])

              def qk_ms(jj):
                  i2 = jj % 2
                  P.i("pe", "matmul", psm[i2][:, :], lhsT=bd[:], rhs=sqb[i2][:], start=True, stop=True,
                      r=["bd", ("sqb", i2)], w=[("psm", i2)])
                  P.i("act", "activation", out=mse[i2][:], in_=psm[i2][:, :], func=AF.Ln, bias=epsb[:, 0:1],
                      r=[("psm", i2), "epsb"], w=[("psm", i2), ("mse", i2)])
                  P.i("act", "activation", out=rs[jj % 3][:], in_=mse[i2][:], func=AF.Exp, scale=-0.5,
                      r=[("mse", i2)], w=[("rs", jj % 3)])

              def qk_out(jj):
                  i3 = jj % 3
                  g = gq if jj < 8 else gk
                  P.i("dve", "scalar_tensor_tensor", out=qn[i3][:], in0=psq[i3][:, :], scalar=g[:, 0:1],
                      in1=rs[i3][:], op0=ALU.mult, op1=ALU.mult,
                      r=[("psq", i3), ("rs", i3), "gq", "gk"], w=[("psq", i3), ("qn", i3)])
                  dstT = Qs if jj < 8 else Ks
                  pr = jj % 8
                  for hh in range(2):
                      P.dma("pool", dstT[2 * pr + hh, 0:64, cols], qn[i3][hh * 64:(hh + 1) * 64, :],
                            r=[("qn", i3)], w=[("QK", jj, b, hh)])

              vs = b % 2

              def v_group(s, o):
                  pv = psv[(s * 2 + o) % 2]
                  pk = ("psv", (s * 2 + o) % 2)
                  for kc in range(8):
                      P.i("pe", "matmul", pv[:, :], lhsT=hT[:, kc, s * 128:(s + 1) * 128],
                          rhs=wfb[:, kc, 2 * D + o * 512:2 * D + (o + 1) * 512],
                          start=(kc == 0), stop=(kc == 7), r=[("wfb", kc), ("hT", hs, kc)], w=[pk])
                  P.i("act", "copy",
                      out=vb[vs][:, s, 4 * o:4 * o + 4, :].rearrange("p a (t e) -> p a t e", e=64)[:, :, 0:3:2, :],
                      in_=pv[:, :].rearrange("p (a t e) -> p a t e", a=4, t=2),
                      r=[pk], w=[pk, ("vb", vs, s)])
                  if o == 1:
                      P.dma("pool", Vs[:, :, b * NS + s, :].rearrange("pr p e -> p pr e"), vb[vs][:, s, :, :],
                            r=[("vb", vs, s)], w=[("V", b, s)])

              for jj in range(18):
                  if jj == 3 and b + 1 < NB:
                      slots[b + 1] = nt.stats(b + 1)
                  if jj < 16:
                      qk_mm(jj)
                  if 8 <= jj < 16 and b + 1 < NB:
                      nt.tpose(b + 1, jj - 8)
                  if jj % 2 == 1 and jj < 16:
                      g = jj // 2
                      v_group(g // 2, g % 2)
                  if 1 <= jj <= 16:
                      qk_ms(jj - 1)
                  if jj >= 2:
                      qk_out(jj - 2)
              for kc in range(8):
                  P.i("pe", "matmul", psf[0:NH, :], lhsT=wfb[:, kc, 3 * D:3 * D + NH],
                                                        rhs=hT[:, kc, :], start=(kc == 0), stop=(kc == 7),
                        r=[("wfb", kc), ("hT", hs, kc)], w=[("psv", 0)])
              P.i("act", "activation", out=e1[:], in_=psf[0:NH, :], func=AF.Exp, scale=-1.0,
                                                  bias=nbf[:, 0:1], r=[("psv", 0), "nbf"], w=[("psv", 0), "e1"])
              P.i("act", "activation", out=e1[:], in_=e1[:], func=AF.Ln, bias=1.0, r=["e1"], w=["e1"])
              P.i("dve", "tensor_tensor_scan", out=cblk[:], data0=ones16[:], data1=e1[:],
                                                          initial=cstate[:, 0:1], op0=ALU.mult, op1=ALU.subtract,
                    r=["ones16", "e1", "cstate"], w=["cblk"])
              P.i("pool", "tensor_copy", out=cstate[:], in_=cblk[:, T - 1:T], r=["cblk"], w=["cstate"])
              ci = b % 2
              cp = cparts[ci]
              P.i("dve", "tensor_copy", out=cp[0][:], in_=cblk[:], r=["cblk"], w=[("cp", ci, 0)])
              P.i("dve", "tensor_tensor", out=r1[:], in0=cblk[:], in1=cp[0][:], op=ALU.subtract,
                    r=["cblk", ("cp", ci, 0)], w=["r1"])
              P.i("dve", "tensor_copy", out=cp[1][:], in_=r1[:], r=["r1"], w=[("cp", ci, 1)])
              P.i("dve", "tensor_tensor", out=r1[:], in0=r1[:], in1=cp[1][:], op=ALU.subtract,
                    r=["r1", ("cp", ci, 1)], w=["r1"])
              P.i("dve", "tensor_copy", out=cp[2][:], in_=r1[:], r=["r1"], w=[("cp", ci, 2)])
              for k in range(3):
                  P.dma("pool", Qs[:, 64 + k, cols], cp[k][:], r=[("cp", ci, k)], w=[("Qaug", k, b)])
                  P.dma("pool", Ks[:, 67 + k, cols], cp[k][:], r=[("cp", ci, k)], w=[("Kaug", 3 + k, b)])
          P.barrier()

    with contextlib.ExitStack() as es:
        cx = Ctx(nc, P, es)
        oT = cx.sb([128, 8, S], BF16, "oT")
        wob = cx.sb([128, 8, D], BF16, "wob")
        for g in range(2):
            wload(P, wob[:, 4 * g:4 * g + 4, :],
                  w_out[g * 512:(g + 1) * 512, :].rearrange("(f p) o -> p f o", p=128), ("wob", g), cast)
        maskT = cx.sb([128, 128], F32, "maskT")
        P.i("pool", "memset", maskT[:], 0.0, w=["maskT"])
        P.i("pool", "affine_select", out=maskT[:], in_=maskT[:], pattern=[[1, 128]],
                                                compare_op=ALU.is_ge, fill=NEG, base=0,
                                                channel_multiplier=-1, r=["maskT"], w=["maskT"])
        QT = [cx.sb([70, S], BF16, "QT%d" % i) for i in range(2)]
        KT = [cx.sb([70, S], BF16, "KT%d" % i) for i in range(2)]
        Vp = [cx.sb([128, S // 128, 192], BF16, "Vp%d" % i) for i in range(2)]
        NBUF = 6
        LAG = 4
        pss = [cx.ps([128, 512], F32, "pss%d" % i) for i in range(NBUF)]
        pso = [cx.ps([128, 512], F32, "pso%d" % i) for i in range(2)]
        pT = [cx.sb([128, T], BF16, "pT%d" % i) for i in range(NBUF)]
        rr = [cx.sb([128, T], F32, "rr%d" % i) for i in range(2)]
        rbc = [cx.sb([128, T], F32, "rbc%d" % i) for i in range(2)]

        def head_loads(h):
            pair, odd, hsl, vsl = h // 2, h % 2, h % 2, (h // 2) % 2
            P.dma("sp", QT[hsl][:], Qs[h, :, :], r=[("Qall",)], w=[("QT", hsl)])
            P.dma("sp", KT[hsl][:], Ks[h, :, :], r=[("Kall",)], w=[("KT", hsl)])
            if not odd:
                P.dma("sp", Vp[vsl][:], Vs[pair], r=[("Vall",)], w=[("Vp", vsl)])

        def qk(ti, h, qb, kt):
            hsl = h % 2
            si = ti % NBUF
            jd = kt - 4 * qb
            c0 = max(jd, 0) * 128
            P.i("pe", "matmul", pss[si][:, c0:T], lhsT=KT[hsl][:, kt * 128:(kt + 1) * 128],
                rhs=QT[hsl][:, qb * T + c0:(qb + 1) * T], start=True, stop=True,
                r=[("KT", hsl), ("QT", hsl)], w=[("pss", si)])
            P.i("act", "activation", out=pT[si][:, c0:T], in_=pss[si][:, c0:T], func=AF.Exp,
                r=[("pss", si)], w=[("pss", si), ("pT", si)])
            if jd >= 0:
                P.i("pool", "affine_select", out=pT[si][:, c0:c0 + 128], in_=pT[si][:, c0:c0 + 128],
                    pattern=[[1, 128]], compare_op=ALU.is_ge, fill=0.0, base=0, channel_multiplier=-1,
                    r=[("pT", si)], w=[("pT", si)])
            return si, c0

        def pv(h, qb, kt, gi, si, c0):
            pair, odd, vsl = h // 2, h % 2, (h // 2) % 2
            nkt = 4 * (qb + 1)
            oi = gi % 2
            if odd:
                lhsT = Vp[vsl][:, kt, 64:192]
                out = pso[oi][:, c0:T]
            else:
                lhsT = Vp[vsl][:, kt, 0:128]
                out = pso[oi][:, c0:T]
            P.i("pe", "matmul", out, lhsT=lhsT, rhs=pT[si][:, c0:T], start=(kt == 0), stop=(kt == nkt - 1),
                r=[("pT", si), ("Vp", vsl)], w=[("pso", oi)])
            if kt == nkt - 1:
                ri = oi
                if odd:
                    drow, orows = slice(0, 1), slice(64, 128)
                else:
                    drow, orows = slice(64, 65), slice(0, 64)
                P.i("dve", "reciprocal", out=rr[ri][drow, :], in_=pso[oi][drow, :],
                    r=[("pso", oi)], w=[("pso", oi), ("rr", ri)])
                ridx = h * NB + qb
                P.dma("sp", Rs[ridx:ridx + 1, :], rr[ri][drow, :], r=[("rr", ri)], w=[("Rs", ridx)])
                P.dma("sp", rbc[ri][orows, :], Rs[ridx:ridx + 1, :].partition_broadcast(64),
                      r=[("Rs", ridx)], w=[("rbc", ri)])
                P.i("dve", "tensor_tensor", out=oT[orows, pair, qb * T:(qb + 1) * T], in0=pso[oi][orows, :],
                    in1=rbc[ri][orows, :], op=ALU.mult,
                    r=[("pso", oi), ("rbc", ri)], w=[("pso", oi), ("oT", pair, odd, qb)])

        tk = Trickle(pre, NH - 2)
        head_loads(0)
        pending = []
        ti = 0
        gi = 0
        for h in range(NH):
            if h + 1 < NH:
                head_loads(h + 1)
            if h >= 1:
                tk.step()
                if h == NH - 1:
                    tk.flush()
            for qb in range(NB):
                for kt in range(4 * (qb + 1)):
                    si, c0 = qk(ti, h, qb, kt)
                    pending.append((h, qb, kt, gi, si, c0))
                    ti += 1
                    if len(pending) > LAG:
                        pv(*pending.pop(0))
                gi += 1
        while pending:
            pv(*pending.pop(0))
        psw = [pss[0], pss[1]]
        NS2 = 2
        T2 = NS2 * 128
        xt = [cx.sb([128, NS2, D], F32, "xt%d" % i) for i in range(3)]
        NB2 = S // T2

        def c_load(b):
            P.dma("sp", xt[b % 3][:], rows(src, b * T2, T2), r=res_keys(b * T2, T2), w=[("xt", b % 3)])

        c_load(0)
        for b in range(NB2):
            xs = b % 3
            qb = (b * T2) // T
            if b + 1 < NB2:
                c_load(b + 1)
            for s in range(NS2):
                for o in range(2):
                    wi = (s * 2 + o) % 2
                    for pr in range(8):
                        P.i("pe", "matmul",
                            psw[wi][:, :], lhsT=oT[:, pr, b * T2 + s * 128:b * T2 + (s + 1) * 128],
                            rhs=wob[:, pr, o * 512:(o + 1) * 512], start=(pr == 0), stop=(pr == 7),
                            r=[("oT", pr, 0, qb), ("oT", pr, 1, qb), ("wob", pr // 4)], w=[("pss", wi)])
                    P.i("dve", "tensor_tensor",
                        out=xt[xs][:, s, o * 512:(o + 1) * 512], in0=psw[wi][:, :],
                        in1=xt[xs][:, s, o * 512:(o + 1) * 512], op=ALU.add,
                        r=[("pss", wi), ("xt", xs)], w=[("pss", wi), ("xt", xs)])
            P.dma("sp", rows(dst, b * T2, T2), xt[xs][:], r=[("xt", xs)], w=res_keys(b * T2, T2))
        P.barrier()


PARAM_SHAPES = {
    "mix_norm": [2, D], "mlp_norm": [2, D], "mlp_w1": [2, D, DFF], "mlp_w2": [2, DFF, D],
    "lru_w_in": [1, D, 2 * D], "lru_conv_w": [1, 4, D], "lru_conv_b": [1, D],
    "lru_w_r": [1, 16, 64, 64], "lru_b_r": [1, 16, 64], "lru_w_i": [1, 16, 64, 64], "lru_b_i": [1, 16, 64],
    "lru_lambda": [1, D], "lru_w_out": [1, D, D],
    "fox_w_in": [1, D, 3 * D + NH], "fox_b_f": [1, NH], "fox_q_gain": [1, HD], "fox_k_gain": [1, HD],
    "fox_w_out": [1, D, D],
}


def build(phases=("lru", "mlp0", "fox", "mlp1")):
    nc = bass.Bass("TRN2", target_bir_lowering=False)
    x = nc.dram_tensor("x", [S, D], F32, kind="ExternalInput").ap()
    prm = {k: nc.dram_tensor(k, shp, F32, kind="ExternalInput").ap() for k, shp in PARAM_SHAPES.items()}
    out = nc.dram_tensor("out", [S, D], F32, kind="ExternalOutput").ap()
    Qs = nc.dram_tensor("Qs", [NH, 70, S], BF16, kind="Internal").ap()
    Ks = nc.dram_tensor("Ks", [NH, 70, S], BF16, kind="Internal").ap()
    Vs = nc.dram_tensor("Vs", [8, 128, S // 128, 192], BF16, kind="Internal").ap()
    Rs = nc.dram_tensor("Rs", [NH * (S // 512), 512], F32, kind="Internal").ap()
    w1s = [nc.dram_tensor("w1s%d" % l, [8, 128, 8, 512], BF16, kind="Internal").ap() for l in range(2)]
    w2s = [nc.dram_tensor("w2s%d" % l, [8, 128, 4, D], BF16, kind="Internal").ap() for l in range(2)]
    fwi = nc.dram_tensor("fwi", [D, 3 * D + NH], BF16, kind="Internal").ap()
    fwo = nc.dram_tensor("fwo", [D, D], BF16, kind="Internal").ap()
    P = Prog(nc)

    def caster(ph):
        if ph in ("mlp0", "mlp1"):
            l = int(ph[-1])

            w1f, w2f = prm["mlp_w1"][l], prm["mlp_w2"][l]

            def p1(kc):
                return lambda: P.dma("pool", w1s[l][:, :, kc, :],
                                     w1f[kc * 128:(kc + 1) * 128, :].rearrange("p (g f) -> g p f", g=8),
                                     w=[("w1s", l, kc)], max_dma_last_dim=8192)

            def p2(fc):
                return lambda: P.dma("pool", w2s[l][fc // 4, :, fc % 4, :], w2f[fc * 128:(fc + 1) * 128, :],
                                     w=[("w2s", l, fc)], max_dma_last_dim=8192)
            pieces = []
            for i in range(8):
                pieces.append(p1(i))
                pieces += [p2(4 * i + j) for j in range(4)]
            return pieces
        if ph == "fox":
            return (cast_rows(P, fwi, prm["fox_w_in"][0], "fwi", maxb=4096)
                    + cast_rows(P, fwo, prm["fox_w_out"][0], "fwo"))
        return None

    src = x
    for i, ph in enumerate(phases):
        pre = caster(phases[i + 1]) if i + 1 < len(phases) else None
        cast = (i == 0)
        if ph == "lru":
            lru_phase(nc, P, src, out, prm["mix_norm"][0, :], prm["lru_w_in"][0], prm["lru_conv_w"][0],
                      prm["lru_conv_b"][0, :], prm["lru_w_r"][0], prm["lru_b_r"][0].rearrange("n e -> (n e)"),
                      prm["lru_w_i"][0], prm["lru_b_i"][0].rearrange("n e -> (n e)"),
                      prm["lru_lambda"][0, :], prm["lru_w_out"][0], pre=pre)
        elif ph in ("mlp0", "mlp1"):
            l = int(ph[-1])
            mlp_phase(nc, P, src, out, prm["mlp_w1"][l] if cast else w1s[l], prm["mlp_w2"][l] if cast else w2s[l],
                      prm["mlp_norm"][l, :], cast=cast, pre=pre)
        elif ph == "fox":
            fox_phase(nc, P, src, out, prm["mix_norm"][1, :], prm["fox_w_in"][0] if cast else fwi,
                      prm["fox_b_f"][0, :], prm["fox_q_gain"][0, :], prm["fox_k_gain"][0, :],
                      prm["fox_w_out"][0] if cast else fwo, Qs, Ks, Vs, Rs, cast=cast, pre=pre)
        src = out
    P.emit()
    return nc


def kernel(**inputs):
    n = 8
    nc = build()
    x = np.ascontiguousarray(np.asarray(inputs["x"], dtype=np.float32))
    params = {k: np.ascontiguousarray(np.asarray(inputs[k], dtype=np.float32)) for k in PARAM_SHAPES}
    in_maps = []
    for c in range(n):
        m = {"x": x[c]}
        m.update(params)
        in_maps.append(m)
    res = run_bass_kernel_spmd(nc, in_maps, core_ids=list(range(n)))
    return np.stack([np.asarray(r["out"], dtype=np.float32) for r in res.results], axis=0)
```
